# Optimizing a Trainium2 kernel written in Bass

```python
import math
import jax, jax.numpy as jnp
from jax import lax
import numpy as np

D_MODEL = 1024
BATCH = 16
SEQ = 256
DEPTH = 4
DEC_BATCH = 8
DEC_SEQ = 2048
PAST_LEN = 512

GRID_W = 64
HEAD_DIM = 64
A_HEADS = 8
A_KV_HEADS = 2
B_HEADS = 8
C_HEADS = 8
NA_ROWS = 8
NA_COLS = 16
D_FF = 2816
N_MOD = 9
N_EVEN = (DEPTH + 1) // 2
N_ODD = DEPTH // 2
Q_BLOCK = 128
ROPE_THETA = 10000.0
NORM_EPS = 1e-6
MASK_VALUE = -1e30
AB_IN = (A_HEADS + 2 * A_KV_HEADS + 3 * B_HEADS) * HEAD_DIM
AB_OUT = (A_HEADS + B_HEADS) * HEAD_DIM
C_IN = 3 * C_HEADS * 2 * HEAD_DIM
C_OUT = C_HEADS * 2 * HEAD_DIM

kernel_name = 'hybrid_diffusion_prefix_trunk_step'


def rmsnorm(x, w):
    xf = x.astype(jnp.float32)
    y = xf * lax.rsqrt(jnp.mean(xf * xf, axis=-1, keepdims=True) + NORM_EPS)
    return (y * w.astype(jnp.float32)).astype(x.dtype)


def swiglu(h, w1, w3, w2):
    return (jax.nn.silu(h @ w1) * (h @ w3)) @ w2


def rope_tables(n_tokens, dim, dtype):
    t = jnp.arange(n_tokens)
    row = (t // GRID_W).astype(jnp.float32)
    col = (t % GRID_W).astype(jnp.float32)
    n_freq = dim // 4
    inv_freq = ROPE_THETA ** (-jnp.arange(n_freq, dtype=jnp.float32) / n_freq)
    ang_r = row[:, None] * inv_freq
    ang_c = col[:, None] * inv_freq
    ang = jnp.concatenate([ang_r, ang_r, ang_c, ang_c], axis=-1)
    return jnp.cos(ang).astype(dtype), jnp.sin(ang).astype(dtype)


def apply_axial_rope(x, cos, sin):
    S, Dh = cos.shape
    shape = (1, S) + (1,) * (x.ndim - 3) + (Dh,)
    cos = cos.reshape(shape)
    sin = sin.reshape(shape)
    x1r, x2r, x1c, x2c = jnp.split(x, 4, axis=-1)
    rot = jnp.concatenate([-x2r, x1r, -x2c, x1c], axis=-1)
    return x * cos + rot * sin


def map_query_blocks(fn, q):
    B, S = q.shape[0], q.shape[1]
    nb = S // Q_BLOCK
    qb = jnp.moveaxis(q.reshape((B, nb, Q_BLOCK) + q.shape[2:]), 1, 0)
    out = lax.map(fn, (jnp.arange(nb), qb))
    out = jnp.moveaxis(out, 0, 1)
    return out.reshape((B, S) + out.shape[3:])


def gqa_attention(q, k, v):
    B, S, Hq, Dh = q.shape
    Hkv = k.shape[2]
    G = Hq // Hkv
    Dv = v.shape[-1]
    scale = Dh ** -0.5

    def block(args):
        _, qb = args
        qg = qb.reshape(B, Q_BLOCK, Hkv, G, Dh)
        s = jnp.einsum('bqhgd,bkhd->bhgqk', qg, k).astype(jnp.float32) * scale
        p = jax.nn.softmax(s, axis=-1).astype(v.dtype)
        o = jnp.einsum('bhgqk,bkhd->bqhgd', p, v)
        return o.reshape(B, Q_BLOCK, Hq, Dv)

    return map_query_blocks(block, q)


def neighbourhood_attention(q, k, v, k_ctx, v_ctx, rpb):
    B, S, H, Dh = q.shape
    rows = S // GRID_W
    win_r = min(NA_ROWS, rows)
    win_c = min(NA_COLS, GRID_W)
    q_rows = Q_BLOCK // GRID_W
    band = min(win_r + q_rows - 1, rows)
    n_band = band * GRID_W
    scale = Dh ** -0.5
    k_grid = k.reshape(B, rows, GRID_W, H, Dh)
    v_grid = v.reshape(B, rows, GRID_W, H, v.shape[-1])
    q_local = jnp.arange(Q_BLOCK)
    q_col = q_local % GRID_W
    k_local = jnp.arange(n_band)
    k_col = k_local % GRID_W
    col_start = jnp.clip(q_col - win_c // 2, 0, GRID_W - win_c)
    col_in = (k_col[None, :] >= col_start[:, None]) & (k_col[None, :] < col_start[:, None] + win_c)
    dc = jnp.clip(k_col[None, :] - q_col[:, None] + NA_COLS - 1, 0, 2 * NA_COLS - 2)

    def block(args):
        j, qb = args
        q_row = j * q_rows + q_local // GRID_W
        row_start = jnp.clip(q_row - win_r // 2, 0, rows - win_r)
        band_start = jnp.clip(row_start[0], 0, rows - band)
        kb = lax.dynamic_slice_in_dim(k_grid, band_start, band, axis=1).reshape(B, n_band, H, Dh)
        vb = lax.dynamic_slice_in_dim(v_grid, band_start, band, axis=1).reshape(B, n_band, H, v.shape[-1])
        k_row = band_start + k_local // GRID_W
        row_in = (k_row[None, :] >= row_start[:, None]) & (k_row[None, :] < row_start[:, None] + win_r)
        dr = jnp.clip(k_row[None, :] - q_row[:, None] + NA_ROWS - 1, 0, 2 * NA_ROWS - 2)
        bias = rpb[:, dr, dc].astype(jnp.float32)
        s_nb = jnp.einsum('bqhd,bkhd->bhqk', qb, kb).astype(jnp.float32) * scale + bias
        s_nb = jnp.where(row_in & col_in, s_nb, MASK_VALUE)
        s_ctx = jnp.einsum('bqhd,bkhd->bhqk', qb, k_ctx).astype(jnp.float32) * scale
        p = jax.nn.softmax(jnp.concatenate([s_nb, s_ctx], axis=-1), axis=-1).astype(v.dtype)
        return (jnp.einsum('bhqk,bkhd->bqhd', p[..., :n_band], vb)
                + jnp.einsum('bhqk,bkhd->bqhd', p[..., n_band:], v_ctx))

    return map_query_blocks(block, q)


def diff_attention(q, k, v, lam):
    Dh = q.shape[-1]
    scale = Dh ** -0.5

    def block(args):
        _, qb = args
        s = jnp.einsum('bqhjd,bkhjd->bhjqk', qb, k).astype(jnp.float32) * scale
        p = jax.nn.softmax(s, axis=-1)
        a = (p[:, :, 0] - lam * p[:, :, 1]).astype(v.dtype)
        return jnp.einsum('bhqk,bkhd->bqhd', a, v)

    return map_query_blocks(block, q)


def diff_lambda(lam_params, lam_init):
    lp = lam_params.astype(jnp.float32)
    return jnp.exp(jnp.sum(lp[0] * lp[1])) - jnp.exp(jnp.sum(lp[2] * lp[3])) + lam_init


def even_project(h, w_in, q_norm, k_norm):
    B, S, _ = h.shape
    sizes = [A_HEADS * HEAD_DIM, A_KV_HEADS * HEAD_DIM, A_KV_HEADS * HEAD_DIM,
             B_HEADS * HEAD_DIM, B_HEADS * HEAD_DIM]
    splits = [sum(sizes[:i + 1]) for i in range(len(sizes))]
    aq, ak, av, bq, bk, bv = jnp.split(h @ w_in, splits, axis=-1)
    aq = rmsnorm(aq.reshape(B, S, A_HEADS, HEAD_DIM), q_norm)
    ak = rmsnorm(ak.reshape(B, S, A_KV_HEADS, HEAD_DIM), k_norm)
    av = av.reshape(B, S, A_KV_HEADS, HEAD_DIM)
    bq = bq.reshape(B, S, B_HEADS, HEAD_DIM)
    bk = bk.reshape(B, S, B_HEADS, HEAD_DIM)
    bv = bv.reshape(B, S, B_HEADS, HEAD_DIM)
    return aq, ak, av, bq, bk, bv


def even_merge(oa, ob, w_out):
    B, S = oa.shape[0], oa.shape[1]
    return jnp.concatenate([oa.reshape(B, S, -1), ob.reshape(B, S, -1)], axis=-1) @ w_out


def even_mixer_ctx(h, w_in, w_out, q_norm, k_norm):
    aq, ak, av, bq, bk, bv = even_project(h, w_in, q_norm, k_norm)
    oa = gqa_attention(aq, ak, av)
    ob = gqa_attention(bq, bk, bv)
    return even_merge(oa, ob, w_out), (ak, av, bk, bv)


def even_mixer_lat(h, w_in, w_out, q_norm, k_norm, rpb, cos, sin, ak_c, av_c, bk_c, bv_c):
    aq, ak, av, bq, bk, bv = even_project(h, w_in, q_norm, k_norm)
    aq = apply_axial_rope(aq, cos, sin)
    ak = apply_axial_rope(ak, cos, sin)
    oa = gqa_attention(aq, jnp.concatenate([ak, ak_c], axis=1), jnp.concatenate([av, av_c], axis=1))
    ob = neighbourhood_attention(bq, bk, bv, bk_c, bv_c, rpb)
    return even_merge(oa, ob, w_out), ()


def odd_project(h, w_in):
    B, S, _ = h.shape
    q, k, v = jnp.split(h @ w_in, 3, axis=-1)
    return (q.reshape(B, S, C_HEADS, 2, HEAD_DIM), k.reshape(B, S, C_HEADS, 2, HEAD_DIM),
            v.reshape(B, S, C_HEADS, 2 * HEAD_DIM))


def odd_output(o, subln, lam_init, w_out):
    B, S = o.shape[0], o.shape[1]
    o = rmsnorm(o, subln) * (1.0 - lam_init)
    return o.reshape(B, S, C_OUT) @ w_out


def odd_mixer_ctx(h, w_in, w_out, lam_params, subln, lam_init):
    B, S, _ = h.shape
    q, k, v = odd_project(h, w_in)
    o = diff_attention(q, k, v, diff_lambda(lam_params, lam_init))
    return odd_output(o, subln, lam_init, w_out), (k.reshape(B, S, C_HEADS, 2 * HEAD_DIM), v)


def odd_mixer_lat(h, w_in, w_out, lam_params, subln, lam_init, cos, sin, k_c, v_c):
    B, _, _ = h.shape
    q, k, v = odd_project(h, w_in)
    q = apply_axial_rope(q, cos, sin)
    k = apply_axial_rope(k, cos, sin)
    k_c = k_c.reshape(B, k_c.shape[1], C_HEADS, 2, HEAD_DIM)
    o = diff_attention(q, jnp.concatenate([k, k_c], axis=1), jnp.concatenate([v, v_c], axis=1),
                       diff_lambda(lam_params, lam_init))
    return odd_output(o, subln, lam_init, w_out), ()


def macaron_layer(x, mod, norm_w, w1, w3, w2, mixer):
    m = [mod[:, i][:, None, :] for i in range(N_MOD)]
    h = rmsnorm(x, norm_w[0]) * (1.0 + m[1]) + m[0]
    x = x + 0.5 * m[2] * swiglu(h, w1[0], w3[0], w2[0])
    h = rmsnorm(x, norm_w[1]) * (1.0 + m[4]) + m[3]
    mix_out, extras = mixer(h)
    x = x + m[5] * mix_out
    h = rmsnorm(x, norm_w[2]) * (1.0 + m[7]) + m[6]
    x = x + 0.5 * m[8] * swiglu(h, w1[1], w3[1], w2[1])
    return x, extras


def setup_inputs(seed: int = 0) -> dict:
    key = jax.random.key(seed)
    ks = jax.random.split(key, 32)
    f32 = jnp.float32

    def nrm(k, shape, scale=1.0):
        return jax.random.normal(k, shape, f32) * scale

    return {
        'x_prompt': nrm(ks[0], (BATCH, SEQ, D_MODEL)),
        'x_sample': nrm(ks[1], (DEC_BATCH, DEC_SEQ, D_MODEL)),
        'cache_a_k': nrm(ks[2], (DEC_BATCH, N_EVEN, PAST_LEN, A_KV_HEADS, HEAD_DIM)),
        'cache_a_v': nrm(ks[3], (DEC_BATCH, N_EVEN, PAST_LEN, A_KV_HEADS, HEAD_DIM)),
        'cache_b_k': nrm(ks[4], (DEC_BATCH, N_EVEN, PAST_LEN, B_HEADS, HEAD_DIM)),
        'cache_b_v': nrm(ks[5], (DEC_BATCH, N_EVEN, PAST_LEN, B_HEADS, HEAD_DIM)),
        'cache_c_k': nrm(ks[6], (DEC_BATCH, N_ODD, PAST_LEN, C_HEADS, 2 * HEAD_DIM)),
        'cache_c_v': nrm(ks[7], (DEC_BATCH, N_ODD, PAST_LEN, C_HEADS, 2 * HEAD_DIM)),
        'c': nrm(ks[8], (DEC_BATCH, D_MODEL)),
        'c_ctx': nrm(ks[9], (D_MODEL,)),
        'w_mod': nrm(ks[10], (DEPTH, D_MODEL, N_MOD * D_MODEL), 0.5 * D_MODEL ** -0.5),
        'b_mod': nrm(ks[11], (DEPTH, N_MOD * D_MODEL), 0.01),
        'norm_w': 1.0 + nrm(ks[12], (DEPTH, 3, D_MODEL), 0.02),
        'ffn_w1': nrm(ks[13], (DEPTH, 2, D_MODEL, D_FF), D_MODEL ** -0.5),
        'ffn_w3': nrm(ks[14], (DEPTH, 2, D_MODEL, D_FF), D_MODEL ** -0.5),
        'ffn_w2': nrm(ks[15], (DEPTH, 2, D_FF, D_MODEL), D_FF ** -0.5),
        'w_in_ab': nrm(ks[16], (N_EVEN, D_MODEL, AB_IN), D_MODEL ** -0.5),
        'w_out_ab': nrm(ks[17], (N_EVEN, AB_OUT, D_MODEL), AB_OUT ** -0.5),
        'a_q_norm': 1.0 + nrm(ks[18], (N_EVEN, HEAD_DIM), 0.02),
        'a_k_norm': 1.0 + nrm(ks[19], (N_EVEN, HEAD_DIM), 0.02),
        'b_rpb': nrm(ks[20], (N_EVEN, B_HEADS, 2 * NA_ROWS - 1, 2 * NA_COLS - 1), 0.1),
        'w_in_c': nrm(ks[21], (N_ODD, D_MODEL, C_IN), D_MODEL ** -0.5),
        'w_out_c': nrm(ks[22], (N_ODD, C_OUT, D_MODEL), C_OUT ** -0.5),
        'c_lambda': nrm(ks[23], (N_ODD, 4, HEAD_DIM), 0.1),
        'c_subln': 1.0 + nrm(ks[24], (N_ODD, 2 * HEAD_DIM), 0.02),
        'final_norm': 1.0 + nrm(ks[25], (D_MODEL,), 0.02),
    }


def reference(x_prompt, x_sample, cache_a_k, cache_a_v, cache_b_k, cache_b_v, cache_c_k, cache_c_v,
              c, c_ctx, w_mod, b_mod, norm_w, ffn_w1, ffn_w3, ffn_w2, w_in_ab, w_out_ab,
              a_q_norm, a_k_norm, b_rpb, w_in_c, w_out_c, c_lambda, c_subln, final_norm):
    lam_inits = [0.8 - 0.6 * math.exp(-0.3 * l) for l in range(DEPTH)]

    xp = x_prompt
    ak_l, av_l, bk_l, bv_l, ck_l, cv_l = [], [], [], [], [], []
    for l in range(DEPTH):
        mod = (jax.nn.silu(c_ctx) @ w_mod[l] + b_mod[l]).reshape(1, N_MOD, D_MODEL)
        if l % 2 == 0:
            e = l // 2
            mixer = lambda h: even_mixer_ctx(h, w_in_ab[e], w_out_ab[e], a_q_norm[e], a_k_norm[e])
            xp, (ak, av, bk, bv) = macaron_layer(xp, mod, norm_w[l], ffn_w1[l], ffn_w3[l], ffn_w2[l], mixer)
            ak_l.append(ak); av_l.append(av); bk_l.append(bk); bv_l.append(bv)
        else:
            o = l // 2
            mixer = lambda h: odd_mixer_ctx(h, w_in_c[o], w_out_c[o], c_lambda[o], c_subln[o], lam_inits[l])
            xp, (ck, cv) = macaron_layer(xp, mod, norm_w[l], ffn_w1[l], ffn_w3[l], ffn_w2[l], mixer)
            ck_l.append(ck); cv_l.append(cv)
    y_prompt = rmsnorm(xp, final_norm)
    new_a_k = jnp.stack(ak_l, axis=1)
    new_a_v = jnp.stack(av_l, axis=1)
    new_b_k = jnp.stack(bk_l, axis=1)
    new_b_v = jnp.stack(bv_l, axis=1)
    new_c_k = jnp.stack(ck_l, axis=1)
    new_c_v = jnp.stack(cv_l, axis=1)

    xs = x_sample
    cos, sin = rope_tables(x_sample.shape[1], HEAD_DIM, x_sample.dtype)
    for l in range(DEPTH):
        mod = (jax.nn.silu(c) @ w_mod[l] + b_mod[l]).reshape(-1, N_MOD, D_MODEL)
        if l % 2 == 0:
            e = l // 2
            mixer = lambda h: even_mixer_lat(h, w_in_ab[e], w_out_ab[e], a_q_norm[e], a_k_norm[e], b_rpb[e],
                                             cos, sin, cache_a_k[:, e], cache_a_v[:, e],
                                             cache_b_k[:, e], cache_b_v[:, e])
        else:
            o = l // 2
            mixer = lambda h: odd_mixer_lat(h, w_in_c[o], w_out_c[o], c_lambda[o], c_subln[o], lam_inits[l],
                                            cos, sin, cache_c_k[:, o], cache_c_v[:, o])
        xs, _ = macaron_layer(xs, mod, norm_w[l], ffn_w1[l], ffn_w3[l], ffn_w2[l], mixer)
    y_sample = rmsnorm(xs, final_norm)

    return (y_prompt, y_sample, new_a_k, new_a_v, new_b_k, new_b_v, new_c_k, new_c_v)
```

```python
import math
import os
import contextlib
import numpy as np
import concourse.bass as bass
import concourse.mybir as mybir
from concourse.bass_utils import run_bass_kernel_spmd

F32 = mybir.dt.float32
BF16 = mybir.dt.bfloat16
ALU = mybir.AluOpType
AF = mybir.ActivationFunctionType

D = 1024
NCH = 8
DFF = 2816
NF = 22
DEPTH = 4
TS = 2048
TC = 512
T = TS + TC
NTB = 5
NTT = 20
GRID_W = 64
EPS = 1e-6
SCALE = 0.125
LAM_INIT = [0.8 - 0.6 * math.exp(-0.3 * l) for l in range(DEPTH)]
N_CORES = 8
TRACE_BUF = None


class Eng:
    def __init__(self, name, h, sem):
        self.name = name
        self.h = h
        self.sem = sem
        self.cnt = 0
        self.seen = {}


class DSem:
    def __init__(self, sem):
        self.sem = sem
        self.cnt = 0


class Buf:
    __slots__ = ("w", "r", "name", "excl")

    def __init__(self, name=""):
        self.w = None
        self.r = {}
        self.name = name
        self.excl = False


class Tile:
    __slots__ = ("ap", "buf")

    def __init__(self, ap, name=""):
        self.ap = ap
        self.buf = Buf(name)


class Ring:
    def __init__(self, tiles):
        self.tiles = tiles
        self.i = 0

    def next(self):
        t = self.tiles[self.i % len(self.tiles)]
        self.i += 1
        return t


class B:
    def __init__(self, nc, es):
        self.nc = nc
        self.es = es
        mk = lambda n: es.enter_context(nc.semaphore(n))
        self.pe = Eng("pe", nc.tensor, mk("s_pe"))
        self.act = Eng("act", nc.scalar, mk("s_act"))
        self.dve = Eng("dve", nc.vector, mk("s_dve"))
        self.pool = Eng("pool", nc.gpsimd, mk("s_pool"))
        self.sp = Eng("sp", nc.sync, mk("s_sp"))
        self.engs = [self.pe, self.act, self.dve, self.pool, self.sp]
        self.dsems = {}
        for q in (self.sp, self.pool):
            self.dsems[q.name] = [DSem(mk(f"d_{q.name}{i}")) for i in range(8)]
        self.dma_i = {"sp": 0, "pool": 0}
        self.fence = {}
        self.n_ins = 0

    def _need(self, eng, reads, writes):
        need = {}

        def add(tag, raw=False):
            if tag is None:
                return
            o, c = tag
            if o is eng and (not raw or eng.name == "pe"):
                return
            if need.get(o, 0) < c:
                need[o] = c

        for b in reads:
            add(b.w, raw=True)
            if b.excl:
                for o, c in b.r.items():
                    add((o, c))
        for b in writes:
            add(b.w)
            for o, c in b.r.items():
                add((o, c))
        return need

    def _emit_waits(self, eng, need):
        for o, c in need.items():
            if eng.seen.get(o, 0) < c:
                eng.h.wait_ge(o.sem, c)
                eng.seen[o] = c
                self.n_ins += 1

    def op(self, eng, fn, reads=(), writes=(), sig=True):
        reads = [t.buf if isinstance(t, Tile) else t for t in reads]
        writes = [t.buf if isinstance(t, Tile) else t for t in writes]
        need = self._need(eng, reads, writes)
        if TRACE_BUF and any(b.name == TRACE_BUF for b in list(reads) + list(writes)):
            print("TRACE", eng.name, "cnt", eng.cnt, "sig", sig, "reads", [b.name for b in reads], "writes", [b.name for b in writes],
                  "need", {getattr(o, 'name', 'dsem'): c for o, c in need.items()}, "seen", {getattr(o, 'name', 'dsem'): c for o, c in eng.seen.items()})
        self._emit_waits(eng, need)
        ins = fn()
        self.n_ins += 1
        if sig:
            ins.then_inc(eng.sem, 1)
            eng.cnt += 1
            tag = eng.cnt
        else:
            tag = eng.cnt + 1
        for b in reads:
            if b.r.get(eng, 0) < tag:
                b.r[eng] = tag
        for b in writes:
            b.w = (eng, tag)
            b.r = {}
        return ins

    def dma(self, q, out, in_, reads=(), writes=()):
        reads = [t.buf if isinstance(t, Tile) else t for t in reads]
        writes = [t.buf if isinstance(t, Tile) else t for t in writes]
        lst = self.dsems[q.name]
        ds = lst[self.dma_i[q.name] % len(lst)]
        self.dma_i[q.name] += 1
        need = self._need(q, reads, writes)
        if ds.cnt > 0:
            need[ds] = max(need.get(ds, 0), ds.cnt)
        self._emit_waits(q, need)
        q.h.dma_start(out=out, in_=in_).then_inc(ds.sem, 16)
        self.n_ins += 1
        ds.cnt += 16
        for b in reads:
            b.r[ds] = ds.cnt
        for b in writes:
            b.w = (ds, ds.cnt)
            b.r = {}

    def mm(self, out, lhsT, rhs, start, stop, reads=(), writes=(), sig=True, **kw):
        return self.op(self.pe, lambda: self.nc.tensor.matmul(out, lhsT=lhsT, rhs=rhs, start=start, stop=stop, **kw),
                       reads=reads, writes=writes, sig=sig)

    def mm_group(self, out_t, out_ap, pairs, reads):
        n = len(pairs)
        for i, (l, r) in enumerate(pairs):
            self.mm(out_ap, l, r, start=(i == 0), stop=(i == n - 1),
                    reads=reads if i == 0 else (), writes=[out_t] if i == 0 else (), sig=(i == n - 1))

    def finish(self):
        for q in (self.sp, self.pool):
            for ds in self.dsems[q.name]:
                if ds.cnt > 0 and q.seen.get(ds, 0) < ds.cnt:
                    q.h.wait_ge(ds.sem, ds.cnt)
        for e in self.engs:
            if e is not self.sp and e.cnt > 0:
                self.sp.h.wait_ge(e.sem, e.cnt)
        for ds in self.dsems["pool"]:
            if ds.cnt > 0:
                self.sp.h.wait_ge(ds.sem, ds.cnt)


def _const_tables():
    t = np.arange(TS)
    row = (t // GRID_W).astype(np.float32)
    col = (t % GRID_W).astype(np.float32)
    n_freq = 16
    inv_freq = (np.float32(10000.0) ** (-np.arange(n_freq, dtype=np.float32) / n_freq)).astype(np.float32)
    ang_r = row[:, None] * inv_freq
    ang_c = col[:, None] * inv_freq
    ang = np.concatenate([ang_r, ang_r, ang_c, ang_c], axis=-1)
    cos = np.cos(ang).astype(np.float32).T
    sin = np.sin(ang).astype(np.float32).T
    cossin = np.stack([np.concatenate([cos, cos], 0), np.concatenate([sin, sin], 0)], 0)
    ident = np.eye(128, dtype=np.float32)
    ones = np.ones((128, 128), np.float32)
    bd = np.zeros((128, 128), np.float32)
    bd[:64, :64] = 1.0
    bd[64:, 64:] = 1.0
    R = np.zeros((128, 128), np.float32)
    for base in (0, 64):
        for i in range(16):
            R[base + 16 + i, base + i] = -1.0
            R[base + i, base + 16 + i] = 1.0
            R[base + 48 + i, base + 32 + i] = -1.0
            R[base + 32 + i, base + 48 + i] = 1.0
    cmats = np.stack([ones, bd, R], 0)
    qc = np.arange(64)
    kc = np.arange(64)
    col_start = np.clip(qc - 8, 0, 64 - 16)
    cm = ((kc[:, None] >= col_start[None, :]) & (kc[:, None] < col_start[None, :] + 16)).astype(np.float32)
    colmask = np.concatenate([cm, cm], 0)
    return cossin, ident, cmats, colmask


def _expand_rpb(b_rpb):
    kc = np.arange(64)[:, None]
    qc = np.arange(64)[None, :]
    dc = np.clip(kc - qc + 15, 0, 30)
    dr = 14 - np.arange(15)
    g = b_rpb[:, :, dr][:, :, :, dc]
    g = np.transpose(g, (0, 1, 3, 2, 4))
    g = np.concatenate([g, g], axis=2)
    return np.ascontiguousarray(g.reshape(2, 8, 128, 15 * 64)).astype(np.float32)


def build_program(NL=DEPTH, do_mixer=True, do_ffn=True, dbg=False, groups=None, stage=9):
    nc = bass.Bass("TRN2", target_bir_lowering=False)
    es = contextlib.ExitStack()
    bld = B(nc, es)
    pe, act, dve, pool, sp = bld.pe, bld.act, bld.dve, bld.pool, bld.sp
    V = nc.vector
    A = nc.scalar
    G = nc.gpsimd

    def din(name, shape):
        return nc.dram_tensor(name, list(shape), F32, kind="ExternalInput").ap()

    def dout(name, shape):
        return nc.dram_tensor(name, list(shape), F32, kind="ExternalOutput").ap()

    xs_d = din("xs", [TS, D])
    xp_d = din("xp", [TC, D])
    cak_d = din("cak", [2, 512, 128])
    cav_d = din("cav", [2, 512, 128])
    cbk_d = din("cbk", [2, 512, 512])
    cbv_d = din("cbv", [2, 512, 512])
    cck_d = din("cck", [2, 512, 1024])
    ccv_d = din("ccv", [2, 512, 1024])
    cvec_d = din("cvec", [16, 128])
    wmod_d = [din(f"w_mod{l}", [D, 9 * D]) for l in range(NL)]
    bmod_d = din("b_mod", [DEPTH, 72, 128])
    normw_d = din("norm_w", [96, 128])
    w1_d = [[din(f"w1_{l}_{f}", [D, DFF]) for f in range(2)] for l in range(NL)]
    w3_d = [[din(f"w3_{l}_{f}", [D, DFF]) for f in range(2)] for l in range(NL)]
    w2_d = [[din(f"w2_{l}_{f}", [DFF, D]) for f in range(2)] for l in range(NL)]
    n_even = (NL + 1) // 2
    n_odd = NL // 2
    winab_d = [din(f"w_in_ab{e}", [D, 2304]) for e in range(n_even)]
    woutab_d = [din(f"w_out_ab{e}", [D, D]) for e in range(n_even)]
    qkn_d = din("qk_norm", [4, 128])
    rpbx_d = din("rpbx", [2, 8, 128, 960])
    winc_d = [din(f"w_in_c{o}", [D, 3072]) for o in range(n_odd)]
    woutc_d = [din(f"w_out_c{o}", [D, D]) for o in range(n_odd)]
    lam_d = din("c_lambda", [2, 128, 256])
    kwbc_d = din("kw_bc", [2, 128, 64])
    subln_d = din("c_subln", [2, 128])
    fnorm_d = din("final_norm", [8, 128])
    cossin_d = din("cossin", [2, 128, TS])
    ident_d = din("ident", [128, 128])
    cmats_d = din("cmats", [3, 128, 128])
    colmask_d = din("colmask", [128, 64])
    colmaskx_d = din("colmaskx", [128, 960])

    yp_d = dout("yp", [TC, D])
    ys_d = dout("ys", [TS, D])
    nak_d = dout("nak", [2, 2, 256, 128])
    nav_d = dout("nav", [2, 2, 256, 128])
    nbk_d = dout("nbk", [2, 2, 256, 512])
    nbv_d = dout("nbv", [2, 2, 256, 512])
    nck_d = dout("nck", [2, 2, 256, 1024])
    ncv_d = dout("ncv", [2, 2, 256, 1024])

    uid = [0]

    def sb(name, shape, dt, stack=None):
        uid[0] += 1
        return (stack or es).enter_context(nc.sbuf_tensor(f"sb_{name}_{uid[0]}", list(shape), dt))

    xT = sb("xT", [128, NCH, T], F32)
    hT = sb("hT", [128, NCH, T], BF16)
    x_b = [[Buf(f"x{tb}_{c}") for c in range(NCH)] for tb in range(NTB)]
    h_b = [Buf(f"h{tb}") for tb in range(NTB)]
    ident = Tile(sb("ident", [128, 128], F32)[:], "ident")
    cm = Tile(sb("cm", [128, 3, 128], BF16)[:], "cm")
    ones_bf = cm.ap[:, 0, :]
    bd_bf = cm.ap[:, 1, :]
    rot_bf = cm.ap[:, 2, :]
    prm = Tile(sb("prm", [128, 128], F32)[:], "prm")
    modT = Tile(sb("modT", [128, 72, 2], F32)[:], "modT")
    bmodT = Tile(sb("bmodT", [128, 72], F32)[:], "bmodT")
    drv = Tile(sb("drv", [128, 3, 3, NCH, 2], F32)[:], "drv")
    scT = Tile(sb("scT", [128, 16], BF16)[:], "scT")
    lamt = Tile(sb("lamt", [128, 8], F32)[:], "lamt")
    kw8 = Tile(sb("kw8", [128, 64], F32)[:], "kw8")
    colmask = Tile(sb("colmask", [128, 64], F32)[:], "colmask")
    epsc = Tile(sb("epsc", [128, 1], F32)[:], "epsc")
    NSLOT = 6
    wring_t = sb("wring", [128, NSLOT, 2048], BF16)
    wring = Ring([Tile(wring_t[:, i, :], f"w{i}") for i in range(NSLOT)])

    banks = [Tile(es.enter_context(nc.psum_tensor(f"ps{i}", [128, 512], F32))[:], f"ps{i}") for i in range(8)]
    for bk in banks:
        bk.buf.excl = True
    ps_all = Ring(banks)

    def tbcols(tb):
        return slice(tb * 512, (tb + 1) * 512)

    def cond_of(tb):
        return 1 if tb == 4 else 0

    with contextlib.ExitStack() as ss:
        stg = Ring([Tile(sb(f"stg{i}", [128, D], F32, ss)[:], f"stg{i}") for i in range(3)])
        bld.dma(sp, ident.ap, ident_d, writes=[ident])
        bld.dma(pool, cm.ap, cmats_d.rearrange("m p n -> p m n"), writes=[cm])
        bld.dma(sp, colmask.ap, colmask_d, writes=[colmask])
        bld.op(dve, lambda: V.memset(epsc.ap, EPS), writes=[epsc])

        s0 = stg.next()
        bld.dma(sp, s0.ap[0:96, 0:128], normw_d, writes=[s0])
        bld.dma(sp, s0.ap[96:104, 0:128], fnorm_d, writes=[s0])
        bld.dma(sp, s0.ap[104:108, 0:128], qkn_d, writes=[s0])
        bld.dma(sp, s0.ap[108:110, 0:128], subln_d, writes=[s0])
        bld.dma(sp, s0.ap[110:126, 0:128], cvec_d, writes=[s0])
        p0 = ps_all.next()
        bld.op(pe, lambda: nc.tensor.transpose(out=p0.ap[:, 0:126], in_=s0.ap[0:126, 0:128], identity=ident.ap[0:126, 0:126]),
               reads=[s0, ident], writes=[p0])
        bld.op(dve, lambda: V.tensor_copy(out=prm.ap[:, 0:110], in_=p0.ap[:, 0:110]), reads=[p0], writes=[prm])
        bld.op(act, lambda: A.activation(out=scT.ap, in_=p0.ap[:, 110:126], func=AF.Silu), reads=[p0], writes=[scT])

        for i in range(NTT):
            st = stg.next()
            src = xs_d[i * 128:(i + 1) * 128, :] if i < 16 else xp_d[(i - 16) * 128:(i - 15) * 128, :]
            bld.dma(sp, st.ap, src, writes=[st])
            tb = i // 4
            for half in range(2):
                ps = ps_all.next()
                for c4 in range(4):
                    c = half * 4 + c4
                    bld.op(pe, lambda ps=ps, c4=c4, c=c, st=st: nc.tensor.transpose(
                        out=ps.ap[:, c4 * 128:(c4 + 1) * 128], in_=st.ap[:, c * 128:(c + 1) * 128], identity=ident.ap),
                        reads=[st, ident] if c4 == 0 else (), writes=[ps] if c4 == 0 else (), sig=(c4 == 3))
                dst = xT[:, half * 4:half * 4 + 4, i * 128:(i + 1) * 128]
                srcp = ps.ap.rearrange("p (c n) -> p c n", c=4)
                if half == 0:
                    bld.op(dve, lambda dst=dst, srcp=srcp: V.tensor_copy(out=dst, in_=srcp), reads=[ps], writes=x_b[tb][half * 4:half * 4 + 4])
                else:
                    bld.op(act, lambda dst=dst, srcp=srcp: A.copy(out=dst, in_=srcp), reads=[ps], writes=x_b[tb][half * 4:half * 4 + 4])
        setup_fence = [t.buf for t in stg.tiles]

    def wload(dst_ap, src_ap, slot):
        bld.dma(pool, dst_ap, src_ap, writes=[slot])

    def mod_piece(l, pi, ring=None):
        slot = (ring or wring).next()
        w = slot.ap.rearrange("p (k n) -> p k n", k=8)
        wload(w, wmod_d[l][:, pi * 256:(pi + 1) * 256].rearrange("(k p) n -> p k n", p=128), slot)
        for j2 in range(2):
            j = pi * 2 + j2
            out_ap = mod_ps.ap[:, j * 2:(j + 1) * 2]
            for k in range(8):
                rhs = scT.ap.rearrange("p (c k) -> p k c", c=2)[:, k, :]
                bld.mm(out_ap, w[:, k, j2 * 128:(j2 + 1) * 128], rhs, start=(k == 0), stop=(k == 7),
                       reads=[slot, scT] if k == 0 else (), writes=[mod_ps] if (k == 0) else (),
                       sig=(k == 7))

    def mod_begin(l):
        nonlocal mod_ps
        mod_ps = banks[7]
        with contextlib.ExitStack() as s2:
            bst = scoped("bst", [72, 128], F32, s2)
            bld.dma(sp, bst.ap, bmod_d[l], writes=[bst])
            pb = banks[6]
            bld.op(pe, lambda: nc.tensor.transpose(out=pb.ap[:, 0:72], in_=bst.ap, identity=ident.ap[0:72, 0:72]),
                   reads=[bst, ident], writes=[pb])
            bld.op(dve, lambda: V.tensor_copy(out=bmodT.ap, in_=pb.ap[:, 0:72]), reads=[pb], writes=[bmodT])
            _fence_scope([bst])

    def mod_finish(l):
        mp = mod_ps.ap[:, 0:144].rearrange("p (j c) -> p j c", c=2)
        for cd in range(2):
            bld.op(dve, lambda cd=cd: V.tensor_tensor(out=modT.ap[:, :, cd], in0=mp[:, :, cd], in1=bmodT.ap, op=ALU.add),
                   reads=[mod_ps, bmodT], writes=[modT])
        m = modT.ap.rearrange("p (i c) k -> p i c k", c=8)
        for s in range(3):
            nw = prm.ap[:, (l * 3 + s) * 8:(l * 3 + s) * 8 + 8]
            gsc = 1.0 if s == 1 else 0.5
            for cd in range(2):
                bld.op(dve, lambda s=s, cd=cd, nw=nw: V.scalar_tensor_tensor(
                    out=drv.ap[:, 0, s, :, cd], in0=m[:, 3 * s + 1, :, cd], scalar=1.0, in1=nw, op0=ALU.add, op1=ALU.mult),
                    reads=[modT, prm], writes=[drv])
            bld.op(dve, lambda s=s: V.tensor_copy(out=drv.ap[:, 1, s], in_=m[:, 3 * s]), reads=[modT], writes=[drv])
            bld.op(dve, lambda s=s, gsc=gsc: V.tensor_scalar_mul(out=drv.ap[:, 2, s], in0=m[:, 3 * s + 2], scalar1=gsc),
                   reads=[modT], writes=[drv])

    mod_ps = None

    def rstd_op(ps, r, n, eps, cols=slice(0, 512), pr=slice(0, 128)):
        bld.op(act, lambda: A.activation(out=r.ap[pr, cols], in_=ps.ap[pr, cols], func=AF.Ln, bias=epsc.ap[pr, 0:1] if eps == EPS else None, scale=1.0 / n),
               reads=[ps, epsc], writes=[r])
        bld.op(act, lambda: A.activation(out=r.ap[pr, cols], in_=r.ap[pr, cols], func=AF.Exp, scale=-0.5), reads=[r], writes=[r])

    def prenorm(l, s):
        last_group["kind"] = None
        with contextlib.ExitStack() as s2:
            sq = Ring([scoped(f"sq{i}", [128, NCH, 512], BF16, s2) for i in range(1)])
            rs = Ring([scoped(f"rs{i}", [128, 512], F32, s2) for i in range(2)])
            tm = Ring([scoped(f"tm{i}", [128, 512], F32, s2) for i in range(3)])
            for tb in range(NTB):
                cd = cond_of(tb)
                q = sq.next()
                bld.op(act, lambda q=q, tb=tb: A.activation(out=q.ap, in_=xT[:, :, tbcols(tb)], func=AF.Square),
                       reads=x_b[tb], writes=[q])
                ps = ps_all.next()
                bld.mm_group(ps, ps.ap, [(ones_bf, q.ap[:, c, :]) for c in range(NCH)], reads=[q, cm])
                r = rs.next()
                rstd_op(ps, r, D, EPS)
                for c in range(NCH):
                    t = tm.next()
                    bld.op(dve, lambda t=t, c=c, tb=tb, r=r, cd=cd: V.scalar_tensor_tensor(
                        out=t.ap, in0=xT[:, c, tbcols(tb)], scalar=drv.ap[:, 0, s, c, cd:cd + 1], in1=r.ap,
                        op0=ALU.mult, op1=ALU.mult), reads=[x_b[tb][c], r, drv], writes=[t])
                    bld.op(act, lambda t=t, c=c, tb=tb, cd=cd: A.activation(
                        out=hT[:, c, tbcols(tb)], in_=t.ap, func=AF.Identity, bias=drv.ap[:, 1, s, c, cd:cd + 1], scale=1.0),
                        reads=[t, drv], writes=[h_b[tb]])
            _fence_scope([q for q in sq.tiles] + rs.tiles + tm.tiles)

    pending_fence = list(setup_fence)

    def _fence_scope(tiles):
        for t in tiles:
            pending_fence.append(t.buf)

    grp = {"prev": None, "cur": None}

    def _fence_for(name):
        prev = grp["prev"]
        if prev is not None and name in prev:
            return [prev[name]]
        return pending_fence

    def scoped(name, shape, dt, stack):
        t = Tile(sb(name, shape, dt, stack)[:], name)
        if grp["cur"] is not None:
            grp["cur"][name] = t.buf
        rr = {}
        for b in _fence_for(name):
            if b.w is not None:
                o, c = b.w
                rr[o] = max(rr.get(o, 0), c)
            for o, c in b.r.items():
                rr[o] = max(rr.get(o, 0), c)
        t.buf.r = rr
        return t

    def ffn(l, f, s, mod_l=None):
        w1, w3, w2 = w1_d[l][f], w3_d[l][f], w2_d[l][f]
        with contextlib.ExitStack() as s2:
            between = None
            if mod_l is not None:
                mring = Ring([scoped(f"mr{i}", [128, 2048], BF16, s2) for i in range(3)])
                between = lambda sgi: [mod_piece(mod_l, pi, ring=mring) for pi in range(sgi * 6, sgi * 6 + 6)]
            g_t = sb("g_t", [128, 4, T], BF16, s2)
            g_tiles = [[scoped_buf(f"g{j}_{tb}") for tb in range(NTB)] for j in range(4)]
            su = Ring([scoped(f"su{i}", [128, 512], BF16, s2) for i in range(3)])
            ps_uv = Ring(banks[0:4])
            ps_o = Ring(banks[4:7] if mod_ps is not None else banks[4:8])
            sgs = [(i * 4, 4) for i in range(5)] + [(20, 2)]
            for sgi, (c0, ncn) in enumerate(sgs):
                halves = [(c0 + 2 * hh, min(2, ncn - 2 * hh)) for hh in range((ncn + 1) // 2)]
                slots = []
                for (cc, n2) in halves:
                    sa = wring.next()
                    wa = sa.ap.rearrange("p (k n) -> p k n", k=8)
                    wload(wa[:, :, 0:n2 * 128], w1[:, cc * 128:(cc + n2) * 128].rearrange("(k p) n -> p k n", p=128), sa)
                    sb_ = wring.next()
                    wb = sb_.ap.rearrange("p (k n) -> p k n", k=8)
                    wload(wb[:, :, 0:n2 * 128], w3[:, cc * 128:(cc + n2) * 128].rearrange("(k p) n -> p k n", p=128), sb_)
                    slots.append((sa, wa, sb_, wb))
                slots2 = []
                for (cc, n2) in halves:
                    sc = wring.next()
                    wc = sc.ap.rearrange("p (j n) -> p j n", j=2)
                    wload(wc[:, 0:n2, :], w2[cc * 128:(cc + n2) * 128, :].rearrange("(j p) n -> p j n", p=128), sc)
                    slots2.append((sc, wc))
                for j in range(ncn):
                    sa, wa, sb_, wb = slots[j // 2]
                    jj = j % 2
                    for tb in range(NTB):
                        pu = ps_uv.next()
                        bld.mm_group(pu, pu.ap, [(wa[:, k, jj * 128:(jj + 1) * 128], hT[:, k, tbcols(tb)]) for k in range(8)],
                                     reads=[sa, h_b[tb]])
                        pv = ps_uv.next()
                        bld.mm_group(pv, pv.ap, [(wb[:, k, jj * 128:(jj + 1) * 128], hT[:, k, tbcols(tb)]) for k in range(8)],
                                     reads=[sb_, h_b[tb]])
                        sut = su.next()
                        bld.op(act, lambda sut=sut, pu=pu: A.activation(out=sut.ap, in_=pu.ap, func=AF.Silu),
                               reads=[pu], writes=[sut])
                        bld.op(dve, lambda j=j, tb=tb, pv=pv, sut=sut: V.tensor_tensor(
                            out=g_t[:, j, tbcols(tb)], in0=pv.ap, in1=sut.ap, op=ALU.mult),
                            reads=[pv, sut], writes=[g_tiles[j][tb]])
                for tb in range(NTB):
                    cd = cond_of(tb)
                    for n in range(NCH):
                        po = ps_o.next()
                        pairs = []
                        for j in range(ncn):
                            sc, wc = slots2[j // 2]
                            pairs.append((wc[:, j % 2, n * 128:(n + 1) * 128], g_t[:, j, tbcols(tb)]))
                        bld.mm_group(po, po.ap, pairs, reads=[sl[0] for sl in slots2] + [g_tiles[j][tb] for j in range(ncn)])
                        bld.op(dve, lambda po=po, n=n, tb=tb, cd=cd: V.scalar_tensor_tensor(
                            out=xT[:, n, tbcols(tb)], in0=po.ap, scalar=drv.ap[:, 2, s, n, cd:cd + 1], in1=xT[:, n, tbcols(tb)],
                            op0=ALU.mult, op1=ALU.add), reads=[po, drv, x_b[tb][n]], writes=[x_b[tb][n]])
                if between is not None:
                    between(sgi)
            if dbg == 2 and l == 0 and f == 0:
                dbg4_d = dout("dbg4", [128, 4, 128])
                bld.dma(pool, dbg4_d[:, :, 0:64], g_t[:, :, 0:64], reads=[g_tiles[j][0] for j in range(4)])
                bld.dma(pool, dbg4_d[:, :, 64:128], g_t[:, :, 2048:2112], reads=[g_tiles[j][4] for j in range(4)])
            for j in range(4):
                for tb in range(NTB):
                    pending_fence.append(g_tiles[j][tb])
            _fence_scope(su.tiles)
            if mod_l is not None:
                _fence_scope(mring.tiles)

    def scoped_buf(name):
        b = Buf(name)
        if grp["cur"] is not None:
            grp["cur"][name] = b
        rr = {}
        for pb in _fence_for(name):
            if pb.w is not None:
                o, c = pb.w
                rr[o] = max(rr.get(o, 0), c)
            for o, c in pb.r.items():
                rr[o] = max(rr.get(o, 0), c)
        b.r = rr
        return b

    def final():
        with contextlib.ExitStack() as s2:
            sq = scoped("fsq", [128, NCH, 512], BF16, s2)
            rs = scoped("frs", [128, 512], F32, s2)
            yt = Ring([scoped(f"fy{i}", [128, 512], F32, s2) for i in range(3)])
            ot_t = sb("fot", [128, 2, 4, D], F32, s2)
            ot = Ring([scoped_view(ot_t[:, i], f"fot{i}") for i in range(2)])
            for tb in range(NTB):
                bld.op(act, lambda tb=tb: A.activation(out=sq.ap, in_=xT[:, :, tbcols(tb)], func=AF.Square),
                       reads=x_b[tb], writes=[sq])
                ps = ps_all.next()
                bld.mm_group(ps, ps.ap, [(ones_bf, sq.ap[:, c, :]) for c in range(NCH)], reads=[sq, cm])
                rstd_op(ps, rs, D, EPS)
                o = ot.next()
                for c in range(NCH):
                    y = yt.next()
                    bld.op(dve, lambda y=y, c=c, tb=tb: V.scalar_tensor_tensor(
                        out=y.ap, in0=xT[:, c, tbcols(tb)], scalar=prm.ap[:, 96 + c:97 + c], in1=rs.ap,
                        op0=ALU.mult, op1=ALU.mult), reads=[x_b[tb][c], rs, prm], writes=[y])
                    ps2 = ps_all.next()
                    for tt in range(4):
                        bld.op(pe, lambda ps2=ps2, y=y, tt=tt: nc.tensor.transpose(
                            out=ps2.ap[:, tt * 128:(tt + 1) * 128], in_=y.ap[:, tt * 128:(tt + 1) * 128], identity=ident.ap),
                            reads=[y, ident] if tt == 0 else (), writes=[ps2] if tt == 0 else (), sig=(tt == 3))
                    dst = o.ap[:, :, c * 128:(c + 1) * 128]
                    srcp = ps2.ap.rearrange("p (t n) -> p t n", t=4)
                    if c % 2 == 0:
                        bld.op(act, lambda dst=dst, srcp=srcp: A.copy(out=dst, in_=srcp), reads=[ps2], writes=[o])
                    else:
                        bld.op(dve, lambda dst=dst, srcp=srcp: V.tensor_copy(out=dst, in_=srcp), reads=[ps2], writes=[o])
                for tt in range(4):
                    i = tb * 4 + tt
                    dst = ys_d[i * 128:(i + 1) * 128, :] if i < 16 else yp_d[(i - 16) * 128:(i - 15) * 128, :]
                    bld.dma(sp, dst, o.ap[:, tt, :], reads=[o])

    def scoped_view(ap, name):
        t = Tile(ap, name)
        t.buf = scoped_buf(name)
        return t


    def pipeline(n, stages):
        ns = len(stages)
        for step in range(n + ns - 1):
            for st in range(ns):
                i = step - st
                if 0 <= i < n:
                    stages[st](i)

    last_group = {"kind": None, "bufs": None}
    prefetched = {}

    def nb_ranges(j, c):
        out = []
        for krl in range(2):
            kr = 2 * c + krl
            valid = []
            for qr in range(8 * j, 8 * j + 8):
                rs_ = min(max(qr - 4, 0), 32 - 8)
                if rs_ <= kr < rs_ + 8:
                    valid.append(qr)
            if not valid:
                out.append(None)
                continue
            qa, qb = valid[0], valid[-1]
            assert valid == list(range(qa, qb + 1))
            i0 = 7 - kr + qa
            assert 0 <= i0 and i0 + (qb - qa) <= 14
            out.append(((qa - 8 * j) * 64, (qb - 8 * j + 1) * 64, i0))
        return out

    def group_cols(kind, gi):
        if kind == "A":
            kv = gi // 2
            return (slice(128 * gi, 128 * gi + 128), slice(512 + 64 * kv, 512 + 64 * kv + 64),
                    slice(640 + 64 * kv, 640 + 64 * kv + 64), 128 * gi, 64)
        if kind == "B":
            return (slice(768 + 128 * gi, 768 + 128 * gi + 128), slice(1280 + 128 * gi, 1280 + 128 * gi + 128),
                    slice(1792 + 128 * gi, 1792 + 128 * gi + 128), 512 + 128 * gi, 128)
        return (slice(128 * gi, 128 * gi + 128), slice(1024 + 128 * gi, 1024 + 128 * gi + 128),
                slice(2048 + 128 * gi, 2048 + 128 * gi + 128), 128 * gi, 128)

    def load_group_weights(kind, l, gi):
        ei = l // 2
        win = winab_d[ei] if kind != "C" else winc_d[ei]
        wout = woutab_d[ei] if kind != "C" else woutc_d[ei]
        qcols, kcols, vcols, orow, nv = group_cols(kind, gi)
        s1 = wring.next()
        w1v = s1.ap.rearrange("p (k n) -> p k n", k=8)
        wload(w1v[:, :, 0:128], win[:, qcols].rearrange("(k p) n -> p k n", p=128), s1)
        if kind == "A":
            wload(w1v[:, :, 128:192], win[:, kcols].rearrange("(k p) n -> p k n", p=128), s1)
            wload(w1v[:, :, 192:256], win[:, kcols].rearrange("(k p) n -> p k n", p=128), s1)
        else:
            wload(w1v[:, :, 128:256], win[:, kcols].rearrange("(k p) n -> p k n", p=128), s1)
        s2w = wring.next()
        wo = s2w.ap[:, 0:1024]
        wvv = s2w.ap[:, 1024:2048].rearrange("p (k n) -> p k n", k=8)
        wload(wo, wout[orow:orow + 128, :], s2w)
        wload(wvv[:, :, 0:nv], win[:, vcols].rearrange("(k p) n -> p k n", p=128), s2w)
        return (s1, w1v, s2w, wo, wvv)

    def run_group(kind, l, gi, nxt=None):
        even = (kind != "C")
        ei = l // 2
        cd_s = 1
        win = winab_d[ei] if even else winc_d[ei]
        wout = woutab_d[ei] if even else woutc_d[ei]
        if kind == "A":
            kv = gi // 2
            qcols = slice(128 * gi, 128 * gi + 128)
            kcols = slice(512 + 64 * kv, 512 + 64 * kv + 64)
            vcols = slice(640 + 64 * kv, 640 + 64 * kv + 64)
            orow = 128 * gi
            nv = 64
        elif kind == "B":
            qcols = slice(768 + 128 * gi, 768 + 128 * gi + 128)
            kcols = slice(1280 + 128 * gi, 1280 + 128 * gi + 128)
            vcols = slice(1792 + 128 * gi, 1792 + 128 * gi + 128)
            orow = 512 + 128 * gi
            nv = 128
        else:
            qcols = slice(128 * gi, 128 * gi + 128)
            kcols = slice(1024 + 128 * gi, 1024 + 128 * gi + 128)
            vcols = slice(2048 + 128 * gi, 2048 + 128 * gi + 128)
            orow = 128 * gi
            nv = 128
        rope = kind in ("A", "C")
        grp["prev"] = last_group["bufs"] if (last_group["kind"] == kind and not os.environ.get("K_NOPF")) else None
        grp["cur"] = {}
        with contextlib.ExitStack() as s2:
            if kind == "B":
                TF = scoped("TF", [128, 2, 960], BF16, s2)
                with contextlib.ExitStack() as s3:
                    tfst = scoped(f"tfst_{l}_{gi}", [128, 960], F32, s3)
                    cmx = scoped(f"cmx_{l}_{gi}", [128, 960], F32, s3)
                    bld.dma(sp, cmx.ap, colmaskx_d, writes=[cmx])
                    for hh in range(2):
                        bld.dma(sp, tfst.ap, rpbx_d[ei, 2 * gi + hh], writes=[tfst])
                        bld.op(act, lambda: A.activation(out=tfst.ap, in_=tfst.ap, func=AF.Exp), reads=[tfst], writes=[tfst])
                        bld.op(dve, lambda hh=hh: V.tensor_tensor(out=TF.ap[:, hh, :], in0=tfst.ap, in1=cmx.ap, op=ALU.mult),
                               reads=[tfst, cmx], writes=[TF])
                    _fence_scope([tfst, cmx])

            Qz_t = sb("Qz", [128, 2, T], BF16, s2)
            OT_t = sb("OT", [128, T], BF16, s2)
            KT_t = sb("KT", [128, T], BF16, s2)
            ot_b = [scoped_buf(f"ot{q}") for q in range(6)]
            qo_b = [[scoped_buf(f"qo{hh}_{q}") for q in range(6)] for hh in range(2)]
            kt_b = [scoped_buf(f"kt{tb}") for tb in range(NTB)]
            KTc = scoped("KTc", [128, 512], BF16, s2)
            bld.op(dve, lambda: V.memset(Qz_t[64:128, 0, :], 0.0), writes=qo_b[0])
            bld.op(dve, lambda: V.memset(Qz_t[0:64, 1, :], 0.0), writes=qo_b[1])
            nh = 2 if kind == "B" else 1
            VV_t = sb("VV", [128, NTT, nh, 128], BF16, s2)
            vv_b = [scoped_buf(f"vv{tt}") for tt in range(NTT)]
            VVc = scoped("VVc", [128, 4, nh, 128], BF16, s2)
            Fr = Ring([scoped(f"F{i}", [128, 512], F32, s2) for i in range(6)])
            Hr = Ring([scoped(f"H{i}", [128, 512], BF16, s2) for i in range(6)])
            if rope:
                csr = Ring([scoped(f"cs{i}", [128, 2, 512], F32, s2) for i in range(2)])
                cs_tb = {}
                for tb_ in range(4):
                    cst = csr.tiles[tb_ % 2]
                    cs_tb[tb_] = cst
            wts = prefetched.pop((kind, l, gi), None) or load_group_weights(kind, l, gi)
            s1, w1v, s2w, wo, wvv = wts
            kst = Fr.next()
            kstv = kst.ap.rearrange("p (t c) -> p t c", t=4)
            if kind == "A":
                src = cak_d[ei][:, 64 * kv:64 * kv + 64].rearrange("(t p) c -> p t c", p=128)
                bld.dma(sp, kstv[:, :, 0:64], src, writes=[kst])
                bld.dma(sp, kstv[:, :, 64:128], src, writes=[kst])
                vsrc = cav_d[ei][:, 64 * kv:64 * kv + 64].rearrange("(t p) c -> p t c", p=128)
                bld.dma(pool, VVc.ap[:, :, 0, 0:64], vsrc, writes=[VVc])
            elif kind == "B":
                bld.dma(sp, kstv, cbk_d[ei][:, 128 * gi:128 * gi + 128].rearrange("(t p) c -> p t c", p=128), writes=[kst])
                for h_ in range(2):
                    vsrc = cbv_d[ei][:, 128 * gi + 64 * h_:128 * gi + 64 * h_ + 64].rearrange("(t p) c -> p t c", p=128)
                    bld.dma(pool, VVc.ap[:, :, h_, 0:64], vsrc, writes=[VVc])
            else:
                bld.dma(sp, kstv, cck_d[ei][:, 128 * gi:128 * gi + 128].rearrange("(t p) c -> p t c", p=128), writes=[kst])
                vsrc = ccv_d[ei][:, 128 * gi:128 * gi + 128].rearrange("(t p) c -> p t c", p=128)
                bld.dma(pool, VVc.ap[:, :, 0, :], vsrc, writes=[VVc])
            if kind != "C":
                bld.op(dve, lambda: V.memset(VVc.ap[:, :, :, 64:128], 1.0), writes=[VVc])
                bld.op(dve, lambda: V.memset(VV_t[:, :, :, 64:128], 1.0), writes=vv_b)
            pk = banks[7]
            for t4 in range(4):
                bld.op(pe, lambda t4=t4: nc.tensor.transpose(out=pk.ap[:, t4 * 128:(t4 + 1) * 128], in_=kstv[:, t4, :], identity=ident.ap),
                       reads=[kst, ident] if t4 == 0 else (), writes=[pk] if t4 == 0 else (), sig=(t4 == 3))
            bld.op(dve, lambda: V.tensor_copy(out=KTc.ap, in_=pk.ap), reads=[pk], writes=[KTc])

            ps_pr = Ring(banks[0:4])
            ps_x = Ring(banks[4:7])
            tiles = [(w, tb) for tb in range(NTB) for w in (0, 1)]
            cs_loaded = set()
            st = {}
            nrm_col = 104 + ei * 2

            def final_write(w, tb, mk, reads):
                if w == 1:
                    bld.op(dve, lambda: mk(KT_t[:, tbcols(tb)], slice(0, 128)), reads=reads, writes=dst_bufs(w, tb))
                else:
                    for hh_ in range(2):
                        pr_ = slice(hh_ * 64, hh_ * 64 + 64)
                        wb_ = [qo_b[hh_][tb]] if tb < 4 else [qo_b[hh_][4], qo_b[hh_][5]]
                        bld.op(dve, lambda: mk(Qz_t[pr_, hh_, tbcols(tb)], pr_), reads=reads, writes=wb_)

            def dst_bufs(w, tb):
                if w == 1:
                    return [kt_b[tb]]
                if tb < 4:
                    return [qo_b[0][tb], qo_b[1][tb]]
                return [qo_b[0][4], qo_b[1][4], qo_b[0][5], qo_b[1][5]]

            def st0(i):
                w, tb = tiles[i]
                ps = ps_pr.next()
                bld.mm_group(ps, ps.ap, [(w1v[:, k, w * 128:(w + 1) * 128], hT[:, k, tbcols(tb)]) for k in range(8)],
                             reads=[s1, h_b[tb]])
                d = {"ps": ps}
                st[i] = d
                if kind == "B" or (kind == "C" and tb == 4):
                    final_write(w, tb, lambda o_, pr_: V.tensor_copy(out=o_, in_=ps.ap[pr_, :]), [ps])
                elif kind == "A":
                    sq = Hr.next()
                    bld.op(act, lambda: A.activation(out=sq.ap, in_=ps.ap, func=AF.Square), reads=[ps], writes=[sq])
                    d["sq"] = sq
                else:
                    qb_ = Hr.next()
                    bld.op(act, lambda: A.copy(out=qb_.ap, in_=ps.ap), reads=[ps], writes=[qb_])
                    d["qb"] = qb_

            def st1(i):
                w, tb = tiles[i]
                d = st[i]
                ps = d["ps"]
                if kind == "A":
                    pss = ps_x.next()
                    bld.mm(pss.ap, bd_bf, d["sq"].ap, True, True, reads=[d["sq"], cm], writes=[pss])
                    r = Fr.next()
                    rstd_op(pss, r, 64, EPS)
                    wcol = prm.ap[:, nrm_col + w:nrm_col + w + 1]
                    if tb == 4:
                        final_write(w, tb, lambda o_, pr_: V.scalar_tensor_tensor(out=o_, in0=ps.ap[pr_, :], scalar=wcol[pr_, :], in1=r.ap[pr_, :],
                                                                                  op0=ALU.mult, op1=ALU.mult), [ps, r, prm])
                    else:
                        qn = Fr.next()
                        bld.op(dve, lambda: V.scalar_tensor_tensor(out=qn.ap, in0=ps.ap, scalar=wcol, in1=r.ap,
                                                                   op0=ALU.mult, op1=ALU.mult),
                               reads=[ps, r, prm], writes=[qn])
                        qb_ = Hr.next()
                        bld.op(act, lambda: A.copy(out=qb_.ap, in_=qn.ap), reads=[qn], writes=[qb_])
                        d["qn"] = qn
                        d["qb"] = qb_
                if rope and tb < 4:
                    psr = ps_x.next()
                    bld.mm(psr.ap, rot_bf, d["qb"].ap, True, True, reads=[d["qb"], cm], writes=[psr])
                    d["psr"] = psr

            def st2(i):
                w, tb = tiles[i]
                d = st[i]
                if not (rope and tb < 4):
                    return
                cs = cs_tb[tb]
                if (w, tb) not in cs_loaded:
                    if not any(k_[1] == tb for k_ in cs_loaded):
                        bld.dma(sp, cs.ap, cossin_d[:, :, tbcols(tb)].rearrange("m p t -> p m t"), writes=[cs])
                    cs_loaded.add((w, tb))
                cosv = cs.ap[:, 0, :]
                sinv = cs.ap[:, 1, :]
                t2 = Fr.next()
                bld.op(dve, lambda: V.tensor_tensor(out=t2.ap, in0=d["psr"].ap, in1=sinv, op=ALU.mult),
                       reads=[d["psr"], cs], writes=[t2])
                if kind == "A":
                    t1 = d["qn"]
                    bld.op(dve, lambda: V.tensor_tensor(out=t1.ap, in0=t1.ap, in1=cosv, op=ALU.mult), reads=[t1, cs], writes=[t1])
                else:
                    t1 = Fr.next()
                    bld.op(dve, lambda: V.tensor_tensor(out=t1.ap, in0=d["ps"].ap, in1=cosv, op=ALU.mult),
                           reads=[d["ps"], cs], writes=[t1])
                final_write(w, tb, lambda o_, pr_: V.tensor_tensor(out=o_, in0=t1.ap[pr_, :], in1=t2.ap[pr_, :], op=ALU.add), [t1, t2])

            if stage >= 1:
                pipeline(len(tiles), [st0, st1, st2])

            own_out = (kind != "A") or (gi % 2 == 0)
            if kind == "A":
                kd, vd, oc = nak_d, nav_d, slice(64 * kv, 64 * kv + 64)
            elif kind == "B":
                kd, vd, oc = nbk_d, nbv_d, slice(128 * gi, 128 * gi + 128)
            else:
                kd, vd, oc = nck_d, ncv_d, slice(128 * gi, 128 * gi + 128)
            ps_v = Ring(banks[0:4])
            for tt in (range(NTT) if stage >= 2 else ()):
                ps = ps_v.next()
                bld.mm_group(ps, ps.ap[:, 0:nv], [(hT[:, k, tt * 128:(tt + 1) * 128], wvv[:, k, 0:nv]) for k in range(8)],
                             reads=[s2w, h_b[tt // 4]])
                if kind == "B":
                    bld.op(dve, lambda ps=ps, tt=tt: V.tensor_copy(out=VV_t[:, tt, :, 0:64], in_=ps.ap[:, 0:128].rearrange("p (h c) -> p h c", h=2)),
                           reads=[ps], writes=[vv_b[tt]])
                else:
                    bld.op(dve, lambda ps=ps, tt=tt: V.tensor_copy(out=VV_t[:, tt, 0, 0:nv], in_=ps.ap[:, 0:nv]), reads=[ps], writes=[vv_b[tt]])
                if tt >= 16 and own_out and stage >= 2.3:
                    sq_, r0 = (tt - 16) // 2, ((tt - 16) % 2) * 128
                    f = Fr.next()
                    if os.environ.get("K_DVECOPY"):
                        bld.op(dve, lambda ps=ps, f=f: V.tensor_copy(out=f.ap[:, 0:nv], in_=ps.ap[:, 0:nv]), reads=[ps], writes=[f])
                    else:
                        bld.op(act, lambda ps=ps, f=f: A.copy(out=f.ap[:, 0:nv], in_=ps.ap[:, 0:nv]), reads=[ps], writes=[f])
                    if not os.environ.get("K_NODMA"):
                        bld.dma(sp, vd[sq_, ei, r0:r0 + 128, oc], f.ap[:, 0:nv], reads=[f])
                    if stage < 2.6:
                        continue
                    psk = ps_v.next()
                    bld.mm_group(psk, psk.ap[:, 0:nv], [(hT[:, k, tt * 128:(tt + 1) * 128], w1v[:, k, 128:128 + nv]) for k in range(8)],
                                 reads=[s1, h_b[4]])
                    fk = Fr.next()
                    if kind == "A" and stage >= 2.9:
                        junk = Fr.next()
                        ssk = Fr.next()
                        bld.op(act, lambda: A.activation(out=junk.ap[:, 0:64], in_=psk.ap[:, 0:64], func=AF.Square), reads=[psk], writes=[junk])
                        bld.op(dve, lambda: V.reduce_sum(out=ssk.ap[:, 0:1], in_=junk.ap[:, 0:64], axis=mybir.AxisListType.X),
                               reads=[junk], writes=[ssk])
                        bld.op(act, lambda: A.activation(out=ssk.ap[:, 0:1], in_=ssk.ap[:, 0:1], func=AF.Ln, bias=epsc.ap[:, 0:1], scale=1.0 / 64),
                               reads=[ssk, epsc], writes=[ssk])
                        bld.op(act, lambda: A.activation(out=ssk.ap[:, 0:1], in_=ssk.ap[:, 0:1], func=AF.Exp, scale=-0.5), reads=[ssk], writes=[ssk])
                        bld.op(dve, lambda: V.scalar_tensor_tensor(out=fk.ap[:, 0:64], in0=psk.ap[:, 0:64], scalar=ssk.ap[:, 0:1], in1=kw8.ap,
                                                                   op0=ALU.mult, op1=ALU.mult), reads=[psk, ssk, kw8], writes=[fk])
                    else:
                        bld.op(act, lambda: A.copy(out=fk.ap[:, 0:nv], in_=psk.ap[:, 0:nv]), reads=[psk], writes=[fk])
                    bld.dma(sp, kd[sq_, ei, r0:r0 + 128, oc], fk.ap[:, 0:nv], reads=[fk])
                    if dbg == 3 and kind == "A" and tt == 16 and gi == 0:
                        dbg5_d = dout("dbg5", [128, 4, 64])
                        bld.dma(sp, dbg5_d[:, 0, :], junk.ap[:, 0:64], reads=[junk])
                        bld.dma(sp, dbg5_d[:, 1, :], ssk.ap[:, 0:64], reads=[ssk])
                        bld.dma(sp, dbg5_d[:, 2, :], fk.ap[:, 0:64], reads=[fk])
                        bld.dma(sp, dbg5_d[:, 3, :], kw8.ap[:, 0:64], reads=[kw8])

            if nxt is not None and nxt not in prefetched:
                prefetched[nxt] = load_group_weights(*nxt)

            S_ring = Ring(banks[0:4])
            acc_ring = Ring(banks[4:6])
            qblocks = [(q, slice(q * 512, (q + 1) * 512), 512) for q in range(4)] + \
                      [(4 + s_, slice(TS + 256 * s_, TS + 256 * s_ + 256), 256) for s_ in range(2)]

            def key_chunks(q):
                res = []
                if q < 4:
                    res += [("cache", c) for c in range(4)]
                    if kind == "B":
                        rows = set()
                        for qr in range(8 * q, 8 * q + 8):
                            rs_ = min(max(qr - 4, 0), 24)
                            rows.update(range(rs_, rs_ + 8))
                        res += [("nb", c) for c in range(min(rows) // 2, max(rows) // 2 + 1)]
                    else:
                        res += [("own", c) for c in range(16)]
                else:
                    s_ = q - 4
                    res += [("own", 16 + 2 * s_), ("own", 17 + 2 * s_)]
                return res

            def kT_of(typ, c, pr):
                if typ == "cache":
                    return KTc.ap[pr, c * 128:(c + 1) * 128], [KTc]
                return KT_t[pr, c * 128:(c + 1) * 128], [kt_b[c // 4]]

            def v_of(typ, c, hh, pr=slice(0, 128)):
                if typ == "cache":
                    return VVc.ap[pr, c, hh, :], [VVc]
                return VV_t[pr, c, hh, :], [vv_b[c]]

            items = []
            if stage < 3:
                pass
            elif kind in ("A", "B"):
                for hh in range(2):
                    pr = slice(hh * 64, hh * 64 + 64)
                    vh = hh if kind == "B" else 0
                    for (q, qc, nq) in qblocks:
                        chunks = key_chunks(q)
                        blk = {"acc": None}
                        for ci, (typ, c) in enumerate(chunks):
                            first, last = ci == 0, ci == len(chunks) - 1

                            def s_stage(d, typ=typ, c=c, pr=pr, q=q, qc=qc, nq=nq, hh=hh):
                                kT, kr_ = kT_of(typ, c, slice(0, 128))
                                if typ == "nb":
                                    rng = nb_ranges(q, c)
                                    act_r = [r_ for r_ in rng if r_ is not None]
                                    cu0, cu1 = min(r_[0] for r_ in act_r), max(r_[1] for r_ in act_r)
                                else:
                                    rng = None
                                    cu0, cu1 = 0, nq
                                S = S_ring.next()
                                qs = Qz_t[:, hh, qc.start + cu0:qc.start + cu1]
                                bld.mm(S.ap[:, cu0:cu1], kT, qs, True, True, reads=kr_ + [qo_b[hh][q]], writes=[S])
                                P = Hr.next()
                                bld.op(act, lambda: A.activation(out=P.ap[:, cu0:cu1], in_=S.ap[:, cu0:cu1], func=AF.Exp, scale=SCALE),
                                       reads=[S], writes=[P])
                                if typ == "nb":
                                    for krl in range(2):
                                        kp = slice(krl * 64, krl * 64 + 64)
                                        if rng[krl] is None:
                                            zr = [(cu0, cu1)]
                                        else:
                                            a0, a1, i0 = rng[krl]
                                            bld.op(dve, lambda: V.tensor_tensor(
                                                out=P.ap[kp, a0:a1], in0=P.ap[kp, a0:a1], in1=TF.ap[kp, hh, i0 * 64:i0 * 64 + (a1 - a0)], op=ALU.mult),
                                                reads=[P, TF], writes=[P])
                                            zr = [(cu0, a0), (a1, cu1)]
                                        for (z0, z1) in zr:
                                            if z1 > z0:
                                                bld.op(dve, lambda: V.memset(P.ap[kp, z0:z1], 0.0), writes=[P])
                                d["P"], d["cu"] = P, (cu0, cu1)

                            def pv_stage(d, typ=typ, c=c, vh=vh, first=first, last=last, blk=blk, nq=nq):
                                if first:
                                    blk["acc"] = acc_ring.next()
                                acc = blk["acc"]
                                P = d["P"]
                                cu0, cu1 = d["cu"]
                                vl, vr_ = v_of(typ, c, vh)
                                kw = {"skip_group_check": True} if kind == "B" else {}
                                bld.mm(acc.ap[:, cu0:cu1], vl, P.ap[:, cu0:cu1], first, last, reads=vr_ + [P], writes=[acc], **kw)

                            fin = None
                            if last:
                                def fin(blk=blk, nq=nq, qc=qc, pr=pr, hh=hh, q=q):
                                    acc = blk["acc"]
                                    rc = Fr.next()
                                    bld.op(dve, lambda: V.reciprocal(out=rc.ap[0:64, 0:nq], in_=acc.ap[64:128, 0:nq]), reads=[acc], writes=[rc])
                                    bld.op(dve, lambda: V.tensor_tensor(out=OT_t[pr, qc], in0=acc.ap[0:64, 0:nq], in1=rc.ap[0:64, 0:nq], op=ALU.mult),
                                           reads=[acc, rc], writes=[ot_b[q]])
                            items.append((s_stage, pv_stage, fin))
                LA = 2
            else:
                accC = Ring([(banks[4], banks[5]), (banks[6], banks[7])])
                dab = Ring([scoped(f"dab{i}", [128, 512], BF16, s2) for i in range(4)])
                blk_state = {"pending": []}
                for bi, (q, qc, nq) in enumerate(qblocks):
                    chunks = key_chunks(q)
                    blk = {}
                    for ci, (typ, c) in enumerate(chunks):
                        first, last = ci == 0, ci == len(chunks) - 1

                        def s_stage(d, typ=typ, c=c, q=q, qc=qc, nq=nq):
                            d["P"] = []
                            for mj in range(2):
                                kT, kr_ = kT_of(typ, c, slice(0, 128))
                                S = S_ring.next()
                                bld.mm(S.ap[:, 0:nq], kT, Qz_t[:, mj, qc], True, True, reads=kr_ + [qo_b[mj][q]], writes=[S])
                                P = Hr.next()
                                bld.op(act, lambda: A.activation(out=P.ap[:, 0:nq], in_=S.ap[:, 0:nq], func=AF.Exp, scale=SCALE),
                                       reads=[S], writes=[P])
                                d["P"].append(P)

                        def pv_stage(d, typ=typ, c=c, first=first, last=last, nq=nq, blk=blk):
                            if first:
                                while len(blk_state["pending"]) >= 2:
                                    run_chain(blk_state["pending"].pop(0))
                                blk["O"] = accC.next()
                                blk["DA"] = (Fr.next(), Fr.next())
                            vl, vr_ = v_of(typ, c, 0)
                            O1, O2 = blk["O"]
                            DA1, DA2 = blk["DA"]
                            P1, P2 = d["P"]
                            bld.mm(O1.ap[:, 0:nq], vl, P1.ap[:, 0:nq], first, last, reads=vr_ + [P1], writes=[O1])
                            bld.mm(O2.ap[:, 0:nq], vl, P2.ap[:, 0:nq], first, last, reads=vr_ + [P2], writes=[O2])
                            if first:
                                bld.op(pool, lambda: G.tensor_copy(out=DA1.ap[:, 0:nq], in_=P1.ap[:, 0:nq]), reads=[P1], writes=[DA1])
                                bld.op(dve, lambda: V.tensor_copy(out=DA2.ap[:, 0:nq], in_=P2.ap[:, 0:nq]), reads=[P2], writes=[DA2])
                            else:
                                bld.op(pool, lambda: G.tensor_tensor(out=DA1.ap[:, 0:nq], in0=DA1.ap[:, 0:nq], in1=P1.ap[:, 0:nq], op=ALU.add),
                                       reads=[P1, DA1], writes=[DA1])
                                bld.op(dve, lambda: V.tensor_tensor(out=DA2.ap[:, 0:nq], in0=DA2.ap[:, 0:nq], in1=P2.ap[:, 0:nq], op=ALU.add),
                                       reads=[P2, DA2], writes=[DA2])

                        fin = None
                        if last:
                            def fin(nq=nq, qc=qc, q=q, blk=blk):
                                O1, O2 = blk["O"]
                                DA1, DA2 = blk["DA"]
                                b1, b2 = dab.next(), dab.next()
                                bld.op(dve, lambda: V.tensor_copy(out=b1.ap[:, 0:nq], in_=DA1.ap[:, 0:nq]), reads=[DA1], writes=[b1])
                                bld.op(dve, lambda: V.tensor_copy(out=b2.ap[:, 0:nq], in_=DA2.ap[:, 0:nq]), reads=[DA2], writes=[b2])

                                def fin_b():
                                    r1, r2 = DA1, DA2
                                    pd1 = S_ring.next()
                                    bld.mm(pd1.ap[:, 0:nq], ones_bf, b1.ap[:, 0:nq], True, True, reads=[b1, cm], writes=[pd1])
                                    pd2 = S_ring.next()
                                    bld.mm(pd2.ap[:, 0:nq], ones_bf, b2.ap[:, 0:nq], True, True, reads=[b2, cm], writes=[pd2])
                                    bld.op(dve, lambda: V.reciprocal(out=r1.ap[:, 0:nq], in_=pd1.ap[:, 0:nq]), reads=[pd1], writes=[r1])
                                    bld.op(dve, lambda: V.tensor_tensor(out=r1.ap[:, 0:nq], in0=O1.ap[:, 0:nq], in1=r1.ap[:, 0:nq], op=ALU.mult),
                                           reads=[O1, r1], writes=[r1])
                                    bld.op(dve, lambda: V.reciprocal(out=r2.ap[:, 0:nq], in_=pd2.ap[:, 0:nq]), reads=[pd2], writes=[r2])
                                    bld.op(dve, lambda: V.tensor_tensor(out=r2.ap[:, 0:nq], in0=O2.ap[:, 0:nq], in1=r2.ap[:, 0:nq], op=ALU.mult),
                                           reads=[O2, r2], writes=[r2])
                                    bld.op(dve, lambda: V.scalar_tensor_tensor(out=r1.ap[:, 0:nq], in0=r2.ap[:, 0:nq], scalar=lamt.ap[:, 0:1], in1=r1.ap[:, 0:nq],
                                                                               op0=ALU.mult, op1=ALU.add), reads=[r1, r2, lamt], writes=[r1])
                                    sqc = b1
                                    bld.op(act, lambda: A.activation(out=sqc.ap[:, 0:nq], in_=r1.ap[:, 0:nq], func=AF.Square), reads=[r1], writes=[sqc])

                                    def fin_c():
                                        pss = S_ring.next()
                                        bld.mm(pss.ap[:, 0:nq], ones_bf, sqc.ap[:, 0:nq], True, True, reads=[sqc, cm], writes=[pss])
                                        rstd_op(pss, r2, 128, EPS, cols=slice(0, nq))
                                        bld.op(dve, lambda: V.scalar_tensor_tensor(out=OT_t[:, qc], in0=r1.ap[:, 0:nq], scalar=lamt.ap[:, 1:2], in1=r2.ap[:, 0:nq],
                                                                                   op0=ALU.mult, op1=ALU.mult), reads=[r1, r2, lamt], writes=[ot_b[q]])
                                        return None
                                    return fin_c
                                return fin_b
                        items.append((s_stage, pv_stage, fin))
                LA = 1
            if items:
                ds_ = [dict() for _ in items]
                n_it = len(items)
                DEF = 5
                chains = blk_state["pending"] if kind == "C" else []

                def run_chain(ch):
                    while ch["fn"] is not None:
                        ch["fn"] = ch["fn"]()

                for step in range(n_it + LA):
                    if step < n_it:
                        items[step][0](ds_[step])
                    for ch in list(chains):
                        if ch["due"] <= step and ch["fn"] is not None:
                            ch["fn"] = ch["fn"]()
                            ch["due"] = step + DEF
                        if ch["fn"] is None and ch in chains:
                            chains.remove(ch)
                    j_ = step - LA
                    if j_ >= 0:
                        items[j_][1](ds_[j_])
                        if items[j_][2] is not None:
                            later = items[j_][2]()
                            if later is not None:
                                chains.append({"fn": later, "due": step + DEF})
                for ch in list(chains):
                    run_chain(ch)
                    if ch in chains:
                        chains.remove(ch)
                if kind == "C":
                    _fence_scope(dab.tiles)

            ps_o = Ring(banks[0:8])
            for tb in (range(NTB) if stage >= 4 else ()):
                cd = cond_of(tb)
                qbufs = [ot_b[tb]] if tb < 4 else [ot_b[4], ot_b[5]]
                for n in range(NCH):
                    po = ps_o.next()
                    bld.mm(po.ap, wo[:, n * 128:(n + 1) * 128], OT_t[:, tbcols(tb)], True, True, reads=[s2w] + qbufs, writes=[po])
                    bld.op(dve, lambda po=po, n=n, tb=tb, cd=cd: V.scalar_tensor_tensor(
                        out=xT[:, n, tbcols(tb)], in0=po.ap, scalar=drv.ap[:, 2, 1, n, cd:cd + 1], in1=xT[:, n, tbcols(tb)],
                        op0=ALU.mult, op1=ALU.add), reads=[po, drv, x_b[tb][n]], writes=[x_b[tb][n]])

            for lst in (qo_b[0], qo_b[1], kt_b, vv_b, ot_b):
                pending_fence.extend(lst)
            last_group["kind"] = kind
            last_group["bufs"] = grp["cur"]
            grp["cur"] = None
            grp["prev"] = None
            _fence_scope([KTc, VVc] + Fr.tiles + Hr.tiles)
            if rope:
                _fence_scope(csr.tiles)
            if kind == "B":
                _fence_scope([TF])

    def mixer(l):
        ei = l // 2
        if l % 2 == 0:
            seq = [("A", l, gi) for gi in range(4)] + [("B", l, gi) for gi in range(4)]
        else:
            seq = [("C", l, gi) for gi in range(8)]
        if groups is not None:
            seq = [g_ for g_ in seq if (g_[0], g_[2]) in groups]
        if seq:
            prefetched[seq[0]] = load_group_weights(*seq[0])
        prenorm(l, 1)
        if l % 2 == 0:
            bld.dma(sp, kw8.ap, kwbc_d[ei], writes=[kw8])
            for i_, g_ in enumerate(seq):
                run_group(g_[0], l, g_[2], nxt=seq[i_ + 1] if i_ + 1 < len(seq) else None)
        else:
            with contextlib.ExitStack() as s2:
                lt = scoped("lt", [128, 256], F32, s2)
                lp = scoped("lp", [128, 128], F32, s2)
                ls = scoped("ls", [128, 2], F32, s2)
                bld.dma(sp, lt.ap, lam_d[ei], writes=[lt])
                ltv = lt.ap.rearrange("p (a d) -> p a d", a=4)
                bld.op(dve, lambda: V.tensor_tensor(out=lp.ap[:, 0:64], in0=ltv[:, 0, :], in1=ltv[:, 1, :], op=ALU.mult), reads=[lt], writes=[lp])
                bld.op(dve, lambda: V.tensor_tensor(out=lp.ap[:, 64:128], in0=ltv[:, 2, :], in1=ltv[:, 3, :], op=ALU.mult), reads=[lt], writes=[lp])
                bld.op(dve, lambda: V.reduce_sum(out=ls.ap, in_=lp.ap.rearrange("p (a d) -> p a d", a=2), axis=mybir.AxisListType.X),
                       reads=[lp], writes=[ls])
                bld.op(act, lambda: A.activation(out=ls.ap, in_=ls.ap, func=AF.Exp), reads=[ls], writes=[ls])
                bld.op(dve, lambda: V.tensor_tensor(out=lamt.ap[:, 0:1], in0=ls.ap[:, 1:2], in1=ls.ap[:, 0:1], op=ALU.subtract),
                       reads=[ls], writes=[lamt])
                bld.op(dve, lambda: V.tensor_scalar_add(out=lamt.ap[:, 0:1], in0=lamt.ap[:, 0:1], scalar1=-LAM_INIT[l]), reads=[lamt], writes=[lamt])
                bld.op(dve, lambda: V.tensor_scalar_mul(out=lamt.ap[:, 1:2], in0=prm.ap[:, 108 + ei:109 + ei], scalar1=1.0 - LAM_INIT[l]),
                       reads=[prm], writes=[lamt])
                _fence_scope([lt, lp, ls])
            for i_, g_ in enumerate(seq):
                run_group(g_[0], l, g_[2], nxt=seq[i_ + 1] if i_ + 1 < len(seq) else None)

    for l in range(NL):
        if l == 0 or not do_ffn:
            mod_begin(l)
            for pi in range(36):
                mod_piece(l, pi)
        mod_finish(l)
        mod_ps = None
        if do_ffn:
            prenorm(l, 0)
            if dbg == 2 and l == 0:
                dbg2_d = dout("dbg2", [128, 8, 128])
                bld.dma(pool, dbg2_d[:, :, 0:64], hT[:, :, 0:64], reads=[h_b[0]])
                bld.dma(pool, dbg2_d[:, :, 64:128], hT[:, :, 2048:2112], reads=[h_b[4]])
            ffn(l, 0, 0)
            if dbg == 2 and l == 0:
                dbg3_d = dout("dbg3", [128, 8, 160])
                for tb_ in range(5):
                    bld.dma(sp, dbg3_d[:, :, tb_ * 32:(tb_ + 1) * 32], xT[:, :, tb_ * 512 + 100:tb_ * 512 + 132], reads=x_b[tb_])
        if do_mixer:
            mixer(l)
        if do_ffn:
            prenorm(l, 2)
            if l + 1 < NL:
                mod_begin(l + 1)
                ffn(l, 1, 2, mod_l=l + 1)
            else:
                ffn(l, 1, 2)

    if dbg:
        dbg_d = dout("dbg", [128, 144 + 144])
        bld.dma(sp, dbg_d[:, 0:144], modT.ap.rearrange("p j c -> p (j c)"), reads=[modT])
        bld.dma(sp, dbg_d[:, 144:288], drv.ap.rearrange("p a s c k -> p (a s c k)"), reads=[drv])
    final()
    bld.finish()
    return nc, bld


_PROG = {}


def _get_prog(**kw):
    key = tuple(sorted(kw.items()))
    if key not in _PROG:
        _PROG[key] = build_program(**kw)
    return _PROG[key]


def make_in_maps(inputs, cores, NL=DEPTH):
    f = lambda a: np.ascontiguousarray(np.asarray(a, dtype=np.float32))
    cossin, ident, cmats, colmask = _const_tables()
    n_even = (NL + 1) // 2
    n_odd = NL // 2
    shared = {
        "b_mod": f(inputs["b_mod"]).reshape(DEPTH, 72, 128),
        "norm_w": f(inputs["norm_w"]).reshape(96, 128),
        "qk_norm": f(np.stack([np.concatenate([inputs["a_q_norm"][e], inputs["a_q_norm"][e]]) if w == 0 else
                               np.concatenate([inputs["a_k_norm"][e], inputs["a_k_norm"][e]])
                               for e in range(2) for w in range(2)], 0)),
        "rpbx": _expand_rpb(np.asarray(inputs["b_rpb"], np.float32)),
        "c_lambda": f(np.broadcast_to(np.asarray(inputs["c_lambda"], np.float32).reshape(2, 1, 256), (2, 128, 256))),
        "kw_bc": f(np.broadcast_to(np.asarray(inputs["a_k_norm"], np.float32).reshape(2, 1, 64), (2, 128, 64))),
        "c_subln": f(inputs["c_subln"]),
        "final_norm": f(inputs["final_norm"]).reshape(8, 128),
        "cossin": cossin, "ident": ident, "cmats": cmats, "colmask": colmask,
        "colmaskx": np.ascontiguousarray(np.tile(colmask, (1, 15))),
    }
    for l in range(NL):
        shared[f"w_mod{l}"] = f(inputs["w_mod"][l])
        for ff in range(2):
            shared[f"w1_{l}_{ff}"] = f(inputs["ffn_w1"][l, ff])
            shared[f"w3_{l}_{ff}"] = f(inputs["ffn_w3"][l, ff])
            shared[f"w2_{l}_{ff}"] = f(inputs["ffn_w2"][l, ff])
    for e in range(n_even):
        shared[f"w_in_ab{e}"] = f(inputs["w_in_ab"][e])
        shared[f"w_out_ab{e}"] = f(inputs["w_out_ab"][e])
    for o in range(n_odd):
        shared[f"w_in_c{o}"] = f(inputs["w_in_c"][o])
        shared[f"w_out_c{o}"] = f(inputs["w_out_c"][o])
    maps = []
    for b in cores:
        m = dict(shared)
        m["xs"] = f(inputs["x_sample"][b])
        m["xp"] = f(inputs["x_prompt"][2 * b:2 * b + 2]).reshape(TC, D)
        m["cak"] = f(inputs["cache_a_k"][b]).reshape(2, 512, 128)
        m["cav"] = f(inputs["cache_a_v"][b]).reshape(2, 512, 128)
        m["cbk"] = f(inputs["cache_b_k"][b]).reshape(2, 512, 512)
        m["cbv"] = f(inputs["cache_b_v"][b]).reshape(2, 512, 512)
        m["cck"] = f(inputs["cache_c_k"][b]).reshape(2, 512, 1024)
        m["ccv"] = f(inputs["cache_c_v"][b]).reshape(2, 512, 1024)
        m["cvec"] = f(np.stack([inputs["c"][b], inputs["c_ctx"]], 0)).reshape(16, 128)
        maps.append(m)
    return maps


def gather_outputs(results, n):
    yp = np.concatenate([r["yp"].reshape(2, 256, D) for r in results], 0)
    ys = np.stack([r["ys"] for r in results], 0)
    nak = np.concatenate([r["nak"].reshape(2, 2, 256, 2, 64) for r in results], 0)
    nav = np.concatenate([r["nav"].reshape(2, 2, 256, 2, 64) for r in results], 0)
    nbk = np.concatenate([r["nbk"].reshape(2, 2, 256, 8, 64) for r in results], 0)
    nbv = np.concatenate([r["nbv"].reshape(2, 2, 256, 8, 64) for r in results], 0)
    nck = np.concatenate([r["nck"].reshape(2, 2, 256, 8, 128) for r in results], 0)
    ncv = np.concatenate([r["ncv"].reshape(2, 2, 256, 8, 128) for r in results], 0)
    return tuple(np.ascontiguousarray(a.astype(np.float32)) for a in (yp, ys, nak, nav, nbk, nbv, nck, ncv))


def kernel(**inputs):
    nc, _ = _get_prog()
    in_maps = make_in_maps(inputs, list(range(N_CORES)))
    res = run_bass_kernel_spmd(nc, in_maps, core_ids=list(range(N_CORES)))
    return gather_outputs(res.results, N_CORES)
```

```python
import math
import os
import contextlib
import numpy as np
import concourse.bass as bass
import concourse.mybir as mybir
from concourse.bass_utils import run_bass_kernel_spmd

F32 = mybir.dt.float32
BF16 = mybir.dt.bfloat16
ALU = mybir.AluOpType
AF = mybir.ActivationFunctionType

D = 1024
NCH = 8
DFF = 2816
NF = 22
DEPTH = 4
TS = 2048
TC = 512
T = TS + TC
NTB = 5
NTT = 20
GRID_W = 64
EPS = 1e-6
SCALE = 0.125
LAM_INIT = [0.8 - 0.6 * math.exp(-0.3 * l) for l in range(DEPTH)]
N_CORES = 8
TRACE_BUF = None
STRICT = not os.environ.get("K_NOSTRICT")


class Eng:
    def __init__(self, name, h, sem):
        self.name = name
        self.h = h
        self.sem = sem
        self.cnt = 0
        self.seen = {}


class DSem:
    def __init__(self, sem):
        self.sem = sem
        self.cnt = 0


class Buf:
    __slots__ = ("w", "r", "name", "excl")

    def __init__(self, name=""):
        self.w = None
        self.r = {}
        self.name = name
        self.excl = False


class Tile:
    __slots__ = ("ap", "buf")

    def __init__(self, ap, name=""):
        self.ap = ap
        self.buf = Buf(name)


class Ring:
    def __init__(self, tiles):
        self.tiles = tiles
        self.i = 0

    def next(self):
        t = self.tiles[self.i % len(self.tiles)]
        self.i += 1
        return t


class B:
    def __init__(self, nc, es):
        self.nc = nc
        self.es = es
        mk = lambda n: es.enter_context(nc.semaphore(n))
        self.pe = Eng("pe", nc.tensor, mk("s_pe"))
        self.act = Eng("act", nc.scalar, mk("s_act"))
        self.dve = Eng("dve", nc.vector, mk("s_dve"))
        self.pool = Eng("pool", nc.gpsimd, mk("s_pool"))
        self.sp = Eng("sp", nc.sync, mk("s_sp"))
        self.engs = [self.pe, self.act, self.dve, self.pool, self.sp]
        self.dsems = {}
        for q in (self.sp, self.pool):
            self.dsems[q.name] = [DSem(mk(f"d_{q.name}{i}")) for i in range(8)]
        self.dma_i = {"sp": 0, "pool": 0}
        self.fence = {}
        self.n_ins = 0

    def _need(self, eng, reads, writes):
        need = {}

        def add(tag, raw=False):
            if tag is None:
                return
            o, c = tag
            if o is eng and (eng.name == "pe" or not (raw or STRICT)):
                return
            if need.get(o, 0) < c:
                need[o] = c

        for b in reads:
            add(b.w, raw=True)
            if b.excl:
                for o, c in b.r.items():
                    add((o, c))
        for b in writes:
            add(b.w)
            for o, c in b.r.items():
                add((o, c))
        return need

    def _emit_waits(self, eng, need):
        for o, c in need.items():
            if eng.seen.get(o, 0) < c:
                eng.h.wait_ge(o.sem, c)
                eng.seen[o] = c
                self.n_ins += 1

    def op(self, eng, fn, reads=(), writes=(), sig=True):
        reads = [t.buf if isinstance(t, Tile) else t for t in reads]
        writes = [t.buf if isinstance(t, Tile) else t for t in writes]
        need = self._need(eng, reads, writes)
        if TRACE_BUF and any(b.name == TRACE_BUF for b in list(reads) + list(writes)):
            print("TRACE", eng.name, "cnt", eng.cnt, "sig", sig, "reads", [b.name for b in reads], "writes", [b.name for b in writes],
                  "need", {getattr(o, 'name', 'dsem'): c for o, c in need.items()}, "seen", {getattr(o, 'name', 'dsem'): c for o, c in eng.seen.items()})
        self._emit_waits(eng, need)
        ins = fn()
        self.n_ins += 1
        if sig:
            ins.then_inc(eng.sem, 1)
            eng.cnt += 1
            tag = eng.cnt
        else:
            tag = eng.cnt + 1
        for b in reads:
            if b.r.get(eng, 0) < tag:
                b.r[eng] = tag
        for b in writes:
            b.w = (eng, tag)
            b.r = {}
        return ins

    def dma(self, q, out, in_, reads=(), writes=()):
        reads = [t.buf if isinstance(t, Tile) else t for t in reads]
        writes = [t.buf if isinstance(t, Tile) else t for t in writes]
        lst = self.dsems[q.name]
        ds = lst[self.dma_i[q.name] % len(lst)]
        self.dma_i[q.name] += 1
        need = self._need(q, reads, writes)
        if ds.cnt > 0:
            need[ds] = max(need.get(ds, 0), ds.cnt)
        self._emit_waits(q, need)
        q.h.dma_start(out=out, in_=in_).then_inc(ds.sem, 16)
        self.n_ins += 1
        ds.cnt += 16
        for b in reads:
            b.r[ds] = ds.cnt
        for b in writes:
            b.w = (ds, ds.cnt)
            b.r = {}

    def mm(self, out, lhsT, rhs, start, stop, reads=(), writes=(), sig=True, **kw):
        return self.op(self.pe, lambda: self.nc.tensor.matmul(out, lhsT=lhsT, rhs=rhs, start=start, stop=stop, **kw),
                       reads=reads, writes=writes, sig=sig)

    def mm_group(self, out_t, out_ap, pairs, reads):
        n = len(pairs)
        for i, (l, r) in enumerate(pairs):
            self.mm(out_ap, l, r, start=(i == 0), stop=(i == n - 1),
                    reads=reads if i == 0 else (), writes=[out_t] if i == 0 else (), sig=(i == n - 1))

    def finish(self):
        for q in (self.sp, self.pool):
            for ds in self.dsems[q.name]:
                if ds.cnt > 0 and q.seen.get(ds, 0) < ds.cnt:
                    q.h.wait_ge(ds.sem, ds.cnt)
        for e in self.engs:
            if e is not self.sp and e.cnt > 0:
                self.sp.h.wait_ge(e.sem, e.cnt)
        for ds in self.dsems["pool"]:
            if ds.cnt > 0:
                self.sp.h.wait_ge(ds.sem, ds.cnt)


def _const_tables():
    t = np.arange(TS)
    row = (t // GRID_W).astype(np.float32)
    col = (t % GRID_W).astype(np.float32)
    n_freq = 16
    inv_freq = (np.float32(10000.0) ** (-np.arange(n_freq, dtype=np.float32) / n_freq)).astype(np.float32)
    ang_r = row[:, None] * inv_freq
    ang_c = col[:, None] * inv_freq
    ang = np.concatenate([ang_r, ang_r, ang_c, ang_c], axis=-1)
    cos = np.cos(ang).astype(np.float32).T
    sin = np.sin(ang).astype(np.float32).T
    cossin = np.stack([np.concatenate([cos, cos], 0), np.concatenate([sin, sin], 0)], 0)
    ident = np.eye(128, dtype=np.float32)
    ones = np.ones((128, 128), np.float32)
    bd = np.zeros((128, 128), np.float32)
    bd[:64, :64] = 1.0
    bd[64:, 64:] = 1.0
    R = np.zeros((128, 128), np.float32)
    for base in (0, 64):
        for i in range(16):
            R[base + 16 + i, base + i] = -1.0
            R[base + i, base + 16 + i] = 1.0
            R[base + 48 + i, base + 32 + i] = -1.0
            R[base + 32 + i, base + 48 + i] = 1.0
    cmats = np.stack([ones, bd, R], 0)
    qc = np.arange(64)
    kc = np.arange(64)
    col_start = np.clip(qc - 8, 0, 64 - 16)
    cm = ((kc[:, None] >= col_start[None, :]) & (kc[:, None] < col_start[None, :] + 16)).astype(np.float32)
    colmask = np.concatenate([cm, cm], 0)
    return cossin, ident, cmats, colmask


def _expand_rpb(b_rpb):
    kc = np.arange(64)[:, None]
    qc = np.arange(64)[None, :]
    dc = np.clip(kc - qc + 15, 0, 30)
    dr = 14 - np.arange(15)
    g = b_rpb[:, :, dr][:, :, :, dc]
    g = np.transpose(g, (0, 1, 3, 2, 4))
    g = np.concatenate([g, g], axis=2)
    return np.ascontiguousarray(g.reshape(2, 8, 128, 15 * 64)).astype(np.float32)


def build_program(NL=DEPTH, do_mixer=True, do_ffn=True, dbg=False, groups=None, stage=9):
    nc = bass.Bass("TRN2", target_bir_lowering=False)
    es = contextlib.ExitStack()
    bld = B(nc, es)
    pe, act, dve, pool, sp = bld.pe, bld.act, bld.dve, bld.pool, bld.sp
    V = nc.vector
    A = nc.scalar
    G = nc.gpsimd

    def din(name, shape):
        return nc.dram_tensor(name, list(shape), F32, kind="ExternalInput").ap()

    def dout(name, shape):
        return nc.dram_tensor(name, list(shape), F32, kind="ExternalOutput").ap()

    xs_d = din("xs", [TS, D])
    xp_d = din("xp", [TC, D])
    cak_d = din("cak", [2, 512, 128])
    cav_d = din("cav", [2, 512, 128])
    cbk_d = din("cbk", [2, 512, 512])
    cbv_d = din("cbv", [2, 512, 512])
    cck_d = din("cck", [2, 512, 1024])
    ccv_d = din("ccv", [2, 512, 1024])
    cvec_d = din("cvec", [16, 128])
    wmod_d = [din(f"w_mod{l}", [D, 9 * D]) for l in range(NL)]
    bmod_d = din("b_mod", [DEPTH, 72, 128])
    normw_d = din("norm_w", [96, 128])
    w1_d = [[din(f"w1_{l}_{f}", [D, DFF]) for f in range(2)] for l in range(NL)]
    w3_d = [[din(f"w3_{l}_{f}", [D, DFF]) for f in range(2)] for l in range(NL)]
    w2_d = [[din(f"w2_{l}_{f}", [DFF, D]) for f in range(2)] for l in range(NL)]
    n_even = (NL + 1) // 2
    n_odd = NL // 2
    winab_d = [din(f"w_in_ab{e}", [D, 2304]) for e in range(n_even)]
    woutab_d = [din(f"w_out_ab{e}", [D, D]) for e in range(n_even)]
    qkn_d = din("qk_norm", [4, 128])
    rpbx_d = din("rpbx", [2, 8, 128, 960])
    winc_d = [din(f"w_in_c{o}", [D, 3072]) for o in range(n_odd)]
    woutc_d = [din(f"w_out_c{o}", [D, D]) for o in range(n_odd)]
    lam_d = din("c_lambda", [2, 128, 256])
    kwbc_d = din("kw_bc", [2, 128, 64])
    subln_d = din("c_subln", [2, 128])
    fnorm_d = din("final_norm", [8, 128])
    cossin_d = din("cossin", [2, 128, TS])
    ident_d = din("ident", [128, 128])
    cmats_d = din("cmats", [3, 128, 128])
    colmask_d = din("colmask", [128, 64])
    colmaskx_d = din("colmaskx", [128, 960])

    yp_d = dout("yp", [TC, D])
    ys_d = dout("ys", [TS, D])
    nak_d = dout("nak", [2, 2, 256, 128])
    nav_d = dout("nav", [2, 2, 256, 128])
    nbk_d = dout("nbk", [2, 2, 256, 512])
    nbv_d = dout("nbv", [2, 2, 256, 512])
    nck_d = dout("nck", [2, 2, 256, 1024])
    ncv_d = dout("ncv", [2, 2, 256, 1024])

    uid = [0]

    def sb(name, shape, dt, stack=None):
        uid[0] += 1
        return (stack or es).enter_context(nc.sbuf_tensor(f"sb_{name}_{uid[0]}", list(shape), dt))

    xT = sb("xT", [128, NCH, T], F32)
    hT = sb("hT", [128, NCH, T], BF16)
    x_b = [[Buf(f"x{tb}_{c}") for c in range(NCH)] for tb in range(NTB)]
    h_b = [Buf(f"h{tb}") for tb in range(NTB)]
    ident = Tile(sb("ident", [128, 128], F32)[:], "ident")
    cm = Tile(sb("cm", [128, 3, 128], BF16)[:], "cm")
    ones_bf = cm.ap[:, 0, :]
    bd_bf = cm.ap[:, 1, :]
    rot_bf = cm.ap[:, 2, :]
    prm = Tile(sb("prm", [128, 128], F32)[:], "prm")
    modT = Tile(sb("modT", [128, 72, 2], F32)[:], "modT")
    bmodT = Tile(sb("bmodT", [128, 72], F32)[:], "bmodT")
    drv = Tile(sb("drv", [128, 3, 3, NCH, 2], F32)[:], "drv")
    scT = Tile(sb("scT", [128, 16], BF16)[:], "scT")
    lamt = Tile(sb("lamt", [128, 8], F32)[:], "lamt")
    kw8 = Tile(sb("kw8", [128, 64], F32)[:], "kw8")
    colmask = Tile(sb("colmask", [128, 64], F32)[:], "colmask")
    epsc = Tile(sb("epsc", [128, 1], F32)[:], "epsc")
    NSLOT = 6
    wring_t = sb("wring", [128, NSLOT, 2048], BF16)
    wring = Ring([Tile(wring_t[:, i, :], f"w{i}") for i in range(NSLOT)])

    banks = [Tile(es.enter_context(nc.psum_tensor(f"ps{i}", [128, 512], F32))[:], f"ps{i}") for i in range(8)]
    for bk in banks:
        bk.buf.excl = True
    ps_all = Ring(banks)

    def tbcols(tb):
        return slice(tb * 512, (tb + 1) * 512)

    def cond_of(tb):
        return 1 if tb == 4 else 0

    with contextlib.ExitStack() as ss:
        stg = Ring([Tile(sb(f"stg{i}", [128, D], F32, ss)[:], f"stg{i}") for i in range(3)])
        bld.dma(sp, ident.ap, ident_d, writes=[ident])
        bld.dma(pool, cm.ap, cmats_d.rearrange("m p n -> p m n"), writes=[cm])
        bld.dma(sp, colmask.ap, colmask_d, writes=[colmask])
        bld.op(dve, lambda: V.memset(epsc.ap, EPS), writes=[epsc])

        s0 = stg.next()
        bld.dma(sp, s0.ap[0:96, 0:128], normw_d, writes=[s0])
        bld.dma(sp, s0.ap[96:104, 0:128], fnorm_d, writes=[s0])
        bld.dma(sp, s0.ap[104:108, 0:128], qkn_d, writes=[s0])
        bld.dma(sp, s0.ap[108:110, 0:128], subln_d, writes=[s0])
        bld.dma(sp, s0.ap[110:126, 0:128], cvec_d, writes=[s0])
        p0 = ps_all.next()
        bld.op(pe, lambda: nc.tensor.transpose(out=p0.ap[:, 0:126], in_=s0.ap[0:126, 0:128], identity=ident.ap[0:126, 0:126]),
               reads=[s0, ident], writes=[p0])
        bld.op(dve, lambda: V.tensor_copy(out=prm.ap[:, 0:110], in_=p0.ap[:, 0:110]), reads=[p0], writes=[prm])
        bld.op(act, lambda: A.activation(out=scT.ap, in_=p0.ap[:, 110:126], func=AF.Silu), reads=[p0], writes=[scT])

        for i in range(NTT):
            st = stg.next()
            src = xs_d[i * 128:(i + 1) * 128, :] if i < 16 else xp_d[(i - 16) * 128:(i - 15) * 128, :]
            bld.dma(sp, st.ap, src, writes=[st])
            tb = i // 4
            for half in range(2):
                ps = ps_all.next()
                for c4 in range(4):
                    c = half * 4 + c4
                    bld.op(pe, lambda ps=ps, c4=c4, c=c, st=st: nc.tensor.transpose(
                        out=ps.ap[:, c4 * 128:(c4 + 1) * 128], in_=st.ap[:, c * 128:(c + 1) * 128], identity=ident.ap),
                        reads=[st, ident] if c4 == 0 else (), writes=[ps] if c4 == 0 else (), sig=(c4 == 3))
                dst = xT[:, half * 4:half * 4 + 4, i * 128:(i + 1) * 128]
                srcp = ps.ap.rearrange("p (c n) -> p c n", c=4)
                if half == 0:
                    bld.op(dve, lambda dst=dst, srcp=srcp: V.tensor_copy(out=dst, in_=srcp), reads=[ps], writes=x_b[tb][half * 4:half * 4 + 4])
                else:
                    bld.op(act, lambda dst=dst, srcp=srcp: A.copy(out=dst, in_=srcp), reads=[ps], writes=x_b[tb][half * 4:half * 4 + 4])
        setup_fence = [t.buf for t in stg.tiles]

    def wload(dst_ap, src_ap, slot):
        bld.dma(pool, dst_ap, src_ap, writes=[slot])

    def mod_piece(l, pi):
        slot = wring.next()
        w = slot.ap.rearrange("p (k n) -> p k n", k=8)
        wload(w, wmod_d[l][:, pi * 256:(pi + 1) * 256].rearrange("(k p) n -> p k n", p=128), slot)
        for j2 in range(2):
            j = pi * 2 + j2
            out_ap = mod_ps.ap[:, j * 2:(j + 1) * 2]
            for k in range(8):
                rhs = scT.ap.rearrange("p (c k) -> p k c", c=2)[:, k, :]
                bld.mm(out_ap, w[:, k, j2 * 128:(j2 + 1) * 128], rhs, start=(k == 0), stop=(k == 7),
                       reads=[slot, scT] if k == 0 else (), writes=[mod_ps] if (k == 0) else (),
                       sig=(k == 7))

    def mod_begin(l):
        nonlocal mod_ps
        mod_ps = banks[7]
        with contextlib.ExitStack() as s2:
            bst = scoped("bst", [72, 128], F32, s2)
            bld.dma(sp, bst.ap, bmod_d[l], writes=[bst])
            pb = banks[6]
            bld.op(pe, lambda: nc.tensor.transpose(out=pb.ap[:, 0:72], in_=bst.ap, identity=ident.ap[0:72, 0:72]),
                   reads=[bst, ident], writes=[pb])
            bld.op(dve, lambda: V.tensor_copy(out=bmodT.ap, in_=pb.ap[:, 0:72]), reads=[pb], writes=[bmodT])
            _fence_scope([bst])

    def mod_finish(l):
        mp = mod_ps.ap[:, 0:144].rearrange("p (j c) -> p j c", c=2)
        for cd in range(2):
            bld.op(dve, lambda cd=cd: V.tensor_tensor(out=modT.ap[:, :, cd], in0=mp[:, :, cd], in1=bmodT.ap, op=ALU.add),
                   reads=[mod_ps, bmodT], writes=[modT])
        m = modT.ap.rearrange("p (i c) k -> p i c k", c=8)
        for s in range(3):
            nw = prm.ap[:, (l * 3 + s) * 8:(l * 3 + s) * 8 + 8]
            gsc = 1.0 if s == 1 else 0.5
            for cd in range(2):
                bld.op(dve, lambda s=s, cd=cd, nw=nw: V.scalar_tensor_tensor(
                    out=drv.ap[:, 0, s, :, cd], in0=m[:, 3 * s + 1, :, cd], scalar=1.0, in1=nw, op0=ALU.add, op1=ALU.mult),
                    reads=[modT, prm], writes=[drv])
            bld.op(dve, lambda s=s: V.tensor_copy(out=drv.ap[:, 1, s], in_=m[:, 3 * s]), reads=[modT], writes=[drv])
            bld.op(dve, lambda s=s, gsc=gsc: V.tensor_scalar_mul(out=drv.ap[:, 2, s], in0=m[:, 3 * s + 2], scalar1=gsc),
                   reads=[modT], writes=[drv])

    mod_ps = None

    def rstd_op(ps, r, n, eps, cols=slice(0, 512), pr=slice(0, 128)):
        bld.op(act, lambda: A.activation(out=r.ap[pr, cols], in_=ps.ap[pr, cols], func=AF.Ln, bias=epsc.ap[pr, 0:1] if eps == EPS else None, scale=1.0 / n),
               reads=[ps, epsc], writes=[r])
        bld.op(act, lambda: A.activation(out=r.ap[pr, cols], in_=r.ap[pr, cols], func=AF.Exp, scale=-0.5), reads=[r], writes=[r])

    def prenorm(l, s):
        last_group["kind"] = None
        with contextlib.ExitStack() as s2:
            sq = Ring([scoped(f"sq{i}", [128, NCH, 512], BF16, s2) for i in range(1)])
            rs = Ring([scoped(f"rs{i}", [128, 512], F32, s2) for i in range(2)])
            tm = Ring([scoped(f"tm{i}", [128, 512], F32, s2) for i in range(3)])
            for tb in range(NTB):
                cd = cond_of(tb)
                q = sq.next()
                bld.op(act, lambda q=q, tb=tb: A.activation(out=q.ap, in_=xT[:, :, tbcols(tb)], func=AF.Square),
                       reads=x_b[tb], writes=[q])
                ps = ps_all.next()
                bld.mm_group(ps, ps.ap, [(ones_bf, q.ap[:, c, :]) for c in range(NCH)], reads=[q, cm])
                r = rs.next()
                rstd_op(ps, r, D, EPS)
                for c in range(NCH):
                    t = tm.next()
                    bld.op(dve, lambda t=t, c=c, tb=tb, r=r, cd=cd: V.scalar_tensor_tensor(
                        out=t.ap, in0=xT[:, c, tbcols(tb)], scalar=drv.ap[:, 0, s, c, cd:cd + 1], in1=r.ap,
                        op0=ALU.mult, op1=ALU.mult), reads=[x_b[tb][c], r, drv], writes=[t])
                    bld.op(act, lambda t=t, c=c, tb=tb, cd=cd: A.activation(
                        out=hT[:, c, tbcols(tb)], in_=t.ap, func=AF.Identity, bias=drv.ap[:, 1, s, c, cd:cd + 1], scale=1.0),
                        reads=[t, drv], writes=[h_b[tb]])
            _fence_scope([q for q in sq.tiles] + rs.tiles + tm.tiles)

    pending_fence = list(setup_fence)

    def _fence_scope(tiles):
        for t in tiles:
            pending_fence.append(t.buf)

    grp = {"prev": None, "cur": None}

    def _fence_for(name):
        prev = grp["prev"]
        if prev is not None and name in prev:
            return [prev[name]]
        return pending_fence

    def scoped(name, shape, dt, stack):
        t = Tile(sb(name, shape, dt, stack)[:], name)
        if grp["cur"] is not None:
            grp["cur"][name] = t.buf
        rr = {}
        for b in _fence_for(name):
            if b.w is not None:
                o, c = b.w
                rr[o] = max(rr.get(o, 0), c)
            for o, c in b.r.items():
                rr[o] = max(rr.get(o, 0), c)
        t.buf.r = rr
        return t

    def ffn(l, f, s, between=None):
        w1, w3, w2 = w1_d[l][f], w3_d[l][f], w2_d[l][f]
        with contextlib.ExitStack() as s2:
            g_t = sb("g_t", [128, 4, T], BF16, s2)
            g_tiles = [[scoped_buf(f"g{j}_{tb}") for tb in range(NTB)] for j in range(4)]
            su = Ring([scoped(f"su{i}", [128, 512], BF16, s2) for i in range(3)])
            ps_uv = Ring(banks[0:4])
            ps_o = Ring(banks[4:7] if mod_ps is not None else banks[4:8])
            sgs = [(i * 4, 4) for i in range(5)] + [(20, 2)]
            for sgi, (c0, ncn) in enumerate(sgs):
                halves = [(c0 + 2 * hh, min(2, ncn - 2 * hh)) for hh in range((ncn + 1) // 2)]
                slots = []
                for (cc, n2) in halves:
                    sa = wring.next()
                    wa = sa.ap.rearrange("p (k n) -> p k n", k=8)
                    wload(wa[:, :, 0:n2 * 128], w1[:, cc * 128:(cc + n2) * 128].rearrange("(k p) n -> p k n", p=128), sa)
                    sb_ = wring.next()
                    wb = sb_.ap.rearrange("p (k n) -> p k n", k=8)
                    wload(wb[:, :, 0:n2 * 128], w3[:, cc * 128:(cc + n2) * 128].rearrange("(k p) n -> p k n", p=128), sb_)
                    slots.append((sa, wa, sb_, wb))
                slots2 = []
                for (cc, n2) in halves:
                    sc = wring.next()
                    wc = sc.ap.rearrange("p (j n) -> p j n", j=2)
                    wload(wc[:, 0:n2, :], w2[cc * 128:(cc + n2) * 128, :].rearrange("(j p) n -> p j n", p=128), sc)
                    slots2.append((sc, wc))
                for j in range(ncn):
                    sa, wa, sb_, wb = slots[j // 2]
                    jj = j % 2
                    for tb in range(NTB):
                        pu = ps_uv.next()
                        bld.mm_group(pu, pu.ap, [(wa[:, k, jj * 128:(jj + 1) * 128], hT[:, k, tbcols(tb)]) for k in range(8)],
                                     reads=[sa, h_b[tb]])
                        pv = ps_uv.next()
                        bld.mm_group(pv, pv.ap, [(wb[:, k, jj * 128:(jj + 1) * 128], hT[:, k, tbcols(tb)]) for k in range(8)],
                                     reads=[sb_, h_b[tb]])
                        sut = su.next()
                        bld.op(act, lambda sut=sut, pu=pu: A.activation(out=sut.ap, in_=pu.ap, func=AF.Silu),
                               reads=[pu], writes=[sut])
                        bld.op(dve, lambda j=j, tb=tb, pv=pv, sut=sut: V.tensor_tensor(
                            out=g_t[:, j, tbcols(tb)], in0=pv.ap, in1=sut.ap, op=ALU.mult),
                            reads=[pv, sut], writes=[g_tiles[j][tb]])
                for tb in range(NTB):
                    cd = cond_of(tb)
                    for n in range(NCH):
                        po = ps_o.next()
                        pairs = []
                        for j in range(ncn):
                            sc, wc = slots2[j // 2]
                            pairs.append((wc[:, j % 2, n * 128:(n + 1) * 128], g_t[:, j, tbcols(tb)]))
                        bld.mm_group(po, po.ap, pairs, reads=[sl[0] for sl in slots2] + [g_tiles[j][tb] for j in range(ncn)])
                        bld.op(dve, lambda po=po, n=n, tb=tb, cd=cd: V.scalar_tensor_tensor(
                            out=xT[:, n, tbcols(tb)], in0=po.ap, scalar=drv.ap[:, 2, s, n, cd:cd + 1], in1=xT[:, n, tbcols(tb)],
                            op0=ALU.mult, op1=ALU.add), reads=[po, drv, x_b[tb][n]], writes=[x_b[tb][n]])
                if between is not None:
                    between(sgi)
            if dbg == 2 and l == 0 and f == 0:
                dbg4_d = dout("dbg4", [128, 4, 128])
                bld.dma(pool, dbg4_d[:, :, 0:64], g_t[:, :, 0:64], reads=[g_tiles[j][0] for j in range(4)])
                bld.dma(pool, dbg4_d[:, :, 64:128], g_t[:, :, 2048:2112], reads=[g_tiles[j][4] for j in range(4)])
            for j in range(4):
                for tb in range(NTB):
                    pending_fence.append(g_tiles[j][tb])
            _fence_scope(su.tiles)

    def scoped_buf(name):
        b = Buf(name)
        if grp["cur"] is not None:
            grp["cur"][name] = b
        rr = {}
        for pb in _fence_for(name):
            if pb.w is not None:
                o, c = pb.w
                rr[o] = max(rr.get(o, 0), c)
            for o, c in pb.r.items():
                rr[o] = max(rr.get(o, 0), c)
        b.r = rr
        return b

    def final():
        with contextlib.ExitStack() as s2:
            sq = scoped("fsq", [128, NCH, 512], BF16, s2)
            rs = scoped("frs", [128, 512], F32, s2)
            yt = Ring([scoped(f"fy{i}", [128, 512], F32, s2) for i in range(3)])
            ot_t = sb("fot", [128, 2, 4, D], F32, s2)
            ot = Ring([scoped_view(ot_t[:, i], f"fot{i}") for i in range(2)])
            for tb in range(NTB):
                bld.op(act, lambda tb=tb: A.activation(out=sq.ap, in_=xT[:, :, tbcols(tb)], func=AF.Square),
                       reads=x_b[tb], writes=[sq])
                ps = ps_all.next()
                bld.mm_group(ps, ps.ap, [(ones_bf, sq.ap[:, c, :]) for c in range(NCH)], reads=[sq, cm])
                rstd_op(ps, rs, D, EPS)
                o = ot.next()
                for c in range(NCH):
                    y = yt.next()
                    bld.op(dve, lambda y=y, c=c, tb=tb: V.scalar_tensor_tensor(
                        out=y.ap, in0=xT[:, c, tbcols(tb)], scalar=prm.ap[:, 96 + c:97 + c], in1=rs.ap,
                        op0=ALU.mult, op1=ALU.mult), reads=[x_b[tb][c], rs, prm], writes=[y])
                    ps2 = ps_all.next()
                    for tt in range(4):
                        bld.op(pe, lambda ps2=ps2, y=y, tt=tt: nc.tensor.transpose(
                            out=ps2.ap[:, tt * 128:(tt + 1) * 128], in_=y.ap[:, tt * 128:(tt + 1) * 128], identity=ident.ap),
                            reads=[y, ident] if tt == 0 else (), writes=[ps2] if tt == 0 else (), sig=(tt == 3))
                    dst = o.ap[:, :, c * 128:(c + 1) * 128]
                    srcp = ps2.ap.rearrange("p (t n) -> p t n", t=4)
                    if c % 2 == 0:
                        bld.op(act, lambda dst=dst, srcp=srcp: A.copy(out=dst, in_=srcp), reads=[ps2], writes=[o])
                    else:
                        bld.op(dve, lambda dst=dst, srcp=srcp: V.tensor_copy(out=dst, in_=srcp), reads=[ps2], writes=[o])
                for tt in range(4):
                    i = tb * 4 + tt
                    dst = ys_d[i * 128:(i + 1) * 128, :] if i < 16 else yp_d[(i - 16) * 128:(i - 15) * 128, :]
                    bld.dma(sp, dst, o.ap[:, tt, :], reads=[o])

    def scoped_view(ap, name):
        t = Tile(ap, name)
        t.buf = scoped_buf(name)
        return t


    def pipeline(n, stages):
        ns = len(stages)
        for step in range(n + ns - 1):
            for st in range(ns):
                i = step - st
                if 0 <= i < n:
                    stages[st](i)

    last_group = {"kind": None, "bufs": None}
    prefetched = {}
    mixst = {"OT": None, "ot_b": None, "units": [], "slot": 0}

    def nb_ranges(j, c):
        out = []
        for krl in range(2):
            kr = 2 * c + krl
            valid = []
            for qr in range(8 * j, 8 * j + 8):
                rs_ = min(max(qr - 4, 0), 32 - 8)
                if rs_ <= kr < rs_ + 8:
                    valid.append(qr)
            if not valid:
                out.append(None)
                continue
            qa, qb = valid[0], valid[-1]
            assert valid == list(range(qa, qb + 1))
            i0 = 7 - kr + qa
            assert 0 <= i0 and i0 + (qb - qa) <= 14
            out.append(((qa - 8 * j) * 64, (qb - 8 * j + 1) * 64, i0))
        return out

    def group_cols(kind, gi):
        if kind == "A":
            kv = gi // 2
            return (slice(128 * gi, 128 * gi + 128), slice(512 + 64 * kv, 512 + 64 * kv + 64),
                    slice(640 + 64 * kv, 640 + 64 * kv + 64), 128 * gi, 64)
        if kind == "B":
            return (slice(768 + 128 * gi, 768 + 128 * gi + 128), slice(1280 + 128 * gi, 1280 + 128 * gi + 128),
                    slice(1792 + 128 * gi, 1792 + 128 * gi + 128), 512 + 128 * gi, 128)
        return (slice(128 * gi, 128 * gi + 128), slice(1024 + 128 * gi, 1024 + 128 * gi + 128),
                slice(2048 + 128 * gi, 2048 + 128 * gi + 128), 128 * gi, 128)

    def load_group_weights(kind, l, gi):
        ei = l // 2
        win = winab_d[ei] if kind != "C" else winc_d[ei]
        wout = woutab_d[ei] if kind != "C" else woutc_d[ei]
        qcols, kcols, vcols, orow, nv = group_cols(kind, gi)
        s1 = wring.next()
        w1v = s1.ap.rearrange("p (k n) -> p k n", k=8)
        wload(w1v[:, :, 0:128], win[:, qcols].rearrange("(k p) n -> p k n", p=128), s1)
        if kind == "A":
            wload(w1v[:, :, 128:192], win[:, kcols].rearrange("(k p) n -> p k n", p=128), s1)
            wload(w1v[:, :, 192:256], win[:, kcols].rearrange("(k p) n -> p k n", p=128), s1)
        else:
            wload(w1v[:, :, 128:256], win[:, kcols].rearrange("(k p) n -> p k n", p=128), s1)
        s2w = wring.next()
        wo = s2w.ap[:, 0:1024]
        wvv = s2w.ap[:, 1024:2048].rearrange("p (k n) -> p k n", k=8)
        wload(wo, wout[orow:orow + 128, :], s2w)
        wload(wvv[:, :, 0:nv], win[:, vcols].rearrange("(k p) n -> p k n", p=128), s2w)
        return (s1, w1v, s2w, wo, wvv)

    def run_group(kind, l, gi, nxt=None):
        even = (kind != "C")
        ei = l // 2
        cd_s = 1
        win = winab_d[ei] if even else winc_d[ei]
        wout = woutab_d[ei] if even else woutc_d[ei]
        if kind == "A":
            kv = gi // 2
            qcols = slice(128 * gi, 128 * gi + 128)
            kcols = slice(512 + 64 * kv, 512 + 64 * kv + 64)
            vcols = slice(640 + 64 * kv, 640 + 64 * kv + 64)
            orow = 128 * gi
            nv = 64
        elif kind == "B":
            qcols = slice(768 + 128 * gi, 768 + 128 * gi + 128)
            kcols = slice(1280 + 128 * gi, 1280 + 128 * gi + 128)
            vcols = slice(1792 + 128 * gi, 1792 + 128 * gi + 128)
            orow = 512 + 128 * gi
            nv = 128
        else:
            qcols = slice(128 * gi, 128 * gi + 128)
            kcols = slice(1024 + 128 * gi, 1024 + 128 * gi + 128)
            vcols = slice(2048 + 128 * gi, 2048 + 128 * gi + 128)
            orow = 128 * gi
            nv = 128
        rope = kind in ("A", "C")
        grp["prev"] = last_group["bufs"] if (last_group["kind"] == kind and not os.environ.get("K_NOPF")) else None
        grp["cur"] = {}
        with contextlib.ExitStack() as s2:
            if kind == "B":
                TF = scoped("TF", [128, 2, 960], BF16, s2)
                with contextlib.ExitStack() as s3:
                    tfst = scoped(f"tfst_{l}_{gi}", [128, 960], F32, s3)
                    cmx = scoped(f"cmx_{l}_{gi}", [128, 960], F32, s3)
                    bld.dma(sp, cmx.ap, colmaskx_d, writes=[cmx])
                    for hh in range(2):
                        bld.dma(sp, tfst.ap, rpbx_d[ei, 2 * gi + hh], writes=[tfst])
                        bld.op(act, lambda: A.activation(out=tfst.ap, in_=tfst.ap, func=AF.Exp), reads=[tfst], writes=[tfst])
                        bld.op(dve, lambda hh=hh: V.tensor_tensor(out=TF.ap[:, hh, :], in0=tfst.ap, in1=cmx.ap, op=ALU.mult),
                               reads=[tfst, cmx], writes=[TF])
                    _fence_scope([tfst, cmx])

            Qz_t = sb("Qz", [128, 2, T], BF16, s2)
            OT_t = mixst["OT"][mixst["slot"]]
            KT_t = sb("KT", [128, T], BF16, s2)
            ot_b = mixst["ot_b"][mixst["slot"]]
            qo_b = [[scoped_buf(f"qo{hh}_{q}") for q in range(6)] for hh in range(2)]
            kt_b = [scoped_buf(f"kt{tb}") for tb in range(NTB)]
            KTc = scoped("KTc", [128, 512], BF16, s2)
            bld.op(dve, lambda: V.memset(Qz_t[64:128, 0, :], 0.0), writes=qo_b[0])
            bld.op(dve, lambda: V.memset(Qz_t[0:64, 1, :], 0.0), writes=qo_b[1])
            nh = 2 if kind == "B" else 1
            VV_t = sb("VV", [128, NTT, nh, 128], BF16, s2)
            vv_b = [scoped_buf(f"vv{tt}") for tt in range(NTT)]
            VVc = scoped("VVc", [128, 4, nh, 128], BF16, s2)
            Fr = Ring([scoped(f"F{i}", [128, 512], F32, s2) for i in range(5 if kind == "B" else 6)])
            Hr = Ring([scoped(f"H{i}", [128, 512], BF16, s2) for i in range(5 if kind == "C" else 6)])
            if rope:
                csr = Ring([scoped(f"cs{i}", [128, 2, 512], F32, s2) for i in range(2)])
                cs_tb = {}
                for tb_ in range(4):
                    cst = csr.tiles[tb_ % 2]
                    cs_tb[tb_] = cst
            wts = prefetched.pop((kind, l, gi), None) or load_group_weights(kind, l, gi)
            s1, w1v, s2w, wo, wvv = wts
            kst = Fr.next()
            kstv = kst.ap.rearrange("p (t c) -> p t c", t=4)
            if kind == "A":
                src = cak_d[ei][:, 64 * kv:64 * kv + 64].rearrange("(t p) c -> p t c", p=128)
                bld.dma(sp, kstv[:, :, 0:64], src, writes=[kst])
                bld.dma(sp, kstv[:, :, 64:128], src, writes=[kst])
                vsrc = cav_d[ei][:, 64 * kv:64 * kv + 64].rearrange("(t p) c -> p t c", p=128)
                bld.dma(pool, VVc.ap[:, :, 0, 0:64], vsrc, writes=[VVc])
            elif kind == "B":
                bld.dma(sp, kstv, cbk_d[ei][:, 128 * gi:128 * gi + 128].rearrange("(t p) c -> p t c", p=128), writes=[kst])
                for h_ in range(2):
                    vsrc = cbv_d[ei][:, 128 * gi + 64 * h_:128 * gi + 64 * h_ + 64].rearrange("(t p) c -> p t c", p=128)
                    bld.dma(pool, VVc.ap[:, :, h_, 0:64], vsrc, writes=[VVc])
            else:
                bld.dma(sp, kstv, cck_d[ei][:, 128 * gi:128 * gi + 128].rearrange("(t p) c -> p t c", p=128), writes=[kst])
                vsrc = ccv_d[ei][:, 128 * gi:128 * gi + 128].rearrange("(t p) c -> p t c", p=128)
                bld.dma(pool, VVc.ap[:, :, 0, :], vsrc, writes=[VVc])
            if kind != "C":
                bld.op(dve, lambda: V.memset(VVc.ap[:, :, :, 64:128], 1.0), writes=[VVc])
                bld.op(dve, lambda: V.memset(VV_t[:, :, :, 64:128], 1.0), writes=vv_b)
            pk = banks[7]
            for t4 in range(4):
                bld.op(pe, lambda t4=t4: nc.tensor.transpose(out=pk.ap[:, t4 * 128:(t4 + 1) * 128], in_=kstv[:, t4, :], identity=ident.ap),
                       reads=[kst, ident] if t4 == 0 else (), writes=[pk] if t4 == 0 else (), sig=(t4 == 3))
            bld.op(dve, lambda: V.tensor_copy(out=KTc.ap, in_=pk.ap), reads=[pk], writes=[KTc])

            ps_pr = Ring(banks[0:4])
            ps_x = Ring(banks[4:7])
            tiles = [(w, tb) for tb in range(NTB) for w in (0, 1)]
            cs_loaded = set()
            st = {}
            nrm_col = 104 + ei * 2

            def final_write(w, tb, mk, reads):
                if w == 1:
                    bld.op(dve, lambda: mk(KT_t[:, tbcols(tb)], slice(0, 128)), reads=reads, writes=dst_bufs(w, tb))
                else:
                    for hh_ in range(2):
                        pr_ = slice(hh_ * 64, hh_ * 64 + 64)
                        wb_ = [qo_b[hh_][tb]] if tb < 4 else [qo_b[hh_][4], qo_b[hh_][5]]
                        bld.op(dve, lambda: mk(Qz_t[pr_, hh_, tbcols(tb)], pr_), reads=reads, writes=wb_)

            def dst_bufs(w, tb):
                if w == 1:
                    return [kt_b[tb]]
                if tb < 4:
                    return [qo_b[0][tb], qo_b[1][tb]]
                return [qo_b[0][4], qo_b[1][4], qo_b[0][5], qo_b[1][5]]

            def st0(i):
                w, tb = tiles[i]
                ps = ps_pr.next()
                bld.mm_group(ps, ps.ap, [(w1v[:, k, w * 128:(w + 1) * 128], hT[:, k, tbcols(tb)]) for k in range(8)],
                             reads=[s1, h_b[tb]])
                d = {"ps": ps}
                st[i] = d
                if kind == "B" or (kind == "C" and tb == 4):
                    final_write(w, tb, lambda o_, pr_: V.tensor_copy(out=o_, in_=ps.ap[pr_, :]), [ps])
                elif kind == "A":
                    sq = Hr.next()
                    bld.op(act, lambda: A.activation(out=sq.ap, in_=ps.ap, func=AF.Square), reads=[ps], writes=[sq])
                    d["sq"] = sq
                else:
                    qb_ = Hr.next()
                    bld.op(act, lambda: A.copy(out=qb_.ap, in_=ps.ap), reads=[ps], writes=[qb_])
                    d["qb"] = qb_

            def st1(i):
                w, tb = tiles[i]
                d = st[i]
                ps = d["ps"]
                if kind == "A":
                    pss = ps_x.next()
                    bld.mm(pss.ap, bd_bf, d["sq"].ap, True, True, reads=[d["sq"], cm], writes=[pss])
                    r = Fr.next()
                    rstd_op(pss, r, 64, EPS)
                    wcol = prm.ap[:, nrm_col + w:nrm_col + w + 1]
                    if tb == 4:
                        final_write(w, tb, lambda o_, pr_: V.scalar_tensor_tensor(out=o_, in0=ps.ap[pr_, :], scalar=wcol[pr_, :], in1=r.ap[pr_, :],
                                                                                  op0=ALU.mult, op1=ALU.mult), [ps, r, prm])
                    else:
                        qn = Fr.next()
                        bld.op(dve, lambda: V.scalar_tensor_tensor(out=qn.ap, in0=ps.ap, scalar=wcol, in1=r.ap,
                                                                   op0=ALU.mult, op1=ALU.mult),
                               reads=[ps, r, prm], writes=[qn])
                        qb_ = Hr.next()
                        bld.op(act, lambda: A.copy(out=qb_.ap, in_=qn.ap), reads=[qn], writes=[qb_])
                        d["qn"] = qn
                        d["qb"] = qb_
                if rope and tb < 4:
                    psr = ps_x.next()
                    bld.mm(psr.ap, rot_bf, d["qb"].ap, True, True, reads=[d["qb"], cm], writes=[psr])
                    d["psr"] = psr

            def st2(i):
                w, tb = tiles[i]
                d = st[i]
                if not (rope and tb < 4):
                    return
                cs = cs_tb[tb]
                if (w, tb) not in cs_loaded:
                    if not any(k_[1] == tb for k_ in cs_loaded):
                        bld.dma(sp, cs.ap, cossin_d[:, :, tbcols(tb)].rearrange("m p t -> p m t"), writes=[cs])
                    cs_loaded.add((w, tb))
                cosv = cs.ap[:, 0, :]
                sinv = cs.ap[:, 1, :]
                t2 = Fr.next()
                bld.op(dve, lambda: V.tensor_tensor(out=t2.ap, in0=d["psr"].ap, in1=sinv, op=ALU.mult),
                       reads=[d["psr"], cs], writes=[t2])
                if kind == "A":
                    t1 = d["qn"]
                    bld.op(dve, lambda: V.tensor_tensor(out=t1.ap, in0=t1.ap, in1=cosv, op=ALU.mult), reads=[t1, cs], writes=[t1])
                else:
                    t1 = Fr.next()
                    bld.op(dve, lambda: V.tensor_tensor(out=t1.ap, in0=d["ps"].ap, in1=cosv, op=ALU.mult),
                           reads=[d["ps"], cs], writes=[t1])
                final_write(w, tb, lambda o_, pr_: V.tensor_tensor(out=o_, in0=t1.ap[pr_, :], in1=t2.ap[pr_, :], op=ALU.add), [t1, t2])

            if stage >= 1:
                pipeline(len(tiles), [st0, st1, st2])

            own_out = (kind != "A") or (gi % 2 == 0)
            if kind == "A":
                kd, vd, oc = nak_d, nav_d, slice(64 * kv, 64 * kv + 64)
            elif kind == "B":
                kd, vd, oc = nbk_d, nbv_d, slice(128 * gi, 128 * gi + 128)
            else:
                kd, vd, oc = nck_d, ncv_d, slice(128 * gi, 128 * gi + 128)
            ps_v = Ring(banks[0:4])
            for tt in (range(NTT) if stage >= 2 else ()):
                ps = ps_v.next()
                bld.mm_group(ps, ps.ap[:, 0:nv], [(hT[:, k, tt * 128:(tt + 1) * 128], wvv[:, k, 0:nv]) for k in range(8)],
                             reads=[s2w, h_b[tt // 4]])
                if kind == "B":
                    bld.op(dve, lambda ps=ps, tt=tt: V.tensor_copy(out=VV_t[:, tt, :, 0:64], in_=ps.ap[:, 0:128].rearrange("p (h c) -> p h c", h=2)),
                           reads=[ps], writes=[vv_b[tt]])
                else:
                    bld.op(dve, lambda ps=ps, tt=tt: V.tensor_copy(out=VV_t[:, tt, 0, 0:nv], in_=ps.ap[:, 0:nv]), reads=[ps], writes=[vv_b[tt]])
                if tt >= 16 and own_out and stage >= 2.3:
                    sq_, r0 = (tt - 16) // 2, ((tt - 16) % 2) * 128
                    f = Fr.next()
                    if os.environ.get("K_DVECOPY"):
                        bld.op(dve, lambda ps=ps, f=f: V.tensor_copy(out=f.ap[:, 0:nv], in_=ps.ap[:, 0:nv]), reads=[ps], writes=[f])
                    else:
                        bld.op(act, lambda ps=ps, f=f: A.copy(out=f.ap[:, 0:nv], in_=ps.ap[:, 0:nv]), reads=[ps], writes=[f])
                    if not os.environ.get("K_NODMA"):
                        bld.dma(sp, vd[sq_, ei, r0:r0 + 128, oc], f.ap[:, 0:nv], reads=[f])
                    if stage < 2.6:
                        continue
                    psk = ps_v.next()
                    bld.mm_group(psk, psk.ap[:, 0:nv], [(hT[:, k, tt * 128:(tt + 1) * 128], w1v[:, k, 128:128 + nv]) for k in range(8)],
                                 reads=[s1, h_b[4]])
                    fk = Fr.next()
                    if kind == "A" and stage >= 2.9:
                        junk = Fr.next()
                        ssk = Fr.next()
                        bld.op(act, lambda: A.activation(out=junk.ap[:, 0:64], in_=psk.ap[:, 0:64], func=AF.Square), reads=[psk], writes=[junk])
                        bld.op(dve, lambda: V.reduce_sum(out=ssk.ap[:, 0:1], in_=junk.ap[:, 0:64], axis=mybir.AxisListType.X),
                               reads=[junk], writes=[ssk])
                        bld.op(act, lambda: A.activation(out=ssk.ap[:, 0:1], in_=ssk.ap[:, 0:1], func=AF.Ln, bias=epsc.ap[:, 0:1], scale=1.0 / 64),
                               reads=[ssk, epsc], writes=[ssk])
                        bld.op(act, lambda: A.activation(out=ssk.ap[:, 0:1], in_=ssk.ap[:, 0:1], func=AF.Exp, scale=-0.5), reads=[ssk], writes=[ssk])
                        bld.op(dve, lambda: V.scalar_tensor_tensor(out=fk.ap[:, 0:64], in0=psk.ap[:, 0:64], scalar=ssk.ap[:, 0:1], in1=kw8.ap,
                                                                   op0=ALU.mult, op1=ALU.mult), reads=[psk, ssk, kw8], writes=[fk])
                    else:
                        bld.op(act, lambda: A.copy(out=fk.ap[:, 0:nv], in_=psk.ap[:, 0:nv]), reads=[psk], writes=[fk])
                    bld.dma(sp, kd[sq_, ei, r0:r0 + 128, oc], fk.ap[:, 0:nv], reads=[fk])
                    if dbg == 3 and kind == "A" and tt == 16 and gi == 0:
                        dbg5_d = dout("dbg5", [128, 4, 64])
                        bld.dma(sp, dbg5_d[:, 0, :], junk.ap[:, 0:64], reads=[junk])
                        bld.dma(sp, dbg5_d[:, 1, :], ssk.ap[:, 0:64], reads=[ssk])
                        bld.dma(sp, dbg5_d[:, 2, :], fk.ap[:, 0:64], reads=[fk])
                        bld.dma(sp, dbg5_d[:, 3, :], kw8.ap[:, 0:64], reads=[kw8])

            if nxt is not None and nxt not in prefetched:
                prefetched[nxt] = load_group_weights(*nxt)

            S_ring = Ring(banks[0:4])
            acc_ring = Ring(banks[4:6])
            qblocks = [(q, slice(q * 512, (q + 1) * 512), 512) for q in range(4)] + \
                      [(4 + s_, slice(TS + 256 * s_, TS + 256 * s_ + 256), 256) for s_ in range(2)]

            def key_chunks(q):
                res = []
                if q < 4:
                    res += [("cache", c) for c in range(4)]
                    if kind == "B":
                        rows = set()
                        for qr in range(8 * q, 8 * q + 8):
                            rs_ = min(max(qr - 4, 0), 24)
                            rows.update(range(rs_, rs_ + 8))
                        res += [("nb", c) for c in range(min(rows) // 2, max(rows) // 2 + 1)]
                    else:
                        res += [("own", c) for c in range(16)]
                else:
                    s_ = q - 4
                    res += [("own", 16 + 2 * s_), ("own", 17 + 2 * s_)]
                return res

            def kT_of(typ, c, pr):
                if typ == "cache":
                    return KTc.ap[pr, c * 128:(c + 1) * 128], [KTc]
                return KT_t[pr, c * 128:(c + 1) * 128], [kt_b[c // 4]]

            def v_of(typ, c, hh, pr=slice(0, 128)):
                if typ == "cache":
                    return VVc.ap[pr, c, hh, :], [VVc]
                return VV_t[pr, c, hh, :], [vv_b[c]]

            items = []
            if stage < 3:
                pass
            elif kind in ("A", "B"):
                for hh in range(2):
                    pr = slice(hh * 64, hh * 64 + 64)
                    vh = hh if kind == "B" else 0
                    for (q, qc, nq) in qblocks:
                        chunks = key_chunks(q)
                        blk = {"acc": None}
                        for ci, (typ, c) in enumerate(chunks):
                            first, last = ci == 0, ci == len(chunks) - 1

                            def s_stage(d, typ=typ, c=c, pr=pr, q=q, qc=qc, nq=nq, hh=hh):
                                kT, kr_ = kT_of(typ, c, slice(0, 128))
                                if typ == "nb":
                                    rng = nb_ranges(q, c)
                                    act_r = [r_ for r_ in rng if r_ is not None]
                                    cu0, cu1 = min(r_[0] for r_ in act_r), max(r_[1] for r_ in act_r)
                                else:
                                    rng = None
                                    cu0, cu1 = 0, nq
                                S = S_ring.next()
                                qs = Qz_t[:, hh, qc.start + cu0:qc.start + cu1]
                                bld.mm(S.ap[:, cu0:cu1], kT, qs, True, True, reads=kr_ + [qo_b[hh][q]], writes=[S])
                                P = Hr.next()
                                bld.op(act, lambda: A.activation(out=P.ap[:, cu0:cu1], in_=S.ap[:, cu0:cu1], func=AF.Exp, scale=SCALE),
                                       reads=[S], writes=[P])
                                if typ == "nb":
                                    for krl in range(2):
                                        kp = slice(krl * 64, krl * 64 + 64)
                                        if rng[krl] is None:
                                            zr = [(cu0, cu1)]
                                        else:
                                            a0, a1, i0 = rng[krl]
                                            bld.op(dve, lambda: V.tensor_tensor(
                                                out=P.ap[kp, a0:a1], in0=P.ap[kp, a0:a1], in1=TF.ap[kp, hh, i0 * 64:i0 * 64 + (a1 - a0)], op=ALU.mult),
                                                reads=[P, TF], writes=[P])
                                            zr = [(cu0, a0), (a1, cu1)]
                                        for (z0, z1) in zr:
                                            if z1 > z0:
                                                bld.op(dve, lambda: V.memset(P.ap[kp, z0:z1], 0.0), writes=[P])
                                d["P"], d["cu"] = P, (cu0, cu1)

                            def pv_stage(d, typ=typ, c=c, vh=vh, first=first, last=last, blk=blk, nq=nq):
                                if first:
                                    blk["acc"] = acc_ring.next()
                                acc = blk["acc"]
                                P = d["P"]
                                cu0, cu1 = d["cu"]
                                vl, vr_ = v_of(typ, c, vh)
                                kw = {"skip_group_check": True} if kind == "B" else {}
                                bld.mm(acc.ap[:, cu0:cu1], vl, P.ap[:, cu0:cu1], first, last, reads=vr_ + [P], writes=[acc], **kw)

                            fin = None
                            if last:
                                def fin(blk=blk, nq=nq, qc=qc, pr=pr, hh=hh, q=q):
                                    acc = blk["acc"]
                                    rc = Fr.next()
                                    bld.op(dve, lambda: V.reciprocal(out=rc.ap[0:64, 0:nq], in_=acc.ap[64:128, 0:nq]), reads=[acc], writes=[rc])
                                    bld.op(dve, lambda: V.tensor_tensor(out=OT_t[pr, qc], in0=acc.ap[0:64, 0:nq], in1=rc.ap[0:64, 0:nq], op=ALU.mult),
                                           reads=[acc, rc], writes=[ot_b[q]])
                            items.append((s_stage, pv_stage, fin))
                LA = 2
            else:
                O1, O2, D1, D2 = banks[4], banks[5], banks[6], banks[7]
                lo, hi = slice(0, 64), slice(64, 128)
                sqr = Ring([scoped(f"sqc{i}", [128, 512], BF16, s2) for i in range(2)])
                for (q, qc, nq) in qblocks:
                    chunks = key_chunks(q)
                    for ci, (typ, c) in enumerate(chunks):
                        first, last = ci == 0, ci == len(chunks) - 1

                        def s_stage(d, typ=typ, c=c, q=q, qc=qc, nq=nq):
                            d["P"] = []
                            for mj in range(2):
                                kT, kr_ = kT_of(typ, c, slice(0, 128))
                                S = S_ring.next()
                                bld.mm(S.ap[:, 0:nq], kT, Qz_t[:, mj, qc], True, True, reads=kr_ + [qo_b[mj][q]], writes=[S])
                                P = Hr.next()
                                bld.op(act, lambda: A.activation(out=P.ap[:, 0:nq], in_=S.ap[:, 0:nq], func=AF.Exp, scale=SCALE),
                                       reads=[S], writes=[P])
                                d["P"].append(P)

                        def pv_stage(d, typ=typ, c=c, first=first, last=last, nq=nq):
                            vl, vr_ = v_of(typ, c, 0)
                            for (P, Oa, Da) in ((d["P"][0], O1, D1), (d["P"][1], O2, D2)):
                                bld.mm(Oa.ap[:, 0:nq], vl, P.ap[:, 0:nq], first, last, reads=vr_ + [P], writes=[Oa])
                                bld.mm(Da.ap[:, 0:nq], ones_bf, P.ap[:, 0:nq], first, last, reads=[cm, P], writes=[Da])

                        fin = None
                        if last:
                            def fin(nq=nq, qc=qc, q=q):
                                r1 = Fr.next()
                                r2 = Fr.next()
                                bld.op(dve, lambda: V.reciprocal(out=r1.ap[:, 0:nq], in_=D1.ap[:, 0:nq]), reads=[D1], writes=[r1])
                                bld.op(dve, lambda: V.tensor_tensor(out=r1.ap[:, 0:nq], in0=O1.ap[:, 0:nq], in1=r1.ap[:, 0:nq], op=ALU.mult),
                                       reads=[O1, r1], writes=[r1])
                                bld.op(dve, lambda: V.reciprocal(out=r2.ap[:, 0:nq], in_=D2.ap[:, 0:nq]), reads=[D2], writes=[r2])
                                bld.op(dve, lambda: V.tensor_tensor(out=r2.ap[:, 0:nq], in0=O2.ap[:, 0:nq], in1=r2.ap[:, 0:nq], op=ALU.mult),
                                       reads=[O2, r2], writes=[r2])
                                bld.op(dve, lambda: V.scalar_tensor_tensor(out=r1.ap[:, 0:nq], in0=r2.ap[:, 0:nq], scalar=lamt.ap[:, 0:1], in1=r1.ap[:, 0:nq],
                                                                           op0=ALU.mult, op1=ALU.add), reads=[r1, r2, lamt], writes=[r1])
                                sqc = sqr.next()
                                bld.op(act, lambda: A.activation(out=sqc.ap[:, 0:nq], in_=r1.ap[:, 0:nq], func=AF.Square), reads=[r1], writes=[sqc])

                                def fin_b():
                                    pss = S_ring.next()
                                    bld.mm(pss.ap[:, 0:nq], ones_bf, sqc.ap[:, 0:nq], True, True, reads=[sqc, cm], writes=[pss])
                                    rstd_op(pss, r2, 128, EPS, cols=slice(0, nq))
                                    bld.op(dve, lambda: V.scalar_tensor_tensor(out=OT_t[:, qc], in0=r1.ap[:, 0:nq], scalar=lamt.ap[:, 1:2], in1=r2.ap[:, 0:nq],
                                                                               op0=ALU.mult, op1=ALU.mult), reads=[r1, r2, lamt], writes=[ot_b[q]])
                                return fin_b
                        items.append((s_stage, pv_stage, fin))
                LA = 1
            if items:
                ds_ = [dict() for _ in items]
                n_it = len(items)
                deferred = []
                DEF = 8
                op_ring = S_ring if kind == "C" else Ring(banks[6:8])
                n_prev = len(mixst["units"])
                every = max(1, (n_it - 4) // max(1, n_prev)) if n_prev else 1
                for step in range(n_it + LA):
                    if mixst["units"] and n_prev and step % every == 0 and step > 2:
                        mixst["units"].pop(0)(op_ring.next())
                    if step < n_it:
                        items[step][0](ds_[step])
                    while deferred and deferred[0][0] <= step:
                        deferred.pop(0)[1]()
                    j_ = step - LA
                    if j_ >= 0:
                        items[j_][1](ds_[j_])
                        if items[j_][2] is not None:
                            while deferred:
                                deferred.pop(0)[1]()
                            later = items[j_][2]()
                            if later is not None:
                                deferred.append((step + DEF, later))
                for _, later in deferred:
                    later()
                if kind == "C":
                    _fence_scope(sqr.tiles)

            if mixst["units"]:
                ps_o = Ring(banks[0:8])
                while mixst["units"]:
                    mixst["units"].pop(0)(ps_o.next())

            units = []
            for tb in (range(NTB) if stage >= 4 else ()):
                cd = cond_of(tb)
                qbufs = [ot_b[tb]] if tb < 4 else [ot_b[4], ot_b[5]]
                for n in range(NCH):
                    def unit(po, tb=tb, n=n, cd=cd, qbufs=qbufs, wo=wo, s2w=s2w, OT_t=OT_t):
                        bld.mm(po.ap, wo[:, n * 128:(n + 1) * 128], OT_t[:, tbcols(tb)], True, True, reads=[s2w] + qbufs, writes=[po])
                        bld.op(dve, lambda: V.scalar_tensor_tensor(
                            out=xT[:, n, tbcols(tb)], in0=po.ap, scalar=drv.ap[:, 2, 1, n, cd:cd + 1], in1=xT[:, n, tbcols(tb)],
                            op0=ALU.mult, op1=ALU.add), reads=[po, drv, x_b[tb][n]], writes=[x_b[tb][n]])
                    units.append(unit)
            mixst["units"].extend(units)
            if nxt is None:
                ps_o = Ring(banks[0:8])
                while mixst["units"]:
                    mixst["units"].pop(0)(ps_o.next())

            for lst in (qo_b[0], qo_b[1], kt_b, vv_b):
                pending_fence.extend(lst)
            last_group["kind"] = kind
            last_group["bufs"] = grp["cur"]
            grp["cur"] = None
            grp["prev"] = None
            _fence_scope([KTc, VVc] + Fr.tiles + Hr.tiles)
            if rope:
                _fence_scope(csr.tiles)
            if kind == "B":
                _fence_scope([TF])

    def mixer(l):
        ei = l // 2
        if l % 2 == 0:
            seq = [("A", l, gi) for gi in range(4)] + [("B", l, gi) for gi in range(4)]
        else:
            seq = [("C", l, gi) for gi in range(8)]
        if groups is not None:
            seq = [g_ for g_ in seq if (g_[0], g_[2]) in groups]
        if seq:
            prefetched[seq[0]] = load_group_weights(*seq[0])
        prenorm(l, 1)
        mix_es = contextlib.ExitStack()
        mixst["OT"] = [sb(f"OTp{i}", [128, T], BF16, mix_es) for i in range(2)]
        mixst["ot_b"] = [[scoped_buf(f"otp{i}_{q}") for q in range(6)] for i in range(2)]
        mixst["units"] = []
        if l % 2 == 0:
            bld.dma(sp, kw8.ap, kwbc_d[ei], writes=[kw8])
            for i_, g_ in enumerate(seq):
                mixst["slot"] = i_ % 2
                run_group(g_[0], l, g_[2], nxt=seq[i_ + 1] if i_ + 1 < len(seq) else None)
        else:
            with contextlib.ExitStack() as s2:
                lt = scoped("lt", [128, 256], F32, s2)
                lp = scoped("lp", [128, 128], F32, s2)
                ls = scoped("ls", [128, 2], F32, s2)
                bld.dma(sp, lt.ap, lam_d[ei], writes=[lt])
                ltv = lt.ap.rearrange("p (a d) -> p a d", a=4)
                bld.op(dve, lambda: V.tensor_tensor(out=lp.ap[:, 0:64], in0=ltv[:, 0, :], in1=ltv[:, 1, :], op=ALU.mult), reads=[lt], writes=[lp])
                bld.op(dve, lambda: V.tensor_tensor(out=lp.ap[:, 64:128], in0=ltv[:, 2, :], in1=ltv[:, 3, :], op=ALU.mult), reads=[lt], writes=[lp])
                bld.op(dve, lambda: V.reduce_sum(out=ls.ap, in_=lp.ap.rearrange("p (a d) -> p a d", a=2), axis=mybir.AxisListType.X),
                       reads=[lp], writes=[ls])
                bld.op(act, lambda: A.activation(out=ls.ap, in_=ls.ap, func=AF.Exp), reads=[ls], writes=[ls])
                bld.op(dve, lambda: V.tensor_tensor(out=lamt.ap[:, 0:1], in0=ls.ap[:, 1:2], in1=ls.ap[:, 0:1], op=ALU.subtract),
                       reads=[ls], writes=[lamt])
                bld.op(dve, lambda: V.tensor_scalar_add(out=lamt.ap[:, 0:1], in0=lamt.ap[:, 0:1], scalar1=-LAM_INIT[l]), reads=[lamt], writes=[lamt])
                bld.op(dve, lambda: V.tensor_scalar_mul(out=lamt.ap[:, 1:2], in0=prm.ap[:, 108 + ei:109 + ei], scalar1=1.0 - LAM_INIT[l]),
                       reads=[prm], writes=[lamt])
                _fence_scope([lt, lp, ls])
            for i_, g_ in enumerate(seq):
                mixst["slot"] = i_ % 2
                run_group(g_[0], l, g_[2], nxt=seq[i_ + 1] if i_ + 1 < len(seq) else None)
        for lst in mixst["ot_b"]:
            pending_fence.extend(lst)
        mix_es.close()
        last_group["kind"] = None

    for l in range(NL):
        if l == 0 or not do_ffn:
            mod_begin(l)
            for pi in range(36):
                mod_piece(l, pi)
        mod_finish(l)
        mod_ps = None
        if do_ffn:
            prenorm(l, 0)
            if dbg == 2 and l == 0:
                dbg2_d = dout("dbg2", [128, 8, 128])
                bld.dma(pool, dbg2_d[:, :, 0:64], hT[:, :, 0:64], reads=[h_b[0]])
                bld.dma(pool, dbg2_d[:, :, 64:128], hT[:, :, 2048:2112], reads=[h_b[4]])
            ffn(l, 0, 0)
            if dbg == 2 and l == 0:
                dbg3_d = dout("dbg3", [128, 8, 160])
                for tb_ in range(5):
                    bld.dma(sp, dbg3_d[:, :, tb_ * 32:(tb_ + 1) * 32], xT[:, :, tb_ * 512 + 100:tb_ * 512 + 132], reads=x_b[tb_])
        if do_mixer:
            mixer(l)
        if do_ffn:
            prenorm(l, 2)
            if l + 1 < NL:
                mod_begin(l + 1)
                ffn(l, 1, 2, between=lambda sgi, l=l: [mod_piece(l + 1, pi) for pi in range(sgi * 6, sgi * 6 + 6)])
            else:
                ffn(l, 1, 2)

    if dbg:
        dbg_d = dout("dbg", [128, 144 + 144])
        bld.dma(sp, dbg_d[:, 0:144], modT.ap.rearrange("p j c -> p (j c)"), reads=[modT])
        bld.dma(sp, dbg_d[:, 144:288], drv.ap.rearrange("p a s c k -> p (a s c k)"), reads=[drv])
    final()
    bld.finish()
    return nc, bld


_PROG = {}


def _get_prog(**kw):
    key = tuple(sorted(kw.items()))
    if key not in _PROG:
        _PROG[key] = build_program(**kw)
    return _PROG[key]


def make_in_maps(inputs, cores, NL=DEPTH):
    f = lambda a: np.ascontiguousarray(np.asarray(a, dtype=np.float32))
    cossin, ident, cmats, colmask = _const_tables()
    n_even = (NL + 1) // 2
    n_odd = NL // 2
    shared = {
        "b_mod": f(inputs["b_mod"]).reshape(DEPTH, 72, 128),
        "norm_w": f(inputs["norm_w"]).reshape(96, 128),
        "qk_norm": f(np.stack([np.concatenate([inputs["a_q_norm"][e], inputs["a_q_norm"][e]]) if w == 0 else
                               np.concatenate([inputs["a_k_norm"][e], inputs["a_k_norm"][e]])
                               for e in range(2) for w in range(2)], 0)),
        "rpbx": _expand_rpb(np.asarray(inputs["b_rpb"], np.float32)),
        "c_lambda": f(np.broadcast_to(np.asarray(inputs["c_lambda"], np.float32).reshape(2, 1, 256), (2, 128, 256))),
        "kw_bc": f(np.broadcast_to(np.asarray(inputs["a_k_norm"], np.float32).reshape(2, 1, 64), (2, 128, 64))),
        "c_subln": f(inputs["c_subln"]),
        "final_norm": f(inputs["final_norm"]).reshape(8, 128),
        "cossin": cossin, "ident": ident, "cmats": cmats, "colmask": colmask,
        "colmaskx": np.ascontiguousarray(np.tile(colmask, (1, 15))),
    }
    for l in range(NL):
        shared[f"w_mod{l}"] = f(inputs["w_mod"][l])
        for ff in range(2):
            shared[f"w1_{l}_{ff}"] = f(inputs["ffn_w1"][l, ff])
            shared[f"w3_{l}_{ff}"] = f(inputs["ffn_w3"][l, ff])
            shared[f"w2_{l}_{ff}"] = f(inputs["ffn_w2"][l, ff])
    for e in range(n_even):
        shared[f"w_in_ab{e}"] = f(inputs["w_in_ab"][e])
        shared[f"w_out_ab{e}"] = f(inputs["w_out_ab"][e])
    for o in range(n_odd):
        shared[f"w_in_c{o}"] = f(inputs["w_in_c"][o])
        shared[f"w_out_c{o}"] = f(inputs["w_out_c"][o])
    maps = []
    for b in cores:
        m = dict(shared)
        m["xs"] = f(inputs["x_sample"][b])
        m["xp"] = f(inputs["x_prompt"][2 * b:2 * b + 2]).reshape(TC, D)
        m["cak"] = f(inputs["cache_a_k"][b]).reshape(2, 512, 128)
        m["cav"] = f(inputs["cache_a_v"][b]).reshape(2, 512, 128)
        m["cbk"] = f(inputs["cache_b_k"][b]).reshape(2, 512, 512)
        m["cbv"] = f(inputs["cache_b_v"][b]).reshape(2, 512, 512)
        m["cck"] = f(inputs["cache_c_k"][b]).reshape(2, 512, 1024)
        m["ccv"] = f(inputs["cache_c_v"][b]).reshape(2, 512, 1024)
        m["cvec"] = f(np.stack([inputs["c"][b], inputs["c_ctx"]], 0)).reshape(16, 128)
        maps.append(m)
    return maps


def gather_outputs(results, n):
    yp = np.concatenate([r["yp"].reshape(2, 256, D) for r in results], 0)
    ys = np.stack([r["ys"] for r in results], 0)
    nak = np.concatenate([r["nak"].reshape(2, 2, 256, 2, 64) for r in results], 0)
    nav = np.concatenate([r["nav"].reshape(2, 2, 256, 2, 64) for r in results], 0)
    nbk = np.concatenate([r["nbk"].reshape(2, 2, 256, 8, 64) for r in results], 0)
    nbv = np.concatenate([r["nbv"].reshape(2, 2, 256, 8, 64) for r in results], 0)
    nck = np.concatenate([r["nck"].reshape(2, 2, 256, 8, 128) for r in results], 0)
    ncv = np.concatenate([r["ncv"].reshape(2, 2, 256, 8, 128) for r in results], 0)
    return tuple(np.ascontiguousarray(a.astype(np.float32)) for a in (yp, ys, nak, nav, nbk, nbv, nck, ncv))


def kernel(**inputs):
    nc, _ = _get_prog()
    in_maps = make_in_maps(inputs, list(range(N_CORES)))
    res = run_bass_kernel_spmd(nc, in_maps, core_ids=list(range(N_CORES)))
    return gather_outputs(res.results, N_CORES)
```

```python
import math
import os
import contextlib
import numpy as np
import concourse.bass as bass
import concourse.mybir as mybir
from concourse.bass_utils import run_bass_kernel_spmd

F32 = mybir.dt.float32
BF16 = mybir.dt.bfloat16
ALU = mybir.AluOpType
AF = mybir.ActivationFunctionType

D = 1024
NCH = 8
DFF = 2816
NF = 22
DEPTH = 4
TS = 2048
TC = 512
T = TS + TC
NTB = 5
NTT = 20
GRID_W = 64
EPS = 1e-6
SCALE = 0.125
LAM_INIT = [0.8 - 0.6 * math.exp(-0.3 * l) for l in range(DEPTH)]
N_CORES = 8
TRACE_BUF = None
STRICT = bool(os.environ.get("K_STRICT"))


class Eng:
    def __init__(self, name, h, sem):
        self.name = name
        self.h = h
        self.sem = sem
        self.cnt = 0
        self.seen = {}


class DSem:
    def __init__(self, sem):
        self.sem = sem
        self.cnt = 0


class Buf:
    __slots__ = ("w", "r", "name", "excl")

    def __init__(self, name=""):
        self.w = None
        self.r = {}
        self.name = name
        self.excl = False


class Tile:
    __slots__ = ("ap", "buf")

    def __init__(self, ap, name=""):
        self.ap = ap
        self.buf = Buf(name)


class Ring:
    def __init__(self, tiles):
        self.tiles = tiles
        self.i = 0

    def next(self):
        t = self.tiles[self.i % len(self.tiles)]
        self.i += 1
        return t


class B:
    def __init__(self, nc, es):
        self.nc = nc
        self.es = es
        mk = lambda n: es.enter_context(nc.semaphore(n))
        self.pe = Eng("pe", nc.tensor, mk("s_pe"))
        self.act = Eng("act", nc.scalar, mk("s_act"))
        self.dve = Eng("dve", nc.vector, mk("s_dve"))
        self.pool = Eng("pool", nc.gpsimd, mk("s_pool"))
        self.sp = Eng("sp", nc.sync, mk("s_sp"))
        self.engs = [self.pe, self.act, self.dve, self.pool, self.sp]
        self.dsems = {}
        for q in (self.sp, self.pool):
            self.dsems[q.name] = [DSem(mk(f"d_{q.name}{i}")) for i in range(8)]
        self.dma_i = {"sp": 0, "pool": 0}
        self.fence = {}
        self.n_ins = 0

    def _need(self, eng, reads, writes):
        need = {}

        def add(tag, raw=False):
            if tag is None:
                return
            o, c = tag
            if o is eng and (eng.name == "pe" or not (raw or STRICT)):
                return
            if need.get(o, 0) < c:
                need[o] = c

        for b in reads:
            add(b.w, raw=True)
            if b.excl:
                for o, c in b.r.items():
                    add((o, c))
        for b in writes:
            add(b.w)
            for o, c in b.r.items():
                add((o, c))
        return need

    def _emit_waits(self, eng, need):
        for o, c in need.items():
            if eng.seen.get(o, 0) < c:
                eng.h.wait_ge(o.sem, c)
                eng.seen[o] = c
                self.n_ins += 1

    def op(self, eng, fn, reads=(), writes=(), sig=True):
        reads = [t.buf if isinstance(t, Tile) else t for t in reads]
        writes = [t.buf if isinstance(t, Tile) else t for t in writes]
        need = self._need(eng, reads, writes)
        if TRACE_BUF and any(b.name == TRACE_BUF for b in list(reads) + list(writes)):
            print("TRACE", eng.name, "cnt", eng.cnt, "sig", sig, "reads", [b.name for b in reads], "writes", [b.name for b in writes],
                  "need", {getattr(o, 'name', 'dsem'): c for o, c in need.items()}, "seen", {getattr(o, 'name', 'dsem'): c for o, c in eng.seen.items()})
        self._emit_waits(eng, need)
        ins = fn()
        self.n_ins += 1
        if sig:
            ins.then_inc(eng.sem, 1)
            eng.cnt += 1
            tag = eng.cnt
        else:
            tag = eng.cnt + 1
        for b in reads:
            if b.r.get(eng, 0) < tag:
                b.r[eng] = tag
        for b in writes:
            b.w = (eng, tag)
            b.r = {}
        return ins

    def dma(self, q, out, in_, reads=(), writes=()):
        reads = [t.buf if isinstance(t, Tile) else t for t in reads]
        writes = [t.buf if isinstance(t, Tile) else t for t in writes]
        lst = self.dsems[q.name]
        ds = lst[self.dma_i[q.name] % len(lst)]
        self.dma_i[q.name] += 1
        need = self._need(q, reads, writes)
        if ds.cnt > 0:
            need[ds] = max(need.get(ds, 0), ds.cnt)
        self._emit_waits(q, need)
        q.h.dma_start(out=out, in_=in_).then_inc(ds.sem, 16)
        self.n_ins += 1
        ds.cnt += 16
        for b in reads:
            b.r[ds] = ds.cnt
        for b in writes:
            b.w = (ds, ds.cnt)
            b.r = {}

    def mm(self, out, lhsT, rhs, start, stop, reads=(), writes=(), sig=True, **kw):
        return self.op(self.pe, lambda: self.nc.tensor.matmul(out, lhsT=lhsT, rhs=rhs, start=start, stop=stop, **kw),
                       reads=reads, writes=writes, sig=sig)

    def mm_group(self, out_t, out_ap, pairs, reads):
        n = len(pairs)
        for i, (l, r) in enumerate(pairs):
            self.mm(out_ap, l, r, start=(i == 0), stop=(i == n - 1),
                    reads=reads if i == 0 else (), writes=[out_t] if i == 0 else (), sig=(i == n - 1))

    def finish(self):
        for q in (self.sp, self.pool):
            for ds in self.dsems[q.name]:
                if ds.cnt > 0 and q.seen.get(ds, 0) < ds.cnt:
                    q.h.wait_ge(ds.sem, ds.cnt)
        for e in self.engs:
            if e is not self.sp and e.cnt > 0:
                self.sp.h.wait_ge(e.sem, e.cnt)
        for ds in self.dsems["pool"]:
            if ds.cnt > 0:
                self.sp.h.wait_ge(ds.sem, ds.cnt)


def _const_tables():
    t = np.arange(TS)
    row = (t // GRID_W).astype(np.float32)
    col = (t % GRID_W).astype(np.float32)
    n_freq = 16
    inv_freq = (np.float32(10000.0) ** (-np.arange(n_freq, dtype=np.float32) / n_freq)).astype(np.float32)
    ang_r = row[:, None] * inv_freq
    ang_c = col[:, None] * inv_freq
    ang = np.concatenate([ang_r, ang_r, ang_c, ang_c], axis=-1)
    cos = np.cos(ang).astype(np.float32).T
    sin = np.sin(ang).astype(np.float32).T
    cossin = np.stack([np.concatenate([cos, cos], 0), np.concatenate([sin, sin], 0)], 0)
    ident = np.eye(128, dtype=np.float32)
    ones = np.ones((128, 128), np.float32)
    bd = np.zeros((128, 128), np.float32)
    bd[:64, :64] = 1.0
    bd[64:, 64:] = 1.0
    R = np.zeros((128, 128), np.float32)
    for base in (0, 64):
        for i in range(16):
            R[base + 16 + i, base + i] = -1.0
            R[base + i, base + 16 + i] = 1.0
            R[base + 48 + i, base + 32 + i] = -1.0
            R[base + 32 + i, base + 48 + i] = 1.0
    cmats = np.stack([ones, bd, R], 0)
    qc = np.arange(64)
    kc = np.arange(64)
    col_start = np.clip(qc - 8, 0, 64 - 16)
    cm = ((kc[:, None] >= col_start[None, :]) & (kc[:, None] < col_start[None, :] + 16)).astype(np.float32)
    colmask = np.concatenate([cm, cm], 0)
    return cossin, ident, cmats, colmask


def _expand_rpb(b_rpb):
    kc = np.arange(64)[:, None]
    qc = np.arange(64)[None, :]
    dc = np.clip(kc - qc + 15, 0, 30)
    dr = 14 - np.arange(15)
    g = b_rpb[:, :, dr][:, :, :, dc]
    g = np.transpose(g, (0, 1, 3, 2, 4))
    g = np.concatenate([g, g], axis=2)
    return np.ascontiguousarray(g.reshape(2, 8, 128, 15 * 64)).astype(np.float32)


def build_program(NL=DEPTH, do_mixer=True, do_ffn=True, dbg=False, groups=None, stage=9):
    nc = bass.Bass("TRN2", target_bir_lowering=False)
    es = contextlib.ExitStack()
    bld = B(nc, es)
    pe, act, dve, pool, sp = bld.pe, bld.act, bld.dve, bld.pool, bld.sp
    V = nc.vector
    A = nc.scalar
    G = nc.gpsimd

    def din(name, shape):
        return nc.dram_tensor(name, list(shape), F32, kind="ExternalInput").ap()

    def dout(name, shape):
        return nc.dram_tensor(name, list(shape), F32, kind="ExternalOutput").ap()

    xs_d = din("xs", [TS, D])
    xp_d = din("xp", [TC, D])
    cak_d = din("cak", [2, 512, 128])
    cav_d = din("cav", [2, 512, 128])
    cbk_d = din("cbk", [2, 512, 512])
    cbv_d = din("cbv", [2, 512, 512])
    cck_d = din("cck", [2, 512, 1024])
    ccv_d = din("ccv", [2, 512, 1024])
    cvec_d = din("cvec", [16, 128])
    wmod_d = [din(f"w_mod{l}", [D, 9 * D]) for l in range(NL)]
    bmod_d = din("b_mod", [DEPTH, 72, 128])
    normw_d = din("norm_w", [96, 128])
    w1_d = [[din(f"w1_{l}_{f}", [D, DFF]) for f in range(2)] for l in range(NL)]
    w3_d = [[din(f"w3_{l}_{f}", [D, DFF]) for f in range(2)] for l in range(NL)]
    w2_d = [[din(f"w2_{l}_{f}", [DFF, D]) for f in range(2)] for l in range(NL)]
    n_even = (NL + 1) // 2
    n_odd = NL // 2
    winab_d = [din(f"w_in_ab{e}", [D, 2304]) for e in range(n_even)]
    woutab_d = [din(f"w_out_ab{e}", [D, D]) for e in range(n_even)]
    qkn_d = din("qk_norm", [4, 128])
    rpbx_d = din("rpbx", [2, 8, 128, 960])
    winc_d = [din(f"w_in_c{o}", [D, 3072]) for o in range(n_odd)]
    woutc_d = [din(f"w_out_c{o}", [D, D]) for o in range(n_odd)]
    lam_d = din("c_lambda", [2, 128, 256])
    kwbc_d = din("kw_bc", [2, 128, 64])
    subln_d = din("c_subln", [2, 128])
    fnorm_d = din("final_norm", [8, 128])
    cossin_d = din("cossin", [2, 128, TS])
    ident_d = din("ident", [128, 128])
    cmats_d = din("cmats", [3, 128, 128])
    colmask_d = din("colmask", [128, 64])
    colmaskx_d = din("colmaskx", [128, 960])

    yp_d = dout("yp", [TC, D])
    ys_d = dout("ys", [TS, D])
    nak_d = dout("nak", [2, 2, 256, 128])
    nav_d = dout("nav", [2, 2, 256, 128])
    nbk_d = dout("nbk", [2, 2, 256, 512])
    nbv_d = dout("nbv", [2, 2, 256, 512])
    nck_d = dout("nck", [2, 2, 256, 1024])
    ncv_d = dout("ncv", [2, 2, 256, 1024])

    uid = [0]

    def sb(name, shape, dt, stack=None):
        uid[0] += 1
        return (stack or es).enter_context(nc.sbuf_tensor(f"sb_{name}_{uid[0]}", list(shape), dt))

    xT = sb("xT", [128, NCH, T], F32)
    hT = sb("hT", [128, NCH, T], BF16)
    x_b = [[Buf(f"x{tb}_{c}") for c in range(NCH)] for tb in range(NTB)]
    h_b = [Buf(f"h{tb}") for tb in range(NTB)]
    ident = Tile(sb("ident", [128, 128], F32)[:], "ident")
    cm = Tile(sb("cm", [128, 3, 128], BF16)[:], "cm")
    ones_bf = cm.ap[:, 0, :]
    bd_bf = cm.ap[:, 1, :]
    rot_bf = cm.ap[:, 2, :]
    prm = Tile(sb("prm", [128, 128], F32)[:], "prm")
    modT = Tile(sb("modT", [128, 72, 2], F32)[:], "modT")
    bmodT = Tile(sb("bmodT", [128, 72], F32)[:], "bmodT")
    drv = Tile(sb("drv", [128, 3, 3, NCH, 2], F32)[:], "drv")
    scT = Tile(sb("scT", [128, 16], BF16)[:], "scT")
    lamt = Tile(sb("lamt", [128, 8], F32)[:], "lamt")
    kw8 = Tile(sb("kw8", [128, 64], F32)[:], "kw8")
    colmask = Tile(sb("colmask", [128, 64], F32)[:], "colmask")
    epsc = Tile(sb("epsc", [128, 1], F32)[:], "epsc")
    NSLOT = 6
    wring_t = sb("wring", [128, NSLOT, 2048], BF16)
    wring = Ring([Tile(wring_t[:, i, :], f"w{i}") for i in range(NSLOT)])

    banks = [Tile(es.enter_context(nc.psum_tensor(f"ps{i}", [128, 512], F32))[:], f"ps{i}") for i in range(8)]
    for bk in banks:
        bk.buf.excl = True
    ps_all = Ring(banks)

    def tbcols(tb):
        return slice(tb * 512, (tb + 1) * 512)

    def cond_of(tb):
        return 1 if tb == 4 else 0

    with contextlib.ExitStack() as ss:
        stg = Ring([Tile(sb(f"stg{i}", [128, D], F32, ss)[:], f"stg{i}") for i in range(3)])
        bld.dma(sp, ident.ap, ident_d, writes=[ident])
        bld.dma(pool, cm.ap, cmats_d.rearrange("m p n -> p m n"), writes=[cm])
        bld.dma(sp, colmask.ap, colmask_d, writes=[colmask])
        bld.op(dve, lambda: V.memset(epsc.ap, EPS), writes=[epsc])

        s0 = stg.next()
        bld.dma(sp, s0.ap[0:96, 0:128], normw_d, writes=[s0])
        bld.dma(sp, s0.ap[96:104, 0:128], fnorm_d, writes=[s0])
        bld.dma(sp, s0.ap[104:108, 0:128], qkn_d, writes=[s0])
        bld.dma(sp, s0.ap[108:110, 0:128], subln_d, writes=[s0])
        bld.dma(sp, s0.ap[110:126, 0:128], cvec_d, writes=[s0])
        p0 = ps_all.next()
        bld.op(pe, lambda: nc.tensor.transpose(out=p0.ap[:, 0:126], in_=s0.ap[0:126, 0:128], identity=ident.ap[0:126, 0:126]),
               reads=[s0, ident], writes=[p0])
        bld.op(dve, lambda: V.tensor_copy(out=prm.ap[:, 0:110], in_=p0.ap[:, 0:110]), reads=[p0], writes=[prm])
        bld.op(act, lambda: A.activation(out=scT.ap, in_=p0.ap[:, 110:126], func=AF.Silu), reads=[p0], writes=[scT])

        for i in range(NTT):
            st = stg.next()
            src = xs_d[i * 128:(i + 1) * 128, :] if i < 16 else xp_d[(i - 16) * 128:(i - 15) * 128, :]
            bld.dma(sp, st.ap, src, writes=[st])
            tb = i // 4
            for half in range(2):
                ps = ps_all.next()
                for c4 in range(4):
                    c = half * 4 + c4
                    bld.op(pe, lambda ps=ps, c4=c4, c=c, st=st: nc.tensor.transpose(
                        out=ps.ap[:, c4 * 128:(c4 + 1) * 128], in_=st.ap[:, c * 128:(c + 1) * 128], identity=ident.ap),
                        reads=[st, ident] if c4 == 0 else (), writes=[ps] if c4 == 0 else (), sig=(c4 == 3))
                dst = xT[:, half * 4:half * 4 + 4, i * 128:(i + 1) * 128]
                srcp = ps.ap.rearrange("p (c n) -> p c n", c=4)
                if half == 0:
                    bld.op(dve, lambda dst=dst, srcp=srcp: V.tensor_copy(out=dst, in_=srcp), reads=[ps], writes=x_b[tb][half * 4:half * 4 + 4])
                else:
                    bld.op(act, lambda dst=dst, srcp=srcp: A.copy(out=dst, in_=srcp), reads=[ps], writes=x_b[tb][half * 4:half * 4 + 4])
        setup_fence = [t.buf for t in stg.tiles]

    def wload(dst_ap, src_ap, slot):
        bld.dma(pool, dst_ap, src_ap, writes=[slot])

    def mod_piece(l, pi):
        slot = wring.next()
        w = slot.ap.rearrange("p (k n) -> p k n", k=8)
        wload(w, wmod_d[l][:, pi * 256:(pi + 1) * 256].rearrange("(k p) n -> p k n", p=128), slot)
        for j2 in range(2):
            j = pi * 2 + j2
            out_ap = mod_ps.ap[:, j * 2:(j + 1) * 2]
            for k in range(8):
                rhs = scT.ap.rearrange("p (c k) -> p k c", c=2)[:, k, :]
                bld.mm(out_ap, w[:, k, j2 * 128:(j2 + 1) * 128], rhs, start=(k == 0), stop=(k == 7),
                       reads=[slot, scT] if k == 0 else (), writes=[mod_ps] if (k == 0) else (),
                       sig=(k == 7))

    def mod_begin(l):
        nonlocal mod_ps
        mod_ps = banks[7]
        with contextlib.ExitStack() as s2:
            bst = scoped("bst", [72, 128], F32, s2)
            bld.dma(sp, bst.ap, bmod_d[l], writes=[bst])
            pb = banks[6]
            bld.op(pe, lambda: nc.tensor.transpose(out=pb.ap[:, 0:72], in_=bst.ap, identity=ident.ap[0:72, 0:72]),
                   reads=[bst, ident], writes=[pb])
            bld.op(dve, lambda: V.tensor_copy(out=bmodT.ap, in_=pb.ap[:, 0:72]), reads=[pb], writes=[bmodT])
            _fence_scope([bst])

    def mod_finish(l):
        mp = mod_ps.ap[:, 0:144].rearrange("p (j c) -> p j c", c=2)
        for cd in range(2):
            bld.op(dve, lambda cd=cd: V.tensor_tensor(out=modT.ap[:, :, cd], in0=mp[:, :, cd], in1=bmodT.ap, op=ALU.add),
                   reads=[mod_ps, bmodT], writes=[modT])
        m = modT.ap.rearrange("p (i c) k -> p i c k", c=8)
        for s in range(3):
            nw = prm.ap[:, (l * 3 + s) * 8:(l * 3 + s) * 8 + 8]
            gsc = 1.0 if s == 1 else 0.5
            for cd in range(2):
                bld.op(dve, lambda s=s, cd=cd, nw=nw: V.scalar_tensor_tensor(
                    out=drv.ap[:, 0, s, :, cd], in0=m[:, 3 * s + 1, :, cd], scalar=1.0, in1=nw, op0=ALU.add, op1=ALU.mult),
                    reads=[modT, prm], writes=[drv])
            bld.op(dve, lambda s=s: V.tensor_copy(out=drv.ap[:, 1, s], in_=m[:, 3 * s]), reads=[modT], writes=[drv])
            bld.op(dve, lambda s=s, gsc=gsc: V.tensor_scalar_mul(out=drv.ap[:, 2, s], in0=m[:, 3 * s + 2], scalar1=gsc),
                   reads=[modT], writes=[drv])

    mod_ps = None

    def rstd_op(ps, r, n, eps, cols=slice(0, 512), pr=slice(0, 128)):
        bld.op(act, lambda: A.activation(out=r.ap[pr, cols], in_=ps.ap[pr, cols], func=AF.Ln, bias=epsc.ap[pr, 0:1] if eps == EPS else None, scale=1.0 / n),
               reads=[ps, epsc], writes=[r])
        bld.op(act, lambda: A.activation(out=r.ap[pr, cols], in_=r.ap[pr, cols], func=AF.Exp, scale=-0.5), reads=[r], writes=[r])

    def prenorm(l, s):
        last_group["kind"] = None
        with contextlib.ExitStack() as s2:
            sq = Ring([scoped(f"sq{i}", [128, NCH, 512], BF16, s2) for i in range(1)])
            rs = Ring([scoped(f"rs{i}", [128, 512], F32, s2) for i in range(2)])
            tm = Ring([scoped(f"tm{i}", [128, 512], F32, s2) for i in range(3)])
            for tb in range(NTB):
                cd = cond_of(tb)
                q = sq.next()
                bld.op(act, lambda q=q, tb=tb: A.activation(out=q.ap, in_=xT[:, :, tbcols(tb)], func=AF.Square),
                       reads=x_b[tb], writes=[q])
                ps = ps_all.next()
                bld.mm_group(ps, ps.ap, [(ones_bf, q.ap[:, c, :]) for c in range(NCH)], reads=[q, cm])
                r = rs.next()
                rstd_op(ps, r, D, EPS)
                for c in range(NCH):
                    t = tm.next()
                    bld.op(dve, lambda t=t, c=c, tb=tb, r=r, cd=cd: V.scalar_tensor_tensor(
                        out=t.ap, in0=xT[:, c, tbcols(tb)], scalar=drv.ap[:, 0, s, c, cd:cd + 1], in1=r.ap,
                        op0=ALU.mult, op1=ALU.mult), reads=[x_b[tb][c], r, drv], writes=[t])
                    bld.op(act, lambda t=t, c=c, tb=tb, cd=cd: A.activation(
                        out=hT[:, c, tbcols(tb)], in_=t.ap, func=AF.Identity, bias=drv.ap[:, 1, s, c, cd:cd + 1], scale=1.0),
                        reads=[t, drv], writes=[h_b[tb]])
            _fence_scope([q for q in sq.tiles] + rs.tiles + tm.tiles)

    pending_fence = list(setup_fence)

    def _fence_scope(tiles):
        for t in tiles:
            pending_fence.append(t.buf)

    grp = {"prev": None, "cur": None}

    def _fence_for(name):
        prev = grp["prev"]
        if prev is not None and name in prev:
            return [prev[name]]
        return pending_fence

    def scoped(name, shape, dt, stack):
        t = Tile(sb(name, shape, dt, stack)[:], name)
        if grp["cur"] is not None:
            grp["cur"][name] = t.buf
        rr = {}
        for b in _fence_for(name):
            if b.w is not None:
                o, c = b.w
                rr[o] = max(rr.get(o, 0), c)
            for o, c in b.r.items():
                rr[o] = max(rr.get(o, 0), c)
        t.buf.r = rr
        return t

    def ffn(l, f, s, between=None):
        w1, w3, w2 = w1_d[l][f], w3_d[l][f], w2_d[l][f]
        with contextlib.ExitStack() as s2:
            g_t = sb("g_t", [128, 4, T], BF16, s2)
            g_tiles = [[scoped_buf(f"g{j}_{tb}") for tb in range(NTB)] for j in range(4)]
            su = Ring([scoped(f"su{i}", [128, 512], BF16, s2) for i in range(3)])
            ps_uv = Ring(banks[0:4])
            ps_o = Ring(banks[4:7] if mod_ps is not None else banks[4:8])
            sgs = [(i * 4, 4) for i in range(5)] + [(20, 2)]
            for sgi, (c0, ncn) in enumerate(sgs):
                halves = [(c0 + 2 * hh, min(2, ncn - 2 * hh)) for hh in range((ncn + 1) // 2)]
                slots = []
                for (cc, n2) in halves:
                    sa = wring.next()
                    wa = sa.ap.rearrange("p (k n) -> p k n", k=8)
                    wload(wa[:, :, 0:n2 * 128], w1[:, cc * 128:(cc + n2) * 128].rearrange("(k p) n -> p k n", p=128), sa)
                    sb_ = wring.next()
                    wb = sb_.ap.rearrange("p (k n) -> p k n", k=8)
                    wload(wb[:, :, 0:n2 * 128], w3[:, cc * 128:(cc + n2) * 128].rearrange("(k p) n -> p k n", p=128), sb_)
                    slots.append((sa, wa, sb_, wb))
                slots2 = []
                for (cc, n2) in halves:
                    sc = wring.next()
                    wc = sc.ap.rearrange("p (j n) -> p j n", j=2)
                    wload(wc[:, 0:n2, :], w2[cc * 128:(cc + n2) * 128, :].rearrange("(j p) n -> p j n", p=128), sc)
                    slots2.append((sc, wc))
                for j in range(ncn):
                    sa, wa, sb_, wb = slots[j // 2]
                    jj = j % 2
                    for tb in range(NTB):
                        pu = ps_uv.next()
                        bld.mm_group(pu, pu.ap, [(wa[:, k, jj * 128:(jj + 1) * 128], hT[:, k, tbcols(tb)]) for k in range(8)],
                                     reads=[sa, h_b[tb]])
                        pv = ps_uv.next()
                        bld.mm_group(pv, pv.ap, [(wb[:, k, jj * 128:(jj + 1) * 128], hT[:, k, tbcols(tb)]) for k in range(8)],
                                     reads=[sb_, h_b[tb]])
                        sut = su.next()
                        bld.op(act, lambda sut=sut, pu=pu: A.activation(out=sut.ap, in_=pu.ap, func=AF.Silu),
                               reads=[pu], writes=[sut])
                        bld.op(dve, lambda j=j, tb=tb, pv=pv, sut=sut: V.tensor_tensor(
                            out=g_t[:, j, tbcols(tb)], in0=pv.ap, in1=sut.ap, op=ALU.mult),
                            reads=[pv, sut], writes=[g_tiles[j][tb]])
                for tb in range(NTB):
                    cd = cond_of(tb)
                    for n in range(NCH):
                        po = ps_o.next()
                        pairs = []
                        for j in range(ncn):
                            sc, wc = slots2[j // 2]
                            pairs.append((wc[:, j % 2, n * 128:(n + 1) * 128], g_t[:, j, tbcols(tb)]))
                        bld.mm_group(po, po.ap, pairs, reads=[sl[0] for sl in slots2] + [g_tiles[j][tb] for j in range(ncn)])
                        bld.op(dve, lambda po=po, n=n, tb=tb, cd=cd: V.scalar_tensor_tensor(
                            out=xT[:, n, tbcols(tb)], in0=po.ap, scalar=drv.ap[:, 2, s, n, cd:cd + 1], in1=xT[:, n, tbcols(tb)],
                            op0=ALU.mult, op1=ALU.add), reads=[po, drv, x_b[tb][n]], writes=[x_b[tb][n]])
                if between is not None:
                    between(sgi)
            if dbg == 2 and l == 0 and f == 0:
                dbg4_d = dout("dbg4", [128, 4, 128])
                bld.dma(pool, dbg4_d[:, :, 0:64], g_t[:, :, 0:64], reads=[g_tiles[j][0] for j in range(4)])
                bld.dma(pool, dbg4_d[:, :, 64:128], g_t[:, :, 2048:2112], reads=[g_tiles[j][4] for j in range(4)])
            for j in range(4):
                for tb in range(NTB):
                    pending_fence.append(g_tiles[j][tb])
            _fence_scope(su.tiles)

    def scoped_buf(name):
        b = Buf(name)
        if grp["cur"] is not None:
            grp["cur"][name] = b
        rr = {}
        for pb in _fence_for(name):
            if pb.w is not None:
                o, c = pb.w
                rr[o] = max(rr.get(o, 0), c)
            for o, c in pb.r.items():
                rr[o] = max(rr.get(o, 0), c)
        b.r = rr
        return b

    def final():
        with contextlib.ExitStack() as s2:
            sq = scoped("fsq", [128, NCH, 512], BF16, s2)
            rs = scoped("frs", [128, 512], F32, s2)
            yt = Ring([scoped(f"fy{i}", [128, 512], F32, s2) for i in range(3)])
            ot_t = sb("fot", [128, 2, 4, D], F32, s2)
            ot = Ring([scoped_view(ot_t[:, i], f"fot{i}") for i in range(2)])
            for tb in range(NTB):
                bld.op(act, lambda tb=tb: A.activation(out=sq.ap, in_=xT[:, :, tbcols(tb)], func=AF.Square),
                       reads=x_b[tb], writes=[sq])
                ps = ps_all.next()
                bld.mm_group(ps, ps.ap, [(ones_bf, sq.ap[:, c, :]) for c in range(NCH)], reads=[sq, cm])
                rstd_op(ps, rs, D, EPS)
                o = ot.next()
                for c in range(NCH):
                    y = yt.next()
                    bld.op(dve, lambda y=y, c=c, tb=tb: V.scalar_tensor_tensor(
                        out=y.ap, in0=xT[:, c, tbcols(tb)], scalar=prm.ap[:, 96 + c:97 + c], in1=rs.ap,
                        op0=ALU.mult, op1=ALU.mult), reads=[x_b[tb][c], rs, prm], writes=[y])
                    ps2 = ps_all.next()
                    for tt in range(4):
                        bld.op(pe, lambda ps2=ps2, y=y, tt=tt: nc.tensor.transpose(
                            out=ps2.ap[:, tt * 128:(tt + 1) * 128], in_=y.ap[:, tt * 128:(tt + 1) * 128], identity=ident.ap),
                            reads=[y, ident] if tt == 0 else (), writes=[ps2] if tt == 0 else (), sig=(tt == 3))
                    dst = o.ap[:, :, c * 128:(c + 1) * 128]
                    srcp = ps2.ap.rearrange("p (t n) -> p t n", t=4)
                    if c % 2 == 0:
                        bld.op(act, lambda dst=dst, srcp=srcp: A.copy(out=dst, in_=srcp), reads=[ps2], writes=[o])
                    else:
                        bld.op(dve, lambda dst=dst, srcp=srcp: V.tensor_copy(out=dst, in_=srcp), reads=[ps2], writes=[o])
                for tt in range(4):
                    i = tb * 4 + tt
                    dst = ys_d[i * 128:(i + 1) * 128, :] if i < 16 else yp_d[(i - 16) * 128:(i - 15) * 128, :]
                    bld.dma(sp, dst, o.ap[:, tt, :], reads=[o])

    def scoped_view(ap, name):
        t = Tile(ap, name)
        t.buf = scoped_buf(name)
        return t


    def pipeline(n, stages):
        ns = len(stages)
        for step in range(n + ns - 1):
            for st in range(ns):
                i = step - st
                if 0 <= i < n:
                    stages[st](i)

    last_group = {"kind": None, "bufs": None}
    prefetched = {}
    mixst = {"OT": None, "ot_b": None, "units": [], "slot": 0}

    def nb_ranges(j, c):
        out = []
        for krl in range(2):
            kr = 2 * c + krl
            valid = []
            for qr in range(8 * j, 8 * j + 8):
                rs_ = min(max(qr - 4, 0), 32 - 8)
                if rs_ <= kr < rs_ + 8:
                    valid.append(qr)
            if not valid:
                out.append(None)
                continue
            qa, qb = valid[0], valid[-1]
            assert valid == list(range(qa, qb + 1))
            i0 = 7 - kr + qa
            assert 0 <= i0 and i0 + (qb - qa) <= 14
            out.append(((qa - 8 * j) * 64, (qb - 8 * j + 1) * 64, i0))
        return out

    def group_cols(kind, gi):
        if kind == "A":
            kv = gi // 2
            return (slice(128 * gi, 128 * gi + 128), slice(512 + 64 * kv, 512 + 64 * kv + 64),
                    slice(640 + 64 * kv, 640 + 64 * kv + 64), 128 * gi, 64)
        if kind == "B":
            return (slice(768 + 128 * gi, 768 + 128 * gi + 128), slice(1280 + 128 * gi, 1280 + 128 * gi + 128),
                    slice(1792 + 128 * gi, 1792 + 128 * gi + 128), 512 + 128 * gi, 128)
        return (slice(128 * gi, 128 * gi + 128), slice(1024 + 128 * gi, 1024 + 128 * gi + 128),
                slice(2048 + 128 * gi, 2048 + 128 * gi + 128), 128 * gi, 128)

    def load_group_weights(kind, l, gi):
        ei = l // 2
        win = winab_d[ei] if kind != "C" else winc_d[ei]
        wout = woutab_d[ei] if kind != "C" else woutc_d[ei]
        qcols, kcols, vcols, orow, nv = group_cols(kind, gi)
        s1 = wring.next()
        w1v = s1.ap.rearrange("p (k n) -> p k n", k=8)
        wload(w1v[:, :, 0:128], win[:, qcols].rearrange("(k p) n -> p k n", p=128), s1)
        if kind == "A":
            wload(w1v[:, :, 128:192], win[:, kcols].rearrange("(k p) n -> p k n", p=128), s1)
            wload(w1v[:, :, 192:256], win[:, kcols].rearrange("(k p) n -> p k n", p=128), s1)
        else:
            wload(w1v[:, :, 128:256], win[:, kcols].rearrange("(k p) n -> p k n", p=128), s1)
        s2w = wring.next()
        wo = s2w.ap[:, 0:1024]
        wvv = s2w.ap[:, 1024:2048].rearrange("p (k n) -> p k n", k=8)
        wload(wo, wout[orow:orow + 128, :], s2w)
        wload(wvv[:, :, 0:nv], win[:, vcols].rearrange("(k p) n -> p k n", p=128), s2w)
        return (s1, w1v, s2w, wo, wvv)

    def run_group(kind, l, gi, nxt=None):
        even = (kind != "C")
        ei = l // 2
        cd_s = 1
        win = winab_d[ei] if even else winc_d[ei]
        wout = woutab_d[ei] if even else woutc_d[ei]
        if kind == "A":
            kv = gi // 2
            qcols = slice(128 * gi, 128 * gi + 128)
            kcols = slice(512 + 64 * kv, 512 + 64 * kv + 64)
            vcols = slice(640 + 64 * kv, 640 + 64 * kv + 64)
            orow = 128 * gi
            nv = 64
        elif kind == "B":
            qcols = slice(768 + 128 * gi, 768 + 128 * gi + 128)
            kcols = slice(1280 + 128 * gi, 1280 + 128 * gi + 128)
            vcols = slice(1792 + 128 * gi, 1792 + 128 * gi + 128)
            orow = 512 + 128 * gi
            nv = 128
        else:
            qcols = slice(128 * gi, 128 * gi + 128)
            kcols = slice(1024 + 128 * gi, 1024 + 128 * gi + 128)
            vcols = slice(2048 + 128 * gi, 2048 + 128 * gi + 128)
            orow = 128 * gi
            nv = 128
        rope = kind in ("A", "C")
        grp["prev"] = last_group["bufs"] if (last_group["kind"] == kind and not os.environ.get("K_NOPF")) else None
        grp["cur"] = {}
        with contextlib.ExitStack() as s2:
            if kind == "B":
                TF = scoped("TF", [128, 2, 960], BF16, s2)
                with contextlib.ExitStack() as s3:
                    tfst = scoped(f"tfst_{l}_{gi}", [128, 960], F32, s3)
                    cmx = scoped(f"cmx_{l}_{gi}", [128, 960], F32, s3)
                    bld.dma(sp, cmx.ap, colmaskx_d, writes=[cmx])
                    for hh in range(2):
                        bld.dma(sp, tfst.ap, rpbx_d[ei, 2 * gi + hh], writes=[tfst])
                        bld.op(act, lambda: A.activation(out=tfst.ap, in_=tfst.ap, func=AF.Exp), reads=[tfst], writes=[tfst])
                        bld.op(dve, lambda hh=hh: V.tensor_tensor(out=TF.ap[:, hh, :], in0=tfst.ap, in1=cmx.ap, op=ALU.mult),
                               reads=[tfst, cmx], writes=[TF])
                    _fence_scope([tfst, cmx])

            Qz_t = sb("Qz", [128, 2, T], BF16, s2)
            OT_t = mixst["OT"][mixst["slot"]]
            KT_t = sb("KT", [128, T], BF16, s2)
            ot_b = mixst["ot_b"][mixst["slot"]]
            qo_b = [[scoped_buf(f"qo{hh}_{q}") for q in range(6)] for hh in range(2)]
            kt_b = [scoped_buf(f"kt{tb}") for tb in range(NTB)]
            KTc = scoped("KTc", [128, 512], BF16, s2)
            bld.op(dve, lambda: V.memset(Qz_t[64:128, 0, :], 0.0), writes=qo_b[0])
            bld.op(dve, lambda: V.memset(Qz_t[0:64, 1, :], 0.0), writes=qo_b[1])
            nh = 2 if kind == "B" else 1
            VV_t = sb("VV", [128, NTT, nh, 128], BF16, s2)
            vv_b = [scoped_buf(f"vv{tt}") for tt in range(NTT)]
            VVc = scoped("VVc", [128, 4, nh, 128], BF16, s2)
            Fr = Ring([scoped(f"F{i}", [128, 512], F32, s2) for i in range(5 if kind == "B" else 6)])
            Hr = Ring([scoped(f"H{i}", [128, 512], BF16, s2) for i in range(5 if kind == "C" else 6)])
            if rope:
                csr = Ring([scoped(f"cs{i}", [128, 2, 512], F32, s2) for i in range(2)])
                cs_tb = {}
                for tb_ in range(4):
                    cst = csr.tiles[tb_ % 2]
                    cs_tb[tb_] = cst
            wts = prefetched.pop((kind, l, gi), None) or load_group_weights(kind, l, gi)
            s1, w1v, s2w, wo, wvv = wts
            kst = Fr.next()
            kstv = kst.ap.rearrange("p (t c) -> p t c", t=4)
            if kind == "A":
                src = cak_d[ei][:, 64 * kv:64 * kv + 64].rearrange("(t p) c -> p t c", p=128)
                bld.dma(sp, kstv[:, :, 0:64], src, writes=[kst])
                bld.dma(sp, kstv[:, :, 64:128], src, writes=[kst])
                vsrc = cav_d[ei][:, 64 * kv:64 * kv + 64].rearrange("(t p) c -> p t c", p=128)
                bld.dma(pool, VVc.ap[:, :, 0, 0:64], vsrc, writes=[VVc])
            elif kind == "B":
                bld.dma(sp, kstv, cbk_d[ei][:, 128 * gi:128 * gi + 128].rearrange("(t p) c -> p t c", p=128), writes=[kst])
                for h_ in range(2):
                    vsrc = cbv_d[ei][:, 128 * gi + 64 * h_:128 * gi + 64 * h_ + 64].rearrange("(t p) c -> p t c", p=128)
                    bld.dma(pool, VVc.ap[:, :, h_, 0:64], vsrc, writes=[VVc])
            else:
                bld.dma(sp, kstv, cck_d[ei][:, 128 * gi:128 * gi + 128].rearrange("(t p) c -> p t c", p=128), writes=[kst])
                vsrc = ccv_d[ei][:, 128 * gi:128 * gi + 128].rearrange("(t p) c -> p t c", p=128)
                bld.dma(pool, VVc.ap[:, :, 0, :], vsrc, writes=[VVc])
            if kind != "C":
                bld.op(dve, lambda: V.memset(VVc.ap[:, :, :, 64:128], 1.0), writes=[VVc])
                bld.op(dve, lambda: V.memset(VV_t[:, :, :, 64:128], 1.0), writes=vv_b)
            pk = banks[7]
            for t4 in range(4):
                bld.op(pe, lambda t4=t4: nc.tensor.transpose(out=pk.ap[:, t4 * 128:(t4 + 1) * 128], in_=kstv[:, t4, :], identity=ident.ap),
                       reads=[kst, ident] if t4 == 0 else (), writes=[pk] if t4 == 0 else (), sig=(t4 == 3))
            bld.op(dve, lambda: V.tensor_copy(out=KTc.ap, in_=pk.ap), reads=[pk], writes=[KTc])

            ps_pr = Ring(banks[0:4])
            ps_x = Ring(banks[4:7])
            tiles = [(w, tb) for tb in range(NTB) for w in (0, 1)]
            cs_loaded = set()
            st = {}
            nrm_col = 104 + ei * 2

            def final_write(w, tb, mk, reads):
                if w == 1:
                    if kind == "B":
                        bld.op(act, lambda: A.copy(out=KT_t[:, tbcols(tb)], in_=reads[0].ap), reads=reads, writes=dst_bufs(w, tb))
                    else:
                        bld.op(dve, lambda: mk(KT_t[:, tbcols(tb)], slice(0, 128)), reads=reads, writes=dst_bufs(w, tb))
                else:
                    for hh_ in range(2):
                        pr_ = slice(hh_ * 64, hh_ * 64 + 64)
                        wb_ = [qo_b[hh_][tb]] if tb < 4 else [qo_b[hh_][4], qo_b[hh_][5]]
                        bld.op(dve, lambda: mk(Qz_t[pr_, hh_, tbcols(tb)], pr_), reads=reads, writes=wb_)

            def dst_bufs(w, tb):
                if w == 1:
                    return [kt_b[tb]]
                if tb < 4:
                    return [qo_b[0][tb], qo_b[1][tb]]
                return [qo_b[0][4], qo_b[1][4], qo_b[0][5], qo_b[1][5]]

            def st0(i):
                w, tb = tiles[i]
                ps = ps_pr.next()
                bld.mm_group(ps, ps.ap, [(w1v[:, k, w * 128:(w + 1) * 128], hT[:, k, tbcols(tb)]) for k in range(8)],
                             reads=[s1, h_b[tb]])
                d = {"ps": ps}
                st[i] = d
                if kind == "B" or (kind == "C" and tb == 4):
                    final_write(w, tb, lambda o_, pr_: V.tensor_copy(out=o_, in_=ps.ap[pr_, :]), [ps])
                elif kind == "A":
                    sq = Hr.next()
                    bld.op(act, lambda: A.activation(out=sq.ap, in_=ps.ap, func=AF.Square), reads=[ps], writes=[sq])
                    d["sq"] = sq
                else:
                    qb_ = Hr.next()
                    bld.op(act, lambda: A.copy(out=qb_.ap, in_=ps.ap), reads=[ps], writes=[qb_])
                    d["qb"] = qb_

            def st1(i):
                w, tb = tiles[i]
                d = st[i]
                ps = d["ps"]
                if kind == "A":
                    pss = ps_x.next()
                    bld.mm(pss.ap, bd_bf, d["sq"].ap, True, True, reads=[d["sq"], cm], writes=[pss])
                    r = Fr.next()
                    rstd_op(pss, r, 64, EPS)
                    wcol = prm.ap[:, nrm_col + w:nrm_col + w + 1]
                    if tb == 4:
                        final_write(w, tb, lambda o_, pr_: V.scalar_tensor_tensor(out=o_, in0=ps.ap[pr_, :], scalar=wcol[pr_, :], in1=r.ap[pr_, :],
                                                                                  op0=ALU.mult, op1=ALU.mult), [ps, r, prm])
                    else:
                        qn = Fr.next()
                        bld.op(dve, lambda: V.scalar_tensor_tensor(out=qn.ap, in0=ps.ap, scalar=wcol, in1=r.ap,
                                                                   op0=ALU.mult, op1=ALU.mult),
                               reads=[ps, r, prm], writes=[qn])
                        qb_ = Hr.next()
                        bld.op(act, lambda: A.copy(out=qb_.ap, in_=qn.ap), reads=[qn], writes=[qb_])
                        d["qn"] = qn
                        d["qb"] = qb_
                if rope and tb < 4:
                    psr = ps_x.next()
                    bld.mm(psr.ap, rot_bf, d["qb"].ap, True, True, reads=[d["qb"], cm], writes=[psr])
                    d["psr"] = psr

            def st2(i):
                w, tb = tiles[i]
                d = st[i]
                if not (rope and tb < 4):
                    return
                cs = cs_tb[tb]
                if (w, tb) not in cs_loaded:
                    if not any(k_[1] == tb for k_ in cs_loaded):
                        bld.dma(sp, cs.ap, cossin_d[:, :, tbcols(tb)].rearrange("m p t -> p m t"), writes=[cs])
                    cs_loaded.add((w, tb))
                cosv = cs.ap[:, 0, :]
                sinv = cs.ap[:, 1, :]
                t2 = Fr.next()
                bld.op(dve, lambda: V.tensor_tensor(out=t2.ap, in0=d["psr"].ap, in1=sinv, op=ALU.mult),
                       reads=[d["psr"], cs], writes=[t2])
                if kind == "A":
                    t1 = d["qn"]
                    bld.op(dve, lambda: V.tensor_tensor(out=t1.ap, in0=t1.ap, in1=cosv, op=ALU.mult), reads=[t1, cs], writes=[t1])
                else:
                    t1 = Fr.next()
                    bld.op(dve, lambda: V.tensor_tensor(out=t1.ap, in0=d["ps"].ap, in1=cosv, op=ALU.mult),
                           reads=[d["ps"], cs], writes=[t1])
                final_write(w, tb, lambda o_, pr_: V.tensor_tensor(out=o_, in0=t1.ap[pr_, :], in1=t2.ap[pr_, :], op=ALU.add), [t1, t2])

            if stage >= 1:
                pipeline(len(tiles), [st0, st1, st2])

            own_out = (kind != "A") or (gi % 2 == 0)
            if kind == "A":
                kd, vd, oc = nak_d, nav_d, slice(64 * kv, 64 * kv + 64)
            elif kind == "B":
                kd, vd, oc = nbk_d, nbv_d, slice(128 * gi, 128 * gi + 128)
            else:
                kd, vd, oc = nck_d, ncv_d, slice(128 * gi, 128 * gi + 128)
            ps_v = Ring(banks[0:4])
            for tt in (range(NTT) if stage >= 2 else ()):
                ps = ps_v.next()
                bld.mm_group(ps, ps.ap[:, 0:nv], [(hT[:, k, tt * 128:(tt + 1) * 128], wvv[:, k, 0:nv]) for k in range(8)],
                             reads=[s2w, h_b[tt // 4]])
                if kind == "B":
                    bld.op(dve, lambda ps=ps, tt=tt: V.tensor_copy(out=VV_t[:, tt, :, 0:64], in_=ps.ap[:, 0:128].rearrange("p (h c) -> p h c", h=2)),
                           reads=[ps], writes=[vv_b[tt]])
                else:
                    bld.op(dve, lambda ps=ps, tt=tt: V.tensor_copy(out=VV_t[:, tt, 0, 0:nv], in_=ps.ap[:, 0:nv]), reads=[ps], writes=[vv_b[tt]])
                if tt >= 16 and own_out and stage >= 2.3:
                    sq_, r0 = (tt - 16) // 2, ((tt - 16) % 2) * 128
                    f = Fr.next()
                    if os.environ.get("K_DVECOPY"):
                        bld.op(dve, lambda ps=ps, f=f: V.tensor_copy(out=f.ap[:, 0:nv], in_=ps.ap[:, 0:nv]), reads=[ps], writes=[f])
                    else:
                        bld.op(act, lambda ps=ps, f=f: A.copy(out=f.ap[:, 0:nv], in_=ps.ap[:, 0:nv]), reads=[ps], writes=[f])
                    if not os.environ.get("K_NODMA"):
                        bld.dma(sp, vd[sq_, ei, r0:r0 + 128, oc], f.ap[:, 0:nv], reads=[f])
                    if stage < 2.6:
                        continue
                    psk = ps_v.next()
                    bld.mm_group(psk, psk.ap[:, 0:nv], [(hT[:, k, tt * 128:(tt + 1) * 128], w1v[:, k, 128:128 + nv]) for k in range(8)],
                                 reads=[s1, h_b[4]])
                    fk = Fr.next()
                    if kind == "A" and stage >= 2.9:
                        junk = Fr.next()
                        ssk = Fr.next()
                        bld.op(act, lambda: A.activation(out=junk.ap[:, 0:64], in_=psk.ap[:, 0:64], func=AF.Square), reads=[psk], writes=[junk])
                        bld.op(dve, lambda: V.reduce_sum(out=ssk.ap[:, 0:1], in_=junk.ap[:, 0:64], axis=mybir.AxisListType.X),
                               reads=[junk], writes=[ssk])
                        bld.op(act, lambda: A.activation(out=ssk.ap[:, 0:1], in_=ssk.ap[:, 0:1], func=AF.Ln, bias=epsc.ap[:, 0:1], scale=1.0 / 64),
                               reads=[ssk, epsc], writes=[ssk])
                        bld.op(act, lambda: A.activation(out=ssk.ap[:, 0:1], in_=ssk.ap[:, 0:1], func=AF.Exp, scale=-0.5), reads=[ssk], writes=[ssk])
                        bld.op(dve, lambda: V.scalar_tensor_tensor(out=fk.ap[:, 0:64], in0=psk.ap[:, 0:64], scalar=ssk.ap[:, 0:1], in1=kw8.ap,
                                                                   op0=ALU.mult, op1=ALU.mult), reads=[psk, ssk, kw8], writes=[fk])
                    else:
                        bld.op(act, lambda: A.copy(out=fk.ap[:, 0:nv], in_=psk.ap[:, 0:nv]), reads=[psk], writes=[fk])
                    bld.dma(sp, kd[sq_, ei, r0:r0 + 128, oc], fk.ap[:, 0:nv], reads=[fk])
                    if dbg == 3 and kind == "A" and tt == 16 and gi == 0:
                        dbg5_d = dout("dbg5", [128, 4, 64])
                        bld.dma(sp, dbg5_d[:, 0, :], junk.ap[:, 0:64], reads=[junk])
                        bld.dma(sp, dbg5_d[:, 1, :], ssk.ap[:, 0:64], reads=[ssk])
                        bld.dma(sp, dbg5_d[:, 2, :], fk.ap[:, 0:64], reads=[fk])
                        bld.dma(sp, dbg5_d[:, 3, :], kw8.ap[:, 0:64], reads=[kw8])

            if nxt is not None and nxt not in prefetched:
                prefetched[nxt] = load_group_weights(*nxt)

            S_ring = Ring(banks[0:4])
            acc_ring = Ring(banks[4:6])
            qblocks = [(q, slice(q * 512, (q + 1) * 512), 512) for q in range(4)] + \
                      [(4 + s_, slice(TS + 256 * s_, TS + 256 * s_ + 256), 256) for s_ in range(2)]

            def key_chunks(q):
                res = []
                if q < 4:
                    res += [("cache", c) for c in range(4)]
                    if kind == "B":
                        rows = set()
                        for qr in range(8 * q, 8 * q + 8):
                            rs_ = min(max(qr - 4, 0), 24)
                            rows.update(range(rs_, rs_ + 8))
                        res += [("nb", c) for c in range(min(rows) // 2, max(rows) // 2 + 1)]
                    else:
                        res += [("own", c) for c in range(16)]
                else:
                    s_ = q - 4
                    res += [("own", 16 + 2 * s_), ("own", 17 + 2 * s_)]
                return res

            def kT_of(typ, c, pr):
                if typ == "cache":
                    return KTc.ap[pr, c * 128:(c + 1) * 128], [KTc]
                return KT_t[pr, c * 128:(c + 1) * 128], [kt_b[c // 4]]

            def v_of(typ, c, hh, pr=slice(0, 128)):
                if typ == "cache":
                    return VVc.ap[pr, c, hh, :], [VVc]
                return VV_t[pr, c, hh, :], [vv_b[c]]

            items = []
            if stage < 3:
                pass
            elif kind in ("A", "B"):
                for hh in range(2):
                    pr = slice(hh * 64, hh * 64 + 64)
                    vh = hh if kind == "B" else 0
                    for (q, qc, nq) in qblocks:
                        chunks = key_chunks(q)
                        blk = {"acc": None}
                        for ci, (typ, c) in enumerate(chunks):
                            first, last = ci == 0, ci == len(chunks) - 1

                            def s_stage(d, typ=typ, c=c, pr=pr, q=q, qc=qc, nq=nq, hh=hh):
                                kT, kr_ = kT_of(typ, c, slice(0, 128))
                                if typ == "nb":
                                    rng = nb_ranges(q, c)
                                    act_r = [r_ for r_ in rng if r_ is not None]
                                    cu0, cu1 = min(r_[0] for r_ in act_r), max(r_[1] for r_ in act_r)
                                else:
                                    rng = None
                                    cu0, cu1 = 0, nq
                                S = S_ring.next()
                                qs = Qz_t[:, hh, qc.start + cu0:qc.start + cu1]
                                bld.mm(S.ap[:, cu0:cu1], kT, qs, True, True, reads=kr_ + [qo_b[hh][q]], writes=[S])
                                P = Hr.next()
                                bld.op(act, lambda: A.activation(out=P.ap[:, cu0:cu1], in_=S.ap[:, cu0:cu1], func=AF.Exp, scale=SCALE),
                                       reads=[S], writes=[P])
                                if typ == "nb":
                                    for krl in range(2):
                                        kp = slice(krl * 64, krl * 64 + 64)
                                        if rng[krl] is None:
                                            zr = [(cu0, cu1)]
                                        else:
                                            a0, a1, i0 = rng[krl]
                                            bld.op(dve, lambda: V.tensor_tensor(
                                                out=P.ap[kp, a0:a1], in0=P.ap[kp, a0:a1], in1=TF.ap[kp, hh, i0 * 64:i0 * 64 + (a1 - a0)], op=ALU.mult),
                                                reads=[P, TF], writes=[P])
                                            zr = [(cu0, a0), (a1, cu1)]
                                        for (z0, z1) in zr:
                                            if z1 > z0:
                                                bld.op(pool, lambda: G.memset(P.ap[kp, z0:z1], 0.0), writes=[P])
                                d["P"], d["cu"] = P, (cu0, cu1)

                            def pv_stage(d, typ=typ, c=c, vh=vh, first=first, last=last, blk=blk, nq=nq):
                                if first:
                                    blk["acc"] = acc_ring.next()
                                acc = blk["acc"]
                                P = d["P"]
                                cu0, cu1 = d["cu"]
                                vl, vr_ = v_of(typ, c, vh)
                                kw = {"skip_group_check": True} if kind == "B" else {}
                                bld.mm(acc.ap[:, cu0:cu1], vl, P.ap[:, cu0:cu1], first, last, reads=vr_ + [P], writes=[acc], **kw)

                            fin = None
                            if last:
                                def fin(blk=blk, nq=nq, qc=qc, pr=pr, hh=hh, q=q):
                                    acc = blk["acc"]
                                    rc = Fr.next()
                                    bld.op(dve, lambda: V.reciprocal(out=rc.ap[0:64, 0:nq], in_=acc.ap[64:128, 0:nq]), reads=[acc], writes=[rc])
                                    bld.op(dve, lambda: V.tensor_tensor(out=OT_t[pr, qc], in0=acc.ap[0:64, 0:nq], in1=rc.ap[0:64, 0:nq], op=ALU.mult),
                                           reads=[acc, rc], writes=[ot_b[q]])
                            items.append((s_stage, pv_stage, fin))
                LA = 2
            else:
                O1, O2, D1, D2 = banks[4], banks[5], banks[6], banks[7]
                lo, hi = slice(0, 64), slice(64, 128)
                sqr = Ring([scoped(f"sqc{i}", [128, 512], BF16, s2) for i in range(2)])
                for (q, qc, nq) in qblocks:
                    chunks = key_chunks(q)
                    for ci, (typ, c) in enumerate(chunks):
                        first, last = ci == 0, ci == len(chunks) - 1

                        def s_stage(d, typ=typ, c=c, q=q, qc=qc, nq=nq):
                            d["P"] = []
                            for mj in range(2):
                                kT, kr_ = kT_of(typ, c, slice(0, 128))
                                S = S_ring.next()
                                bld.mm(S.ap[:, 0:nq], kT, Qz_t[:, mj, qc], True, True, reads=kr_ + [qo_b[mj][q]], writes=[S])
                                P = Hr.next()
                                bld.op(act, lambda: A.activation(out=P.ap[:, 0:nq], in_=S.ap[:, 0:nq], func=AF.Exp, scale=SCALE),
                                       reads=[S], writes=[P])
                                d["P"].append(P)

                        def pv_stage(d, typ=typ, c=c, first=first, last=last, nq=nq):
                            vl, vr_ = v_of(typ, c, 0)
                            for (P, Oa, Da) in ((d["P"][0], O1, D1), (d["P"][1], O2, D2)):
                                bld.mm(Oa.ap[:, 0:nq], vl, P.ap[:, 0:nq], first, last, reads=vr_ + [P], writes=[Oa])
                                bld.mm(Da.ap[:, 0:nq], ones_bf, P.ap[:, 0:nq], first, last, reads=[cm, P], writes=[Da])

                        fin = None
                        if last:
                            def fin(nq=nq, qc=qc, q=q):
                                r1 = Fr.next()
                                r2 = Fr.next()
                                bld.op(dve, lambda: V.reciprocal(out=r1.ap[:, 0:nq], in_=D1.ap[:, 0:nq]), reads=[D1], writes=[r1])
                                bld.op(dve, lambda: V.tensor_tensor(out=r1.ap[:, 0:nq], in0=O1.ap[:, 0:nq], in1=r1.ap[:, 0:nq], op=ALU.mult),
                                       reads=[O1, r1], writes=[r1])
                                bld.op(dve, lambda: V.reciprocal(out=r2.ap[:, 0:nq], in_=D2.ap[:, 0:nq]), reads=[D2], writes=[r2])
                                bld.op(dve, lambda: V.tensor_tensor(out=r2.ap[:, 0:nq], in0=O2.ap[:, 0:nq], in1=r2.ap[:, 0:nq], op=ALU.mult),
                                       reads=[O2, r2], writes=[r2])
                                bld.op(dve, lambda: V.scalar_tensor_tensor(out=r1.ap[:, 0:nq], in0=r2.ap[:, 0:nq], scalar=lamt.ap[:, 0:1], in1=r1.ap[:, 0:nq],
                                                                           op0=ALU.mult, op1=ALU.add), reads=[r1, r2, lamt], writes=[r1])
                                sqc = sqr.next()
                                bld.op(act, lambda: A.activation(out=sqc.ap[:, 0:nq], in_=r1.ap[:, 0:nq], func=AF.Square), reads=[r1], writes=[sqc])

                                def fin_b():
                                    pss = S_ring.next()
                                    bld.mm(pss.ap[:, 0:nq], ones_bf, sqc.ap[:, 0:nq], True, True, reads=[sqc, cm], writes=[pss])
                                    rstd_op(pss, r2, 128, EPS, cols=slice(0, nq))
                                    bld.op(dve, lambda: V.scalar_tensor_tensor(out=OT_t[:, qc], in0=r1.ap[:, 0:nq], scalar=lamt.ap[:, 1:2], in1=r2.ap[:, 0:nq],
                                                                               op0=ALU.mult, op1=ALU.mult), reads=[r1, r2, lamt], writes=[ot_b[q]])
                                return fin_b
                        items.append((s_stage, pv_stage, fin))
                LA = 1
            if items:
                ds_ = [dict() for _ in items]
                n_it = len(items)
                deferred = []
                DEF = 8
                op_ring = S_ring if kind == "C" else Ring(banks[6:8])
                n_prev = len(mixst["units"])
                every = max(1, (n_it - 4) // max(1, n_prev)) if n_prev else 1
                for step in range(n_it + LA):
                    if mixst["units"] and n_prev and step % every == 0 and step > 2:
                        mixst["units"].pop(0)(op_ring.next())
                    if step < n_it:
                        items[step][0](ds_[step])
                    while deferred and deferred[0][0] <= step:
                        deferred.pop(0)[1]()
                    j_ = step - LA
                    if j_ >= 0:
                        items[j_][1](ds_[j_])
                        if items[j_][2] is not None:
                            while deferred:
                                deferred.pop(0)[1]()
                            later = items[j_][2]()
                            if later is not None:
                                deferred.append((step + DEF, later))
                for _, later in deferred:
                    later()
                if kind == "C":
                    _fence_scope(sqr.tiles)

            if mixst["units"]:
                ps_o = Ring(banks[0:8])
                while mixst["units"]:
                    mixst["units"].pop(0)(ps_o.next())

            units = []
            for tb in (range(NTB) if stage >= 4 else ()):
                cd = cond_of(tb)
                qbufs = [ot_b[tb]] if tb < 4 else [ot_b[4], ot_b[5]]
                for n in range(NCH):
                    def unit(po, tb=tb, n=n, cd=cd, qbufs=qbufs, wo=wo, s2w=s2w, OT_t=OT_t):
                        bld.mm(po.ap, wo[:, n * 128:(n + 1) * 128], OT_t[:, tbcols(tb)], True, True, reads=[s2w] + qbufs, writes=[po])
                        bld.op(dve, lambda: V.scalar_tensor_tensor(
                            out=xT[:, n, tbcols(tb)], in0=po.ap, scalar=drv.ap[:, 2, 1, n, cd:cd + 1], in1=xT[:, n, tbcols(tb)],
                            op0=ALU.mult, op1=ALU.add), reads=[po, drv, x_b[tb][n]], writes=[x_b[tb][n]])
                    units.append(unit)
            mixst["units"].extend(units)
            if nxt is None:
                ps_o = Ring(banks[0:8])
                while mixst["units"]:
                    mixst["units"].pop(0)(ps_o.next())

            for lst in (qo_b[0], qo_b[1], kt_b, vv_b):
                pending_fence.extend(lst)
            last_group["kind"] = kind
            last_group["bufs"] = grp["cur"]
            grp["cur"] = None
            grp["prev"] = None
            _fence_scope([KTc, VVc] + Fr.tiles + Hr.tiles)
            if rope:
                _fence_scope(csr.tiles)
            if kind == "B":
                _fence_scope([TF])

    def mixer(l):
        ei = l // 2
        if l % 2 == 0:
            seq = [("A", l, gi) for gi in range(4)] + [("B", l, gi) for gi in range(4)]
        else:
            seq = [("C", l, gi) for gi in range(8)]
        if groups is not None:
            seq = [g_ for g_ in seq if (g_[0], g_[2]) in groups]
        if seq:
            prefetched[seq[0]] = load_group_weights(*seq[0])
        prenorm(l, 1)
        mix_es = contextlib.ExitStack()
        mixst["OT"] = [sb(f"OTp{i}", [128, T], BF16, mix_es) for i in range(2)]
        mixst["ot_b"] = [[scoped_buf(f"otp{i}_{q}") for q in range(6)] for i in range(2)]
        mixst["units"] = []
        if l % 2 == 0:
            bld.dma(sp, kw8.ap, kwbc_d[ei], writes=[kw8])
            for i_, g_ in enumerate(seq):
                mixst["slot"] = i_ % 2
                run_group(g_[0], l, g_[2], nxt=seq[i_ + 1] if i_ + 1 < len(seq) else None)
        else:
            with contextlib.ExitStack() as s2:
                lt = scoped("lt", [128, 256], F32, s2)
                lp = scoped("lp", [128, 128], F32, s2)
                ls = scoped("ls", [128, 2], F32, s2)
                bld.dma(sp, lt.ap, lam_d[ei], writes=[lt])
                ltv = lt.ap.rearrange("p (a d) -> p a d", a=4)
                bld.op(dve, lambda: V.tensor_tensor(out=lp.ap[:, 0:64], in0=ltv[:, 0, :], in1=ltv[:, 1, :], op=ALU.mult), reads=[lt], writes=[lp])
                bld.op(dve, lambda: V.tensor_tensor(out=lp.ap[:, 64:128], in0=ltv[:, 2, :], in1=ltv[:, 3, :], op=ALU.mult), reads=[lt], writes=[lp])
                bld.op(dve, lambda: V.reduce_sum(out=ls.ap, in_=lp.ap.rearrange("p (a d) -> p a d", a=2), axis=mybir.AxisListType.X),
                       reads=[lp], writes=[ls])
                bld.op(act, lambda: A.activation(out=ls.ap, in_=ls.ap, func=AF.Exp), reads=[ls], writes=[ls])
                bld.op(dve, lambda: V.tensor_tensor(out=lamt.ap[:, 0:1], in0=ls.ap[:, 1:2], in1=ls.ap[:, 0:1], op=ALU.subtract),
                       reads=[ls], writes=[lamt])
                bld.op(dve, lambda: V.tensor_scalar_add(out=lamt.ap[:, 0:1], in0=lamt.ap[:, 0:1], scalar1=-LAM_INIT[l]), reads=[lamt], writes=[lamt])
                bld.op(dve, lambda: V.tensor_scalar_mul(out=lamt.ap[:, 1:2], in0=prm.ap[:, 108 + ei:109 + ei], scalar1=1.0 - LAM_INIT[l]),
                       reads=[prm], writes=[lamt])
                _fence_scope([lt, lp, ls])
            for i_, g_ in enumerate(seq):
                mixst["slot"] = i_ % 2
                run_group(g_[0], l, g_[2], nxt=seq[i_ + 1] if i_ + 1 < len(seq) else None)
        for lst in mixst["ot_b"]:
            pending_fence.extend(lst)
        mix_es.close()
        last_group["kind"] = None

    for l in range(NL):
        if l == 0 or not do_ffn:
            mod_begin(l)
            for pi in range(36):
                mod_piece(l, pi)
        mod_finish(l)
        mod_ps = None
        if do_ffn:
            prenorm(l, 0)
            if dbg == 2 and l == 0:
                dbg2_d = dout("dbg2", [128, 8, 128])
                bld.dma(pool, dbg2_d[:, :, 0:64], hT[:, :, 0:64], reads=[h_b[0]])
                bld.dma(pool, dbg2_d[:, :, 64:128], hT[:, :, 2048:2112], reads=[h_b[4]])
            ffn(l, 0, 0)
            if dbg == 2 and l == 0:
                dbg3_d = dout("dbg3", [128, 8, 160])
                for tb_ in range(5):
                    bld.dma(sp, dbg3_d[:, :, tb_ * 32:(tb_ + 1) * 32], xT[:, :, tb_ * 512 + 100:tb_ * 512 + 132], reads=x_b[tb_])
        if do_mixer:
            mixer(l)
        if do_ffn:
            prenorm(l, 2)
            if l + 1 < NL:
                mod_begin(l + 1)
                ffn(l, 1, 2, between=lambda sgi, l=l: [mod_piece(l + 1, pi) for pi in range(sgi * 6, sgi * 6 + 6)])
            else:
                ffn(l, 1, 2)

    if dbg:
        dbg_d = dout("dbg", [128, 144 + 144])
        bld.dma(sp, dbg_d[:, 0:144], modT.ap.rearrange("p j c -> p (j c)"), reads=[modT])
        bld.dma(sp, dbg_d[:, 144:288], drv.ap.rearrange("p a s c k -> p (a s c k)"), reads=[drv])
    final()
    bld.finish()
    return nc, bld


_PROG = {}


def _get_prog(**kw):
    key = tuple(sorted(kw.items()))
    if key not in _PROG:
        _PROG[key] = build_program(**kw)
    return _PROG[key]


def make_in_maps(inputs, cores, NL=DEPTH):
    f = lambda a: np.ascontiguousarray(np.asarray(a, dtype=np.float32))
    cossin, ident, cmats, colmask = _const_tables()
    n_even = (NL + 1) // 2
    n_odd = NL // 2
    shared = {
        "b_mod": f(inputs["b_mod"]).reshape(DEPTH, 72, 128),
        "norm_w": f(inputs["norm_w"]).reshape(96, 128),
        "qk_norm": f(np.stack([np.concatenate([inputs["a_q_norm"][e], inputs["a_q_norm"][e]]) if w == 0 else
                               np.concatenate([inputs["a_k_norm"][e], inputs["a_k_norm"][e]])
                               for e in range(2) for w in range(2)], 0)),
        "rpbx": _expand_rpb(np.asarray(inputs["b_rpb"], np.float32)),
        "c_lambda": f(np.broadcast_to(np.asarray(inputs["c_lambda"], np.float32).reshape(2, 1, 256), (2, 128, 256))),
        "kw_bc": f(np.broadcast_to(np.asarray(inputs["a_k_norm"], np.float32).reshape(2, 1, 64), (2, 128, 64))),
        "c_subln": f(inputs["c_subln"]),
        "final_norm": f(inputs["final_norm"]).reshape(8, 128),
        "cossin": cossin, "ident": ident, "cmats": cmats, "colmask": colmask,
        "colmaskx": np.ascontiguousarray(np.tile(colmask, (1, 15))),
    }
    for l in range(NL):
        shared[f"w_mod{l}"] = f(inputs["w_mod"][l])
        for ff in range(2):
            shared[f"w1_{l}_{ff}"] = f(inputs["ffn_w1"][l, ff])
            shared[f"w3_{l}_{ff}"] = f(inputs["ffn_w3"][l, ff])
            shared[f"w2_{l}_{ff}"] = f(inputs["ffn_w2"][l, ff])
    for e in range(n_even):
        shared[f"w_in_ab{e}"] = f(inputs["w_in_ab"][e])
        shared[f"w_out_ab{e}"] = f(inputs["w_out_ab"][e])
    for o in range(n_odd):
        shared[f"w_in_c{o}"] = f(inputs["w_in_c"][o])
        shared[f"w_out_c{o}"] = f(inputs["w_out_c"][o])
    maps = []
    for b in cores:
        m = dict(shared)
        m["xs"] = f(inputs["x_sample"][b])
        m["xp"] = f(inputs["x_prompt"][2 * b:2 * b + 2]).reshape(TC, D)
        m["cak"] = f(inputs["cache_a_k"][b]).reshape(2, 512, 128)
        m["cav"] = f(inputs["cache_a_v"][b]).reshape(2, 512, 128)
        m["cbk"] = f(inputs["cache_b_k"][b]).reshape(2, 512, 512)
        m["cbv"] = f(inputs["cache_b_v"][b]).reshape(2, 512, 512)
        m["cck"] = f(inputs["cache_c_k"][b]).reshape(2, 512, 1024)
        m["ccv"] = f(inputs["cache_c_v"][b]).reshape(2, 512, 1024)
        m["cvec"] = f(np.stack([inputs["c"][b], inputs["c_ctx"]], 0)).reshape(16, 128)
        maps.append(m)
    return maps


def gather_outputs(results, n):
    yp = np.concatenate([r["yp"].reshape(2, 256, D) for r in results], 0)
    ys = np.stack([r["ys"] for r in results], 0)
    nak = np.concatenate([r["nak"].reshape(2, 2, 256, 2, 64) for r in results], 0)
    nav = np.concatenate([r["nav"].reshape(2, 2, 256, 2, 64) for r in results], 0)
    nbk = np.concatenate([r["nbk"].reshape(2, 2, 256, 8, 64) for r in results], 0)
    nbv = np.concatenate([r["nbv"].reshape(2, 2, 256, 8, 64) for r in results], 0)
    nck = np.concatenate([r["nck"].reshape(2, 2, 256, 8, 128) for r in results], 0)
    ncv = np.concatenate([r["ncv"].reshape(2, 2, 256, 8, 128) for r in results], 0)
    return tuple(np.ascontiguousarray(a.astype(np.float32)) for a in (yp, ys, nak, nav, nbk, nbv, nck, ncv))


def kernel(**inputs):
    nc, _ = _get_prog()
    in_maps = make_in_maps(inputs, list(range(N_CORES)))
    res = run_bass_kernel_spmd(nc, in_maps, core_ids=list(range(N_CORES)))
    return gather_outputs(res.results, N_CORES)
```
